# Optimizing a Trainium2 kernel written in Bass

```python
import math
import jax, jax.numpy as jnp
from jax import lax
import numpy as np

D_MODEL = 1024
BATCH = 16
SEQ = 256
DEPTH = 2
DEC_BATCH = 4
DEC_SEQ = 2048
PAST_LEN = 256

GRID_W = 64
MIX_WIDTH = D_MODEL
HALF = MIX_WIDTH // 2
POOL_WINDOWS = (2, 4, 8, 16)
POOL_GROUP = HALF // len(POOL_WINDOWS)
CONV_WIDTH = 3
HEAD_DIM = 64
N_HEADS_NA = HALF // HEAD_DIM
WIN_H = 8
WIN_W = 16
Q_COLS = 16
K_COLS = Q_COLS + WIN_W
CTX_Q_BLOCK = 128
SSM_GROUP = 16
N_SSM_GROUPS = HALF // SSM_GROUP
SSM_STATE = 64
D_FF = 4 * D_MODEL
N_EVEN = (DEPTH + 1) // 2
N_ODD = DEPTH // 2
EPS = 1e-6

kernel_name = "hybrid_dit_pool_conv_natten_s5_step"


def rmsnorm(x, g):
    xf = x.astype(jnp.float32)
    y = xf * lax.rsqrt(jnp.mean(xf * xf, axis=-1, keepdims=True) + EPS)
    return (y * g.astype(jnp.float32)).astype(x.dtype)


def adaln(cond, w, b):
    m = jax.nn.silu(cond) @ w + b
    return m.reshape(cond.shape[:-1] + (1, 6, D_MODEL))


def modulate(h, shift, scale):
    return h * (1 + scale) + shift


def sqrelu_mlp(h, w1, w2):
    return jnp.square(jax.nn.relu(h @ w1)) @ w2


def to_heads(z):
    b, t, _ = z.shape
    return z.reshape(b, t, N_HEADS_NA, HEAD_DIM).transpose(0, 2, 1, 3)


def from_heads(z):
    b, h, t, d = z.shape
    return z.transpose(0, 2, 1, 3).reshape(b, t, h * d)


def pool_mixer(u, w_grp, scale):
    b, t, _ = u.shape
    uf = u.astype(jnp.float32)
    cs = jnp.concatenate([jnp.zeros((b, 1, HALF), jnp.float32), jnp.cumsum(uf, axis=1)], axis=1)
    pos = jnp.arange(t)
    outs = []
    for gi, w in enumerate(POOL_WINDOWS):
        lo = jnp.maximum(pos - w // 2, 0)
        hi = jnp.minimum(pos + (w - w // 2), t)
        sl = slice(gi * POOL_GROUP, (gi + 1) * POOL_GROUP)
        cnt = (hi - lo).astype(jnp.float32)[None, :, None]
        outs.append((cs[:, hi, sl] - cs[:, lo, sl]) / cnt - uf[:, :, sl])
    pooled = jnp.stack(outs, axis=2).astype(u.dtype)
    mixed = jnp.einsum('btgc,gcd->btgd', pooled, w_grp)
    return mixed.reshape(b, t, HALF) * scale


def short_conv_mixer(b_gate, c_gate, v_in, conv_w):
    z = c_gate * v_in
    t = z.shape[1]
    pad = (CONV_WIDTH - 1) // 2
    zp = jnp.pad(z, ((0, 0), (pad, CONV_WIDTH - 1 - pad), (0, 0)))
    conv = sum(conv_w[j] * zp[:, j:j + t] for j in range(CONV_WIDTH))
    return b_gate * conv


def even_mixer(h, w_in, pool_w, pool_scale, conv_w, w_out):
    proj = h @ w_in
    a_in, b_gate, c_gate, v_in = jnp.split(proj, 4, axis=-1)
    ya = pool_mixer(a_in, pool_w, pool_scale)
    yb = short_conv_mixer(b_gate, c_gate, v_in, conv_w)
    return jnp.concatenate([ya, yb], axis=-1) @ w_out


def context_attention(q, k, v):
    b, h, l, dh = q.shape
    scale = HEAD_DIM ** -0.5
    qb = q.reshape(b, h, l // CTX_Q_BLOCK, CTX_Q_BLOCK, dh).transpose(2, 0, 1, 3, 4)

    def one_block(qi):
        s = jnp.einsum('bhqd,bhkd->bhqk', qi, k).astype(jnp.float32) * scale
        p = jax.nn.softmax(s, axis=-1).astype(v.dtype)
        return jnp.einsum('bhqk,bhkd->bhqd', p, v)

    o = lax.map(one_block, qb)
    return o.transpose(1, 2, 0, 3, 4).reshape(b, h, l, dh)


def neighbourhood_attention(q, k, v, k_ctx, v_ctx, rpb):
    b, h, t, dh = q.shape
    rows = t // GRID_W
    kh = min(WIN_H, rows)
    ncb = GRID_W // Q_COLS
    scale = HEAD_DIM ** -0.5
    r = jnp.arange(rows)
    r0 = jnp.clip(r - kh // 2, 0, rows - kh)
    key_rows = r0[:, None] + jnp.arange(kh)[None, :]
    cb = jnp.arange(ncb)
    s0 = jnp.clip(cb * Q_COLS - WIN_W // 2, 0, GRID_W - K_COLS)
    key_cols = s0[:, None] + jnp.arange(K_COLS)[None, :]
    qcol = cb[:, None] * Q_COLS + jnp.arange(Q_COLS)[None, :]
    c0 = jnp.clip(qcol - WIN_W // 2, 0, GRID_W - WIN_W)
    kc = key_cols[:, None, :]
    col_valid = (kc >= c0[:, :, None]) & (kc < c0[:, :, None] + WIN_W)
    col_off = jnp.clip(kc - qcol[:, :, None], -(WIN_W - 1), WIN_W - 1)
    row_off = key_rows - r[:, None]
    bias = rpb.astype(jnp.float32)[:, row_off[:, None, None, :, None] + (WIN_H - 1),
                                   col_off[None, :, :, None, :] + (WIN_W - 1)]
    bias = jnp.where(col_valid[None, None, :, :, None, :], bias, -jnp.inf)

    qg = q.reshape(b, h, rows, ncb, Q_COLS, dh)
    kg = k.reshape(b, h, rows, GRID_W, dh)
    vg = v.reshape(b, h, rows, GRID_W, dh)
    ridx = key_rows[:, None, :, None]
    cidx = key_cols[None, :, None, :]
    k_blk = kg[:, :, ridx, cidx]
    v_blk = vg[:, :, ridx, cidx]

    s_loc = jnp.einsum('bhrcqd,bhrcyxd->bhrcqyx', qg, k_blk).astype(jnp.float32) * scale + bias
    s_ctx = jnp.einsum('bhrcqd,bhld->bhrcql', qg, k_ctx).astype(jnp.float32) * scale
    n_loc = kh * K_COLS
    s = jnp.concatenate([s_loc.reshape(b, h, rows, ncb, Q_COLS, n_loc), s_ctx], axis=-1)
    p = jax.nn.softmax(s, axis=-1).astype(v.dtype)
    p_loc = p[..., :n_loc].reshape(b, h, rows, ncb, Q_COLS, kh, K_COLS)
    out = (jnp.einsum('bhrcqyx,bhrcyxd->bhrcqd', p_loc, v_blk)
           + jnp.einsum('bhrcql,bhld->bhrcqd', p[..., n_loc:], v_ctx))
    return out.reshape(b, h, t, dh)


def _linear_recurrence(e1, e2):
    a1, b1 = e1
    a2, b2 = e2
    return a2 * a1, a2 * b1 + b2


def s5_bidirectional(u, s_re, s_im, lam_re, lam_im, log_step, b_re, b_im, c_re, c_im, d, glu_w, glu_b):
    bsz, t, _ = u.shape
    f32 = jnp.float32
    uc = u.astype(f32).reshape(bsz, t, N_SSM_GROUPS, SSM_GROUP).astype(jnp.complex64)
    y = d.astype(f32) * u.astype(f32)
    fin_re, fin_im = [], []
    for direction in range(2):
        reverse = direction == 1
        lam = lax.complex(lam_re[direction].astype(f32), lam_im[direction].astype(f32))
        step = jnp.exp(log_step[direction].astype(f32))[:, None]
        lam_bar = jnp.exp(lam * step)
        b_mat = lax.complex(b_re[direction].astype(f32), b_im[direction].astype(f32))
        b_bar = ((lam_bar - 1.0) / lam)[..., None] * b_mat
        c_mat = lax.complex(c_re[direction].astype(f32), c_im[direction].astype(f32))
        s0 = lax.complex(s_re[:, direction].astype(f32), s_im[:, direction].astype(f32))
        bu = jnp.einsum('npg,btng->btnp', b_bar, uc)
        first, last = (t - 1, 0) if reverse else (0, t - 1)
        bu = bu.at[:, first].add(lam_bar * s0)
        a = jnp.broadcast_to(lam_bar, bu.shape)
        _, states = lax.associative_scan(_linear_recurrence, (a, bu), axis=1, reverse=reverse)
        y = y + jnp.einsum('ngp,btnp->btng', c_mat, states).real.reshape(bsz, t, HALF)
        fin = states[:, last]
        fin_re.append(fin.real)
        fin_im.append(fin.imag)
    y = jax.nn.gelu(y).astype(u.dtype)
    y = y * jax.nn.sigmoid(y @ glu_w + glu_b)
    return y, jnp.stack(fin_re, axis=1), jnp.stack(fin_im, axis=1)


def setup_inputs(seed: int = 0) -> dict:
    key = jax.random.key(seed)
    ks = iter(jax.random.split(key, 40))
    nrm = lambda shape, s=1.0: s * jax.random.normal(next(ks), shape, jnp.float32)
    n = jnp.arange(SSM_STATE, dtype=jnp.float32)
    lam_im = jnp.broadcast_to(math.pi * n, (N_ODD, 2, N_SSM_GROUPS, SSM_STATE))
    return {
        "x_prompt": nrm((BATCH, SEQ, D_MODEL)),
        "x_sample": nrm((DEC_BATCH, DEC_SEQ, D_MODEL)),
        "c": nrm((DEC_BATCH, D_MODEL)),
        "cache_k": nrm((DEC_BATCH, N_ODD, N_HEADS_NA, PAST_LEN, HEAD_DIM)),
        "cache_v": nrm((DEC_BATCH, N_ODD, N_HEADS_NA, PAST_LEN, HEAD_DIM)),
        "state_s5_re": nrm((DEC_BATCH, N_ODD, 2, N_SSM_GROUPS, SSM_STATE), 0.1),
        "state_s5_im": nrm((DEC_BATCH, N_ODD, 2, N_SSM_GROUPS, SSM_STATE), 0.1),
        "c_ctx": nrm((D_MODEL,)),
        "norm1_g": 1.0 + nrm((DEPTH, D_MODEL), 0.02),
        "norm2_g": 1.0 + nrm((DEPTH, D_MODEL), 0.02),
        "ada_w": nrm((DEPTH, D_MODEL, 6 * D_MODEL), 0.5 * D_MODEL ** -0.5),
        "ada_b": nrm((DEPTH, 6 * D_MODEL), 0.02),
        "mlp_w1": nrm((DEPTH, D_MODEL, D_FF), D_MODEL ** -0.5),
        "mlp_w2": nrm((DEPTH, D_FF, D_MODEL), D_FF ** -0.5),
        "ab_w_in": nrm((N_EVEN, D_MODEL, 4 * HALF), D_MODEL ** -0.5),
        "pool_w": nrm((N_EVEN, len(POOL_WINDOWS), POOL_GROUP, POOL_GROUP), POOL_GROUP ** -0.5),
        "pool_scale": 1.0 + nrm((N_EVEN, HALF), 0.05),
        "conv_w": nrm((N_EVEN, CONV_WIDTH, HALF), CONV_WIDTH ** -0.5),
        "ab_w_out": nrm((N_EVEN, MIX_WIDTH, D_MODEL), MIX_WIDTH ** -0.5),
        "cd_w_in": nrm((N_ODD, D_MODEL, 4 * HALF), D_MODEL ** -0.5),
        "na_rpb": nrm((N_ODD, N_HEADS_NA, 2 * WIN_H - 1, 2 * WIN_W - 1), 0.1),
        "ssm_lambda_re": -0.5 + nrm((N_ODD, 2, N_SSM_GROUPS, SSM_STATE), 0.01),
        "ssm_lambda_im": lam_im + nrm((N_ODD, 2, N_SSM_GROUPS, SSM_STATE), 0.01),
        "ssm_log_step": jax.random.uniform(next(ks), (N_ODD, 2, N_SSM_GROUPS), jnp.float32,
                                           math.log(1e-3), math.log(1e-1)),
        "ssm_b_re": nrm((N_ODD, 2, N_SSM_GROUPS, SSM_STATE, SSM_GROUP), (2 * SSM_GROUP) ** -0.5),
        "ssm_b_im": nrm((N_ODD, 2, N_SSM_GROUPS, SSM_STATE, SSM_GROUP), (2 * SSM_GROUP) ** -0.5),
        "ssm_c_re": nrm((N_ODD, 2, N_SSM_GROUPS, SSM_GROUP, SSM_STATE), (2 * SSM_STATE) ** -0.5),
        "ssm_c_im": nrm((N_ODD, 2, N_SSM_GROUPS, SSM_GROUP, SSM_STATE), (2 * SSM_STATE) ** -0.5),
        "ssm_d": nrm((N_ODD, HALF)),
        "glu_w": nrm((N_ODD, HALF, HALF), HALF ** -0.5),
        "glu_b": nrm((N_ODD, HALF), 0.02),
        "cd_w_out": nrm((N_ODD, MIX_WIDTH, D_MODEL), MIX_WIDTH ** -0.5),
        "final_g": 1.0 + nrm((D_MODEL,), 0.02),
    }


def reference(x_prompt, x_sample, c, cache_k, cache_v, state_s5_re, state_s5_im, c_ctx,
              norm1_g, norm2_g, ada_w, ada_b, mlp_w1, mlp_w2,
              ab_w_in, pool_w, pool_scale, conv_w, ab_w_out,
              cd_w_in, na_rpb, ssm_lambda_re, ssm_lambda_im, ssm_log_step,
              ssm_b_re, ssm_b_im, ssm_c_re, ssm_c_im, ssm_d, glu_w, glu_b, cd_w_out,
              final_g):
    xp, xs = x_prompt, x_sample
    new_k, new_v, new_re, new_im = [], [], [], []
    for layer in range(DEPTH):
        mod_p = adaln(c_ctx, ada_w[layer], ada_b[layer])
        mod_s = adaln(c, ada_w[layer], ada_b[layer])
        hp = modulate(rmsnorm(xp, norm1_g[layer]), mod_p[..., 0, :], mod_p[..., 1, :])
        hs = modulate(rmsnorm(xs, norm1_g[layer]), mod_s[..., 0, :], mod_s[..., 1, :])
        i = layer // 2
        if layer % 2 == 0:
            yp = even_mixer(hp, ab_w_in[i], pool_w[i], pool_scale[i], conv_w[i], ab_w_out[i])
            ys = even_mixer(hs, ab_w_in[i], pool_w[i], pool_scale[i], conv_w[i], ab_w_out[i])
        else:
            ssm_args = (ssm_lambda_re[i], ssm_lambda_im[i], ssm_log_step[i], ssm_b_re[i], ssm_b_im[i],
                        ssm_c_re[i], ssm_c_im[i], ssm_d[i], glu_w[i], glu_b[i])
            q_p, k_p, v_p, u_p = jnp.split(hp @ cd_w_in[i], 4, axis=-1)
            k_ph, v_ph = to_heads(k_p), to_heads(v_p)
            o_p = from_heads(context_attention(to_heads(q_p), k_ph, v_ph))
            zero_state = jnp.zeros((xp.shape[0], 2, N_SSM_GROUPS, SSM_STATE), jnp.float32)
            d_p, fin_re, fin_im = s5_bidirectional(u_p, zero_state, zero_state, *ssm_args)
            yp = jnp.concatenate([o_p, d_p], axis=-1) @ cd_w_out[i]
            new_k.append(k_ph)
            new_v.append(v_ph)
            new_re.append(fin_re)
            new_im.append(fin_im)
            q_s, k_s, v_s, u_s = jnp.split(hs @ cd_w_in[i], 4, axis=-1)
            o_s = from_heads(neighbourhood_attention(to_heads(q_s), to_heads(k_s), to_heads(v_s),
                                                     cache_k[:, i], cache_v[:, i], na_rpb[i]))
            d_s, _, _ = s5_bidirectional(u_s, state_s5_re[:, i], state_s5_im[:, i], *ssm_args)
            ys = jnp.concatenate([o_s, d_s], axis=-1) @ cd_w_out[i]
        xp = xp + mod_p[..., 2, :] * yp
        xs = xs + mod_s[..., 2, :] * ys
        hp = modulate(rmsnorm(xp, norm2_g[layer]), mod_p[..., 3, :], mod_p[..., 4, :])
        hs = modulate(rmsnorm(xs, norm2_g[layer]), mod_s[..., 3, :], mod_s[..., 4, :])
        xp = xp + mod_p[..., 5, :] * sqrelu_mlp(hp, mlp_w1[layer], mlp_w2[layer])
        xs = xs + mod_s[..., 5, :] * sqrelu_mlp(hs, mlp_w1[layer], mlp_w2[layer])
    y_prompt = rmsnorm(xp, final_g)
    y_sample = rmsnorm(xs, final_g)
    new_cache_k = jnp.stack(new_k, axis=1)
    new_cache_v = jnp.stack(new_v, axis=1)
    new_state_re = jnp.stack(new_re, axis=1)
    new_state_im = jnp.stack(new_im, axis=1)
    return (y_prompt, y_sample, new_cache_k, new_cache_v, new_state_re, new_state_im)
```

```python
import math
from contextlib import ExitStack

import numpy as np
import ml_dtypes

import concourse.bass as bass
import concourse.mybir as mybir
from concourse.bass_utils import run_bass_kernel_spmd

F32 = mybir.dt.float32
BF16 = mybir.dt.bfloat16
ALU = mybir.AluOpType
ACT = mybir.ActivationFunctionType

ENGS = ("pe", "act", "dve", "pool", "sp")

D = 1024
T = 2560
NT = 5
TT = 512
SEQS = ((0, 256), (256, 256), (512, 2048))
EPS = 1e-6
NEG = -30000.0


import types


def _freeze(fn, depth=0):
    if not isinstance(fn, types.FunctionType) or depth > 4:
        return fn
    cells = None
    if fn.__closure__ is not None:
        new = []
        for c in fn.__closure__:
            try:
                v = c.cell_contents
            except ValueError:
                new.append(c)
                continue
            new.append(types.CellType(_freeze(v, depth + 1)))
        cells = tuple(new)
    defaults = fn.__defaults__
    if defaults is not None:
        defaults = tuple(_freeze(v, depth + 1) for v in defaults)
    nf = types.FunctionType(fn.__code__, fn.__globals__, fn.__name__, defaults, cells)
    nf.__kwdefaults__ = fn.__kwdefaults__
    return nf


class Prog:
    def __init__(self, nc, es, n_dma_sems=56):
        self.nc = nc
        self.ops = {e: [] for e in ENGS}
        self.cnt = {e: 0 for e in ENGS}
        self.sem = {e: es.enter_context(nc.semaphore("s_" + e)) for e in ENGS}
        self.dma_sems = [es.enter_context(nc.semaphore("d%d" % i)) for i in range(n_dma_sems)]
        self.dma_cnt = [0] * n_dma_sems
        self.dma_rr = 0
        self.sw_sems = [es.enter_context(nc.semaphore("w%d" % i)) for i in range(24)]
        self.sw_cnt = [0] * 24
        self.sw_rr = 0
        self.bg_sems = [es.enter_context(nc.semaphore("b%d" % i)) for i in range(16)]
        self.bg_cnt = [0] * 16
        self.bg_rr = 0
        self.last_w = {}
        self.readers = {}
        self.seen = {e: {} for e in ENGS}
        self.final_tokens = []
        self.n_inst = 0
        self.sticky = set()

    def _need(self, eng, tok, war=False, is_dma=False):
        kind, a, v = tok
        if kind == "e":
            if a == eng and not is_dma and (war or eng == "pe"):
                return
            sid = ("e", a)
            sem = self.sem[a]
        elif kind == "b":
            sid = ("b", a)
            sem = self.bg_sems[a]
        elif kind == "w":
            sid = ("w", a)
            sem = self.sw_sems[a]
        else:
            sid = ("d", a)
            sem = self.dma_sems[a]
        if self.seen[eng].get(sid, 0) >= v:
            return
        self.seen[eng][sid] = v
        self.ops[eng].append(lambda E, sem=sem, v=v: E.wait_ge(sem, v))

    def _deps(self, eng, reads, writes, is_dma=False):
        for k in reads:
            t = self.last_w.get(k)
            if t is not None:
                self._need(eng, t, is_dma=is_dma)
        for k in writes:
            t = self.last_w.get(k)
            if t is not None:
                self._need(eng, t, is_dma=is_dma)
            for t in self.readers.get(k, ()):
                self._need(eng, t, war=True, is_dma=is_dma)

    def _commit(self, tok, reads, writes):
        for k in reads:
            if k not in writes:
                self.readers.setdefault(k, []).append(tok)
        for k in writes:
            self.last_w[k] = tok
            self.readers[k] = []

    def op(self, eng, fn, reads=(), writes=()):
        fn = _freeze(fn)
        self._deps(eng, reads, writes)
        self.cnt[eng] += 1
        n = self.cnt[eng]
        sem = self.sem[eng]
        self.ops[eng].append(lambda E, fn=fn, sem=sem: fn(E).then_inc(sem, 1))
        self._commit(("e", eng, n), reads, writes)
        self.n_inst += 1

    def dma(self, fn, reads=(), writes=(), q="sp", final=False, bg=False):
        fn = _freeze(fn)
        if bg:
            sems, cnts, kind = self.bg_sems, self.bg_cnt, "b"
            i = self.bg_rr
            self.bg_rr = (self.bg_rr + 1) % len(sems)
        elif q == "pool":
            sems, cnts, kind = self.sw_sems, self.sw_cnt, "w"
            i = self.sw_rr
            self.sw_rr = (self.sw_rr + 1) % len(sems)
        else:
            sems, cnts, kind = self.dma_sems, self.dma_cnt, "d"
            i = self.dma_rr
            self.dma_rr = (self.dma_rr + 1) % len(sems)
        if cnts[i] > 0:
            self._need(q, (kind, i, cnts[i]))
        self._deps(q, reads, writes, is_dma=True)
        cnts[i] += 16
        v = cnts[i]
        sem = sems[i]
        self.ops[q].append(lambda E, fn=fn, sem=sem: fn(E).then_inc(sem, 16))
        tok = (kind, i, v)
        self._commit(tok, reads, writes)
        if final:
            self.final_tokens.append(tok)
        self.n_inst += 1

    def wait_keys(self, eng, keys):
        for k in keys:
            t = self.last_w.get(k)
            if t is not None:
                self._need(eng, t)

    def barrier(self):
        for e in ENGS:
            for o in ENGS:
                if o != "sp" and self.cnt[o] > 0:
                    self._need(e, ("e", o, self.cnt[o]), is_dma=(o == e))
            for i, c in enumerate(self.dma_cnt):
                if c > 0:
                    self._need(e, ("d", i, c))
            for i, c in enumerate(self.sw_cnt):
                if c > 0:
                    self._need(e, ("w", i, c))
        keep = {k: v for k, v in self.last_w.items() if k in self.sticky}
        self.last_w = keep
        self.readers = {}

    def finish(self):
        for tok in self.final_tokens:
            self._need("sp", tok)
        for e in ENGS:
            if e != "sp" and self.cnt[e] > 0:
                self._need("sp", ("e", e, self.cnt[e]))
        for i, c in enumerate(self.dma_cnt):
            if c > 0:
                self._need("sp", ("d", i, c))
        for i, c in enumerate(self.bg_cnt):
            if c > 0:
                self._need("sp", ("b", i, c))
        for i, c in enumerate(self.sw_cnt):
            if c > 0:
                self._need("sp", ("w", i, c))


_UNIQ = [0]


def _sbt_enter(nc, es, name, shape, dt):
    return es.enter_context(_sbt(nc, name, shape, dt))


def _sbt(nc, name, shape, dt):
    _UNIQ[0] += 1
    return nc.sbuf_tensor("%s_u%d" % (name, _UNIQ[0]), shape, dt)


class Ring:
    def __init__(self, nc, es, name, n, shape, dtype):
        _UNIQ[0] += 1
        name = "%s_%d_" % (name, _UNIQ[0])
        self.tiles = [es.enter_context(_sbt(nc, "%s%d" % (name, i), shape, dtype)) for i in range(n)]
        self.name = name
        self.i = 0

    def next(self):
        i = self.i
        self.i = (self.i + 1) % len(self.tiles)
        return self.tiles[i], (self.name, i)


class G:
    pass


def _vec_layout():
    off = {}
    o = 0

    def add(name, n):
        nonlocal o
        off[name] = o
        o += n

    for l in range(2):
        add("n1g%d" % l, 8)
        add("n2g%d" % l, 8)
        add("adab%d" % l, 48)
    add("fing", 8)
    add("pscale", 4)
    add("convw", 12)
    add("ssmd", 4)
    add("glub", 4)
    return off, o


VOFF, NVEC = _vec_layout()


def _psb(g):
    b = g.bank % g.nbanks
    g.bank = (b + 1) % g.nbanks
    return b


def load_cast(g, dst, dkey, src, stg_shape=None, eng=None, q=None):
    g.P.dma(lambda E: E.dma_start(out=dst, in_=src), writes=[dkey], q="pool")


def adaln_group(g, l, cg, bank):
    P, ps = g.P, g.ps
    wt, wkey = g.aring.next()
    wv = wt[:, 0:8 * 512].rearrange("p (a b) -> p a b", a=8)
    src = g.d["ada_w"][l].rearrange("(kc p) n -> p kc n", p=128)[:, :, cg * 512:(cg + 1) * 512]
    if getattr(g, "astg", None) is not None:
        st, sk = g.astg.next()
        sv = st[:, 0:8 * 512].rearrange("p (a b) -> p a b", a=8)
        P.dma(lambda E: E.dma_start(out=sv, in_=src), writes=[sk], q="sp")
        P.op("act", lambda E: E.activation(out=wv, in_=sv, func=ACT.Copy), reads=[sk], writes=[wkey])
    else:
        load_cast(g, wv, wkey, src)
    for c4 in range(4):
        ch = cg * 4 + c4
        for kc in range(8):
            P.op("pe", lambda E, kc=kc, c4=c4, ch=ch: E.matmul(
                ps[:, bank, ch * 2:ch * 2 + 2], lhsT=wv[:, kc, c4 * 128:(c4 + 1) * 128], rhs=g.scT[:, kc, :],
                start=(kc == 0), stop=(kc == 7)), reads=[wkey, "scT"], writes=[("ps", bank)])


def adaln_finish(g, l, bank, sections=range(6)):
    P, ps = g.P, g.ps
    mod = g.mod[l]
    a0 = VOFF["adab%d" % l]
    for sx in sections:
        for j in range(2):
            P.op("dve", lambda E, j=j, sx=sx: E.tensor_tensor(
                out=mod[:, sx * 8:(sx + 1) * 8, j], in0=ps[:, bank, sx * 16 + j:sx * 16 + 16:2],
                in1=g.vecs[:, a0 + sx * 8:a0 + (sx + 1) * 8], op=ALU.add),
                reads=[("ps", bank), "vecs"], writes=[("mod", l)])
        if sx in (1, 4):
            s2 = 0 if sx == 1 else 1
            gname = ("n1g%d" if sx == 1 else "n2g%d") % l
            gv = g.vecs[:, VOFF[gname]:VOFF[gname] + 8]
            for j in range(2):
                P.op("dve", lambda E, j=j, s2=s2, sx=sx, gv=gv: E.scalar_tensor_tensor(
                    out=g.gm[l][:, s2, :, j], in0=mod[:, sx * 8:(sx + 1) * 8, j], scalar=1.0, in1=gv,
                    op0=ALU.add, op1=ALU.mult), reads=[("mod", l), "vecs"], writes=[("gm", l)])


def stage_adaln(g, l):
    nc, P = g.nc, g.P
    with ExitStack() as ss:
        g.aring = Ring(nc, ss, "wrA", 3, [128, 4096], BF16)
        bank = _psb(g)
        for cg in range(12):
            adaln_group(g, l, cg, bank)
        adaln_finish(g, l, bank)
        P.barrier()


def norm_tile(g, xT, xkey, tile, gmv, shv, l, wkeys, out_keys, defer=False):
    nc, P, ps = g.nc, g.P, g.ps
    sq, sqk = g.sqring.next()
    P.op("act", lambda E: E.activation(out=sq[:], in_=xT, func=ACT.Square), reads=[xkey], writes=[sqk])
    bank = _psb(g)
    for c in range(8):
        P.op("pe", lambda E, c=c: E.matmul(ps[:, bank, :], lhsT=g.ones_bf[:], rhs=sq[:, c, :],
                                           start=(c == 0), stop=(c == 7)), reads=[sqk, "consts"], writes=[("ps", bank)])
    rs, rsk = g.rsring.next()
    P.op("act", lambda E: E.activation(out=rs[:], in_=ps[:, bank, :], func=ACT.Sqrt, bias=g.epst[:], scale=1.0 / D),
         reads=[("ps", bank), "consts"], writes=[rsk])
    P.op("dve", lambda E: E.reciprocal(out=rs[:], in_=rs[:]), reads=[rsk], writes=[rsk])
    xn, xnk = g.xnring.next()
    P.op("dve", lambda E: E.tensor_tensor(out=xn[:], in0=xT, in1=rs[:].unsqueeze(1).to_broadcast([128, 8, TT]),
                                          op=ALU.mult), reads=[xkey, rsk], writes=[xnk])
    def part2():
        j = 0 if tile == 0 else 1
        for c in range(8):
            eng = "pool" if c % 2 == 0 else "act"
            if eng == "pool":
                P.op("pool", lambda E, c=c: E.tensor_scalar(
                    out=g.hbuf[:, c, tile * TT:(tile + 1) * TT], in0=xn[:, c, :], scalar1=gmv[:, c, j:j + 1],
                    scalar2=shv[:, c, j:j + 1], op0=ALU.mult, op1=ALU.add),
                    reads=[xnk] + wkeys, writes=[("h", tile, c)])
            else:
                P.op("act", lambda E, c=c: E.activation(
                    out=g.hbuf[:, c, tile * TT:(tile + 1) * TT], in_=xn[:, c, :], func=ACT.Identity,
                    bias=shv[:, c, j:j + 1], scale=gmv[:, c, j:j + 1]),
                    reads=[xnk] + wkeys, writes=[("h", tile, c)])

    if defer:
        return part2
    part2()
    return None


def hkeys(tile):
    return [("h", tile, c) for c in range(8)]


def stage_in_norm1(g, hook=None):
    nc, P, ps = g.nc, g.P, g.ps
    with ExitStack() as ss:
        xtok = Ring(nc, ss, "xtok", 2, [128, D], F32)
        xTr = Ring(nc, ss, "xTa", 2, [128, 8, TT], F32)
        g.sqring = Ring(nc, ss, "sq", 2, [128, 8, TT], BF16)
        g.rsring = Ring(nc, ss, "rs", 2, [128, TT], F32)
        g.xnring = Ring(nc, ss, "xn", 2, [128, 8, TT], F32)
        pend_ = None
        for tile in range(NT):
            if hook is not None:
                hook(tile)
            xT, xk = xTr.next()
            for sub in range(4):
                tt = tile * 4 + sub
                xt, xtk = xtok.next()
                P.dma(lambda E, xt=xt, tt=tt: E.dma_start(out=xt[:], in_=g.d["xin"][tt * 128:(tt + 1) * 128, :]),
                      writes=[xtk], q="sp")
                for hb in range(2):
                    bank = _psb(g)
                    for c4 in range(4):
                        c = hb * 4 + c4
                        P.op("pe", lambda E, xt=xt, c=c, c4=c4, bank=bank: E.transpose(
                            ps[:, bank, c4 * 128:(c4 + 1) * 128], xt[:, c * 128:(c + 1) * 128], g.ident[:]),
                            reads=[xtk, "consts"], writes=[("ps", bank)])
                    P.op("act" if hb == 0 else "dve", lambda E, xT=xT, hb=hb, sub=sub, bank=bank: (
                        E.activation(out=xT[:, hb * 4:(hb + 1) * 4, sub * 128:(sub + 1) * 128],
                                     in_=ps[:, bank, :].rearrange("p (a b) -> p a b", a=4), func=ACT.Copy)
                        if hb == 0 else
                        E.tensor_copy(out=xT[:, hb * 4:(hb + 1) * 4, sub * 128:(sub + 1) * 128],
                                      in_=ps[:, bank, :].rearrange("p (a b) -> p a b", a=4))),
                        reads=[("ps", bank)], writes=[xk])
            P.dma(lambda E, xT=xT, tile=tile: E.dma_start(
                out=g.xs[:, :, tile * TT:(tile + 1) * TT].rearrange("c p t -> p c t"), in_=xT[:]),
                reads=[xk], writes=[("xs", tile)], q="sp")
            nxt_ = norm_tile(g, xT[:], xk, tile, g.gm[0][:, 0], g.mod[0][:, 0:8], 0, [("gm", 0), ("mod", 0)], None, defer=True)
            if pend_ is not None:
                pend_()
            pend_ = nxt_
        pend_()
        P.barrier()


def linear_fm(g, w_of, wkey_of, rhs_of, rkeys_of, nk, n_oc, tiles, evac, ncols=TT):
    P, ps = g.P, g.ps
    for oc in range(n_oc):
        for tile in tiles:
            bank = _psb(g)
            for k in range(nk):
                P.op("pe", lambda E, oc=oc, k=k, tile=tile, bank=bank: E.matmul(
                    ps[:, bank, 0:ncols], lhsT=w_of(oc, k), rhs=rhs_of(k, tile), start=(k == 0), stop=(k == nk - 1)),
                    reads=[wkey_of(oc)] + rkeys_of(k, tile), writes=[("ps", bank)])
            evac(oc, tile, bank)


PADW = 8
PSEQ = []
_o = PADW
for (_s, _n) in SEQS:
    PSEQ.append((_s, _n, _o))
    _o += _n + 2 * PADW
PTOT = _o - PADW


def segs_of_tile(tile):
    out = []
    a, b = tile * TT, (tile + 1) * TT
    for (s, n, p) in PSEQ:
        lo, hi = max(a, s), min(b, s + n)
        if lo < hi:
            out.append((lo, hi - lo, p + lo - s))
    return out


def stage_mixer0(g, hook=None):
    nc, P, ps = g.nc, g.P, g.ps
    with ExitStack() as ss:
        wr = Ring(nc, ss, "w0", 3, [128, 8, 128], BF16)
        abuf = ss.enter_context(_sbt(nc, "abuf", [128, PTOT], F32))
        p2 = ss.enter_context(_sbt(nc, "p2", [128, PTOT], F32))
        p4 = ss.enter_context(_sbt(nc, "p4", [128, PTOT], F32))
        rc = ss.enter_context(_sbt(nc, "rc", [128, T], F32))
        pooled = ss.enter_context(_sbt(nc, "pooled", [128, T], BF16))
        ctmp = Ring(nc, ss, "ctmp", 2, [128, TT], F32)
        pw = ss.enter_context(_sbt(nc, "pw", [128, 4, 128], BF16))
        P.op("pool", lambda E: E.memset(abuf[:], 0.0), writes=["abuf"])
        P.op("pool", lambda E: E.memset(p2[:], 0.0), writes=["p2"])
        P.op("pool", lambda E: E.memset(p4[:], 0.0), writes=["p4"])
        load_cast(g, pw[:], "pw", g.d["pool_w"].rearrange("g c d -> c g d"), [128, 4, 128])
        win = g.d["ab_w_in"].rearrange("(kc p) n -> p kc n", p=128)

        def wload(col0):
            wt, wk = wr.next()
            load_cast(g, wt[:], wk, win[:, :, col0:col0 + 128], [128, 8, 128])
            return wt, wk

        for gi, w in enumerate((2, 4, 8, 16)):
            if hook is not None:
                hook(gi)
            wt, wk = wload(gi * 128)
            P.dma(lambda E, gi=gi: E.dma_start(out=rc[:], in_=g.d["rcnt"][gi:gi + 1, :].to_broadcast([128, T])),
                  writes=["rc"], q="sp")

            def ev(oc, tile, bank):
                for (fs, n, pc) in segs_of_tile(tile):
                    P.op("act", lambda E, fs=fs, n=n, pc=pc, tile=tile, bank=bank: E.activation(
                        out=abuf[:, pc:pc + n], in_=ps[:, bank, fs - tile * TT:fs - tile * TT + n], func=ACT.Copy),
                        reads=[("ps", bank)], writes=["abuf"])

            linear_fm(g, lambda oc, k, wt=wt: wt[:, k, :], lambda oc, wk=wk: wk,
                      lambda k, tile: g.hbuf[:, k, tile * TT:(tile + 1) * TT], lambda k, tile: [("h", tile, k)],
                      8, 1, range(NT), ev)
            cur, curk, ln = abuf, "abuf", 1
            bufs = [(p2, "p2"), (p4, "p4")]
            bi = 0
            while ln < w:
                nb, nbk = bufs[bi]
                bi ^= 1
                P.op("dve", lambda E, cur=cur, nb=nb, ln=ln: E.tensor_tensor(
                    out=nb[:, 0:PTOT - ln], in0=cur[:, 0:PTOT - ln], in1=cur[:, ln:PTOT], op=ALU.add),
                    reads=[curk], writes=[nbk])
                cur, curk, ln = nb, nbk, ln * 2
            for (fs, n, pc) in PSEQ:
                tb, tbk = bufs[bi]
                P.op("dve", lambda E, cur=cur, fs=fs, n=n, pc=pc, tb=tb, w=w: E.tensor_tensor(
                    out=tb[:, pc:pc + n], in0=cur[:, pc - w // 2:pc - w // 2 + n], in1=rc[:, fs:fs + n], op=ALU.mult),
                    reads=[curk, "rc"], writes=[tbk])
                P.op("dve", lambda E, fs=fs, n=n, pc=pc, tb=tb: E.tensor_tensor(
                    out=pooled[:, fs:fs + n], in0=tb[:, pc:pc + n], in1=abuf[:, pc:pc + n], op=ALU.subtract),
                    reads=[tbk, "abuf"], writes=["pooled"])

            def ev2(oc, tile, bank, gi=gi):
                P.op("act", lambda E, tile=tile, bank=bank: E.activation(
                    out=g.mixbuf[:, gi, tile * TT:(tile + 1) * TT], in_=ps[:, bank, :], func=ACT.Copy,
                    scale=g.vecs[:, VOFF["pscale"] + gi:VOFF["pscale"] + gi + 1]),
                    reads=[("ps", bank), "vecs"], writes=[("mix", gi, tile)])

            linear_fm(g, lambda oc, k, gi=gi: pw[:, gi, :], lambda oc: "pw",
                      lambda k, tile: pooled[:, tile * TT:(tile + 1) * TT], lambda k, tile: ["pooled"],
                      1, 1, range(NT), ev2)
            P.op("pool", lambda E: E.memset(p2[:], 0.0), reads=[], writes=["p2"])
            P.op("pool", lambda E: E.memset(p4[:], 0.0), reads=[], writes=["p4"])

        zbuf, cv1, cv2 = abuf, p2, p4
        for j in range(4):
            if hook is not None:
                hook(4 + j)
            wc, wck = wload(1024 + j * 128)
            wv_, wvk = wload(1536 + j * 128)
            wb, wbk = wload(512 + j * 128)
            for tile in range(NT):
                bank = _psb(g)
                for k in range(8):
                    P.op("pe", lambda E, k=k, tile=tile, bank=bank: E.matmul(
                        ps[:, bank, :], lhsT=wc[:, k, :], rhs=g.hbuf[:, k, tile * TT:(tile + 1) * TT],
                        start=(k == 0), stop=(k == 7)), reads=[wck, ("h", tile, k)], writes=[("ps", bank)])
                ct, ctk = ctmp.next()
                P.op("act", lambda E, ct=ct, bank=bank: E.activation(out=ct[:], in_=ps[:, bank, :], func=ACT.Copy),
                     reads=[("ps", bank)], writes=[ctk])
                bank2 = _psb(g)
                for k in range(8):
                    P.op("pe", lambda E, k=k, tile=tile, bank2=bank2: E.matmul(
                        ps[:, bank2, :], lhsT=wv_[:, k, :], rhs=g.hbuf[:, k, tile * TT:(tile + 1) * TT],
                        start=(k == 0), stop=(k == 7)), reads=[wvk, ("h", tile, k)], writes=[("ps", bank2)])
                for (fs, n, pc) in segs_of_tile(tile):
                    P.op("dve", lambda E, fs=fs, n=n, pc=pc, tile=tile, ct=ct, bank2=bank2: E.tensor_tensor(
                        out=zbuf[:, pc:pc + n], in0=ps[:, bank2, fs - tile * TT:fs - tile * TT + n],
                        in1=ct[:, fs - tile * TT:fs - tile * TT + n], op=ALU.mult),
                        reads=[("ps", bank2), ctk], writes=["abuf"])
            cw = VOFF["convw"]
            ch = j
            P.op("dve", lambda E, ch=ch: E.tensor_scalar(
                out=cv1[:, 1:PTOT - 1], in0=zbuf[:, 0:PTOT - 2], scalar1=g.vecs[:, cw + 0 * 4 + ch:cw + 0 * 4 + ch + 1],
                scalar2=None, op0=ALU.mult), reads=["abuf", "vecs"], writes=["p2"])
            P.op("dve", lambda E, ch=ch: E.scalar_tensor_tensor(
                out=cv2[:, 1:PTOT - 1], in0=zbuf[:, 1:PTOT - 1], scalar=g.vecs[:, cw + 1 * 4 + ch:cw + 1 * 4 + ch + 1],
                in1=cv1[:, 1:PTOT - 1], op0=ALU.mult, op1=ALU.add), reads=["abuf", "p2", "vecs"], writes=["p4"])
            P.op("dve", lambda E, ch=ch: E.scalar_tensor_tensor(
                out=cv1[:, 1:PTOT - 1], in0=zbuf[:, 2:PTOT], scalar=g.vecs[:, cw + 2 * 4 + ch:cw + 2 * 4 + ch + 1],
                in1=cv2[:, 1:PTOT - 1], op0=ALU.mult, op1=ALU.add), reads=["abuf", "p4", "vecs"], writes=["p2"])
            for tile in range(NT):
                bank = _psb(g)
                for k in range(8):
                    P.op("pe", lambda E, k=k, tile=tile, bank=bank: E.matmul(
                        ps[:, bank, :], lhsT=wb[:, k, :], rhs=g.hbuf[:, k, tile * TT:(tile + 1) * TT],
                        start=(k == 0), stop=(k == 7)), reads=[wbk, ("h", tile, k)], writes=[("ps", bank)])
                for (fs, n, pc) in segs_of_tile(tile):
                    P.op("dve", lambda E, fs=fs, n=n, pc=pc, tile=tile, bank=bank, j=j: E.tensor_tensor(
                        out=g.mixbuf[:, 4 + j, fs:fs + n], in0=ps[:, bank, fs - tile * TT:fs - tile * TT + n],
                        in1=cv1[:, pc:pc + n], op=ALU.mult), reads=[("ps", bank), "p2"], writes=[("mix", 4 + j, tile)])
        P.barrier()


def stage_wout_norm2(g, l, wname, mix_of, mixkeys_of):
    nc, P, ps = g.nc, g.P, g.ps
    with ExitStack() as ss:
        wo = ss.enter_context(_sbt(nc, "wo", [128, 8, D], BF16))
        xr = Ring(nc, ss, "xw", 2, [128, 8, TT], F32)
        g.sqring = Ring(nc, ss, "sq", 1, [128, 8, TT], BF16)
        g.rsring = Ring(nc, ss, "rs", 2, [128, TT], F32)
        g.xnring = Ring(nc, ss, "xn", 1, [128, 8, TT], F32)
        wsrc = g.d[wname].rearrange("(kc p) n -> p kc n", p=128)
        for h2 in range(4):
            load_cast(g, wo[:, :, h2 * 256:(h2 + 1) * 256], ("wo", h2), wsrc[:, :, h2 * 256:(h2 + 1) * 256], [128, 8, 256])
        for tile in range(NT):
            xT, xk = xr.next()
            P.dma(lambda E, xT=xT, tile=tile: E.dma_start(
                out=xT[:], in_=g.xs[:, :, tile * TT:(tile + 1) * TT].rearrange("c p t -> p c t")),
                reads=[("xs", tile)], writes=[xk])
            j = 0 if tile == 0 else 1
            for oc in range(8):
                bank = _psb(g)
                for k in range(8):
                    P.op("pe", lambda E, oc=oc, k=k, tile=tile, bank=bank: E.matmul(
                        ps[:, bank, :], lhsT=wo[:, k, oc * 128:(oc + 1) * 128], rhs=mix_of(k, tile),
                        start=(k == 0), stop=(k == 7)),
                        reads=[("wo", oc // 2)] + mixkeys_of(k, tile), writes=[("ps", bank)])
                P.op("dve", lambda E, oc=oc, bank=bank, xT=xT, j=j: E.scalar_tensor_tensor(
                    out=xT[:, oc, :], in0=ps[:, bank, :], scalar=g.mod[l][:, 16 + oc, j:j + 1], in1=xT[:, oc, :],
                    op0=ALU.mult, op1=ALU.add), reads=[("ps", bank), xk, ("mod", l)], writes=[xk])
            P.dma(lambda E, xT=xT, tile=tile: E.dma_start(
                out=g.xs[:, :, tile * TT:(tile + 1) * TT].rearrange("c p t -> p c t"), in_=xT[:]),
                reads=[xk], writes=[("xs", tile)], q="sp")
            norm_tile(g, xT[:], xk, tile, g.gm[l][:, 1], g.mod[l][:, 24:32], l, [("gm", l), ("mod", l)], None)
        P.barrier()


def stage_mlp(g, l, X, hook=None):
    nc, P, ps = g.nc, g.P, g.ps
    with ExitStack() as ss:
        hid = Ring(nc, ss, "hid", 2, [128, 4, TT], BF16)
        rl = Ring(nc, ss, "rl", 3, [128, TT], BF16)
        g.wring = Ring(nc, ss, "wrM", 4, [128, 4096], BF16)
        for tile in range(NT):
            P.dma(lambda E, tile=tile: E.dma_start(
                out=X[:, :, tile * TT:(tile + 1) * TT], in_=g.xs[:, :, tile * TT:(tile + 1) * TT].rearrange("c p t -> p c t")),
                reads=[("xs", tile)], writes=[("X", tile)], q="sp")
        w1src = g.d["mlp_w1"][l].rearrange("(kc p) n -> p kc n", p=128)
        w2src = g.d["mlp_w2"][l].rearrange("(fc p) n -> p fc n", p=128)
        FB = 8

        def wl(fb):
            w1t, w1k = g.wring.next()
            w1v = w1t[:, 0:8 * 512].rearrange("p (a b) -> p a b", a=8)
            load_cast(g, w1v, w1k, w1src[:, :, fb * 512:(fb + 1) * 512])
            w2t, w2k = g.wring.next()
            w2v = w2t[:, 0:4 * D].rearrange("p (a b) -> p a b", a=4)
            load_cast(g, w2v, w2k, w2src[:, fb * 4:(fb + 1) * 4, :])
            return (w1v, w1k, w2v, w2k)

        nxt = wl(0)
        for fb in range(FB):
            w1v, w1k, w2v, w2k = nxt
            if fb + 1 < FB:
                nxt = wl(fb + 1)
            if hook is not None:
                hook(fb)
            for tile in range(NT):
                j = 0 if tile == 0 else 1
                ht, hk = hid.next()
                for fc in range(4):
                    bank = _psb(g)
                    for k in range(8):
                        P.op("pe", lambda E, fc=fc, k=k, tile=tile, bank=bank: E.matmul(
                            ps[:, bank, :], lhsT=w1v[:, k, fc * 128:(fc + 1) * 128],
                            rhs=g.hbuf[:, k, tile * TT:(tile + 1) * TT], start=(k == 0), stop=(k == 7)),
                            reads=[w1k, ("h", tile, k)], writes=[("ps", bank)])
                    rt, rk = rl.next()
                    P.op("act", lambda E, rt=rt, bank=bank: E.activation(out=rt[:], in_=ps[:, bank, :], func=ACT.Relu),
                         reads=[("ps", bank)], writes=[rk])
                    if fc % 2 == 0:
                        P.op("act", lambda E, rt=rt, ht=ht, fc=fc: E.activation(out=ht[:, fc, :], in_=rt[:], func=ACT.Square),
                             reads=[rk], writes=[(hk, fc)])
                    else:
                        P.op("pool", lambda E, rt=rt, ht=ht, fc=fc: E.tensor_tensor(out=ht[:, fc, :], in0=rt[:], in1=rt[:],
                                                                                    op=ALU.mult),
                             reads=[rk], writes=[(hk, fc)])
                for oc in range(8):
                    bank = _psb(g)
                    for fc in range(4):
                        P.op("pe", lambda E, oc=oc, fc=fc, bank=bank, ht=ht: E.matmul(
                            ps[:, bank, :], lhsT=w2v[:, fc, oc * 128:(oc + 1) * 128], rhs=ht[:, fc, :],
                            start=(fc == 0), stop=(fc == 3)), reads=[w2k, (hk, fc)], writes=[("ps", bank)])
                    P.op("dve", lambda E, oc=oc, bank=bank, tile=tile, j=j: E.scalar_tensor_tensor(
                        out=X[:, oc, tile * TT:(tile + 1) * TT], in0=ps[:, bank, :],
                        scalar=g.mod[l][:, 40 + oc, j:j + 1], in1=X[:, oc, tile * TT:(tile + 1) * TT],
                        op0=ALU.mult, op1=ALU.add), reads=[("ps", bank), ("X", tile), ("mod", l)], writes=[("X", tile)])
        P.barrier()


def stage_norm1_from_X(g, l, X):
    nc, P = g.nc, g.P
    with ExitStack() as ss:
        g.sqring = Ring(nc, ss, "sq", 2, [128, 8, TT], BF16)
        g.rsring = Ring(nc, ss, "rs", 2, [128, TT], F32)
        g.xnring = Ring(nc, ss, "xn", 2, [128, 8, TT], F32)
        pend_ = None
        for tile in range(NT):
            P.dma(lambda E, tile=tile: E.dma_start(
                out=g.xs[:, :, tile * TT:(tile + 1) * TT].rearrange("c p t -> p c t"), in_=X[:, :, tile * TT:(tile + 1) * TT]),
                reads=[("X", tile)], writes=[("xs", tile)], q="sp")
            nxt_ = norm_tile(g, X[:, :, tile * TT:(tile + 1) * TT], ("X", tile), tile, g.gm[l][:, 0], g.mod[l][:, 0:8], l,
                             [("gm", l), ("mod", l)], None, defer=True)
            if pend_ is not None:
                pend_()
            pend_ = nxt_
        pend_()
        P.barrier()


def stage_final(g, X):
    nc, P, ps = g.nc, g.P, g.ps
    with ExitStack() as ss:
        sqr = Ring(nc, ss, "sq", 2, [128, 8, TT], BF16)
        rsr = Ring(nc, ss, "rs", 2, [128, TT], F32)
        xnr = Ring(nc, ss, "xn", 2, [128, 8, TT], F32)
        otr = Ring(nc, ss, "ot", 2, [128, D], F32)
        fg = g.vecs[:, VOFF["fing"]:VOFF["fing"] + 8]
        pend_ = None
        for tile in range(NT):
            xT = X[:, :, tile * TT:(tile + 1) * TT]
            xkey = ("X", tile)
            sq, sqk = sqr.next()
            P.op("act", lambda E, sq=sq, xT=xT: E.activation(out=sq[:], in_=xT, func=ACT.Square), reads=[xkey], writes=[sqk])
            bank = _psb(g)
            for c in range(8):
                P.op("pe", lambda E, c=c, sq=sq, bank=bank: E.matmul(ps[:, bank, :], lhsT=g.ones_bf[:], rhs=sq[:, c, :],
                                                                     start=(c == 0), stop=(c == 7)),
                     reads=[sqk, "consts"], writes=[("ps", bank)])
            rs, rsk = rsr.next()
            P.op("act", lambda E, rs=rs, bank=bank: E.activation(out=rs[:], in_=ps[:, bank, :], func=ACT.Sqrt,
                                                                 bias=g.epst[:], scale=1.0 / D),
                 reads=[("ps", bank), "consts"], writes=[rsk])
            P.op("dve", lambda E, rs=rs: E.reciprocal(out=rs[:], in_=rs[:]), reads=[rsk], writes=[rsk])
            xn, xnk = xnr.next()
            P.op("dve", lambda E, xn=xn, xT=xT, rs=rs: E.tensor_tensor(
                out=xn[:], in0=xT, in1=rs[:].unsqueeze(1).to_broadcast([128, 8, TT]), op=ALU.mult),
                reads=[xkey, rsk], writes=[(xnk, c_) for c_ in range(8)])
            for c in range(8):
                P.op("act", lambda E, c=c, xn=xn: E.activation(out=xn[:, c, :], in_=xn[:, c, :], func=ACT.Copy,
                                                               scale=fg[:, c:c + 1]),
                     reads=[(xnk, c), "vecs"], writes=[(xnk, c)])
            def p2_(tile=tile, xn=xn, xnk=xnk):
                for sub in range(4):
                    ot, otk = otr.next()
                    for hb in range(2):
                        bank = _psb(g)
                        for c4 in range(4):
                            c = hb * 4 + c4
                            P.op("pe", lambda E, xn=xn, c=c, c4=c4, sub=sub, bank=bank: E.transpose(
                                ps[:, bank, c4 * 128:(c4 + 1) * 128], xn[:, c, sub * 128:(sub + 1) * 128], g.ident[:]),
                                reads=[(xnk, c), "consts"], writes=[("ps", bank)])
                        if hb == 0:
                            P.op("act", lambda E, ot=ot, bank=bank: E.activation(out=ot[:, 0:512], in_=ps[:, bank, :], func=ACT.Copy),
                                 reads=[("ps", bank)], writes=[(otk, 0)])
                        else:
                            P.op("dve", lambda E, ot=ot, bank=bank: E.tensor_copy(out=ot[:, 512:1024], in_=ps[:, bank, :]),
                                 reads=[("ps", bank)], writes=[(otk, 1)])
                    tt = tile * 4 + sub
                    if tt < 4:
                        dst = g.d["yp"][tt * 128:(tt + 1) * 128, :]
                    else:
                        dst = g.d["ys"][(tt - 4) * 128:(tt - 3) * 128, :]
                    P.dma(lambda E, ot=ot, dst=dst: E.dma_start(out=dst, in_=ot[:]), reads=[(otk, 0), (otk, 1)],
                          writes=[], q="sp", final=True)
            if pend_ is not None:
                pend_()
            pend_ = p2_
        pend_()
        P.barrier()


def stage_attn(g, dbl=None):
    nc, P, ps = g.nc, g.P, g.ps
    with ExitStack() as ss:
        wr = Ring(nc, ss, "wa", 3, [128, 8, 128], BF16)
        obuf = ss.enter_context(_sbt(nc, "obuf", [128, 4, T], BF16))
        kTr = Ring(nc, ss, "kT", 2, [128, T], BF16)
        qTr = Ring(nc, ss, "qT", 2, [128, T], BF16)
        Vtr = Ring(nc, ss, "Vt", 2, [128, 20, 2, 65], BF16)
        kcxr = Ring(nc, ss, "kcx", 2, [128, 256], BF16)
        Vcxr = Ring(nc, ss, "Vcx", 2, [128, 2, 2, 65], BF16)
        ckr = Ring(nc, ss, "ck", 2, [128, 2, 2, 64], F32)
        cvr = Ring(nc, ss, "cv", 2, [128, 2, 2, 64], F32)
        Tbr = Ring(nc, ss, "Tb", 2, [128, 2, 16, 64], F32)
        oh = ss.enter_context(_sbt(nc, "oh", [32, 64, 64], F32))
        rpbT = ss.enter_context(_sbt(nc, "rpbT", [32, 8, 17], F32))
        ktm = Ring(nc, ss, "ktm", 2, [128, 128], F32)
        vtm = Ring(nc, ss, "vtm", 2, [128, 128], F32)
        tmpr = Ring(nc, ss, "stmp", 2, [128, 5, 128], F32)
        pTr = Ring(nc, ss, "pT", 3, [128, 7, 128], BF16)
        otm = Ring(nc, ss, "otm", 2, [128, 128], F32)
        rcr = Ring(nc, ss, "rcp", 2, [128, 2], F32)

        P.dma(lambda E: E.dma_start(out=oh[:], in_=g.d["onehot"][:, :, :]), writes=["oh"])
        P.op("pool", lambda E: E.memset(rpbT[:], 0.0), writes=["rpbT"])
        P.op("pool", lambda E: E.memset(rpbT[0:1, :, :], 1.0), writes=["rpbT"])
        for h_ in range(8):
            P.dma(lambda E, h_=h_: E.dma_start(out=rpbT[1:32, h_, 1:16], in_=g.d["na_rpb"][h_].rearrange("r c -> c r"),
                                               allow_slow_non_contiguous=True), writes=["rpbT"], q="sp")
        win = g.d["cd_w_in"].rearrange("(kc p) n -> p kc n", p=128)

        for hp in range(4):
            def wload(col0):
                wt, wk = wr.next()
                load_cast(g, wt[:], wk, win[:, :, col0:col0 + 128], [128, 8, 128])
                return wt, wk

            wq, wqk = wload(hp * 128)
            wk_, wkk = wload(512 + hp * 128)
            wv_, wvk = wload(1024 + hp * 128)
            kT, kTk = kTr.next()
            qT, qTk = qTr.next()
            Vt, Vtk = Vtr.next()
            Tb, Tbk = Tbr.next()
            b0 = 4
            psv = ps[:, b0:b0 + 4, :].rearrange("p b (q x) -> p (b q) x", x=32)
            tk = [("ps", b0 + i) for i in range(4)]
            import os
            _dbg = int(os.environ.get("KDBG", "255"))
            for qc in (range(64) if _dbg & 2 else ()):
                P.op("pe", lambda E, qc=qc: E.matmul(psv[0:64, qc, :], lhsT=oh[:, qc, :], rhs=rpbT[:, 2 * hp:2 * hp + 2, 0:16],
                                                     start=True, stop=True), reads=["oh", "rpbT"], writes=[tk[qc // 16]])
                P.op("pe", lambda E, qc=qc: E.matmul(psv[64:128, qc, :], lhsT=oh[:, qc, :], rhs=rpbT[:, 2 * hp:2 * hp + 2, 1:17],
                                                     start=True, stop=True), reads=["oh", "rpbT"], writes=[tk[qc // 16]])
            for hh in range(2):
                P.op("act" if hh == 0 else "dve", lambda E, hh=hh: (
                    E.activation(out=Tb[:, hh, :, :], in_=psv[:, :, hh * 16:(hh + 1) * 16].rearrange("p q a -> p a q"), func=ACT.Copy)
                    if hh == 0 else
                    E.tensor_copy(out=Tb[:, hh, :, :], in_=psv[:, :, hh * 16:(hh + 1) * 16].rearrange("p q a -> p a q"))),
                    reads=tk, writes=[Tbk])
            g.bank = 0
            ck, ckk = ckr.next()
            cv, cvk = cvr.next()
            kcx, kcxk = kcxr.next()
            Vcx, Vcxk = Vcxr.next()
            for lc in range(2):
                P.dma(lambda E, ck=ck, lc=lc: E.dma_start(
                    out=ck[:, lc], in_=g.d["cache_k"][2 * hp:2 * hp + 2, lc * 128:(lc + 1) * 128, :].rearrange("h p d -> p h d")),
                    writes=[ckk])
                P.dma(lambda E, cv=cv, lc=lc: E.dma_start(
                    out=cv[:, lc], in_=g.d["cache_v"][2 * hp:2 * hp + 2, lc * 128:(lc + 1) * 128, :].rearrange("h p d -> p h d")),
                    writes=[cvk], q="sp")
            for lc in (range(2) if _dbg & 4 else ()):
                bank = _psb(g) % 4
                P.op("pe", lambda E, lc=lc, bank=bank, ck=ck: E.transpose(
                    ps[:, bank, 0:128], ck[:, lc, :, :].rearrange("p h d -> p (h d)"), g.ident[:]),
                    reads=[ckk, "consts"], writes=[("ps", bank)])
                P.op("act", lambda E, lc=lc, bank=bank, kcx=kcx: E.activation(
                    out=kcx[:, lc * 128:(lc + 1) * 128], in_=ps[:, bank, 0:128], func=ACT.Copy),
                    reads=[("ps", bank)], writes=[kcxk])
            P.op("pool", lambda E, Vcx=Vcx: E.memset(Vcx[:, :, :, 64:65], 1.0), writes=[Vcxk])
            P.op("pool", lambda E, Vcx=Vcx, cv=cv: E.tensor_copy(out=Vcx[:, :, :, 0:64], in_=cv[:]), reads=[cvk], writes=[Vcxk])
            P.op("pool", lambda E, Vt=Vt: E.memset(Vt[:, :, :, 64:65], 1.0), writes=[(Vtk, u_) for u_ in range(20)])

            def evk(oc, tile, bank, kT=kT):
                P.op("act", lambda E: E.activation(out=kT[:, tile * TT:(tile + 1) * TT], in_=ps[:, bank, :], func=ACT.Copy),
                     reads=[("ps", bank)], writes=[(kTk, tile)])

            def evq(oc, tile, bank, qT=qT):
                P.op("dve", lambda E: E.tensor_copy(out=qT[:, tile * TT:(tile + 1) * TT], in_=ps[:, bank, :]),
                     reads=[("ps", bank)], writes=[(qTk, tile)])

            def banked(f):
                def w(oc, tile, bank):
                    return f(oc, tile, bank)
                return w

            def lin(wt, wkey, ev):
                for tile in range(NT):
                    bank = _psb(g) % 4
                    for k in range(8):
                        P.op("pe", lambda E, k=k, tile=tile, bank=bank: E.matmul(
                            ps[:, bank, :], lhsT=wt[:, k, :], rhs=g.hbuf[:, k, tile * TT:(tile + 1) * TT],
                            start=(k == 0), stop=(k == 7)), reads=[wkey, ("h", tile, k)], writes=[("ps", bank)])
                    ev(0, tile, bank)

            if _dbg & 16:
                lin(wk_, wkk, evk)
                lin(wq, wqk, evq)
            for u in (range(20) if _dbg & 8 else ()):
                bank = _psb(g) % 4
                for k in range(8):
                    P.op("pe", lambda E, k=k, u=u, bank=bank: E.matmul(
                        ps[:, bank, 0:128], lhsT=g.hbuf[:, k, u * 128:(u + 1) * 128], rhs=wv_[:, k, :],
                        start=(k == 0), stop=(k == 7)), reads=[wvk, ("h", u // 4, k)], writes=[("ps", bank)])
                P.op("act", lambda E, u=u, bank=bank, Vt=Vt: E.activation(
                    out=Vt[:, u, :, 0:64], in_=ps[:, bank, 0:128].rearrange("p (h d) -> p h d", h=2), func=ACT.Copy),
                    reads=[("ps", bank)], writes=[(Vtk, u)])
                if u < 4 and (_dbg & 32):
                    vt, vtk_ = vtm.next()
                    P.op("act", lambda E, bank=bank, vt=vt: E.activation(out=vt[:], in_=ps[:, bank, 0:128], func=ACT.Copy),
                         reads=[("ps", bank)], writes=[vtk_])
                    for hh_ in (range(2) if not os.environ.get("KD2") else ()):
                        P.dma(lambda E, u=u, vt=vt, hh_=hh_: E.dma_start(
                            out=g.d["nv"][u // 2, 2 * hp + hh_, (u % 2) * 128:(u % 2 + 1) * 128, :],
                            in_=vt[:, hh_ * 64:(hh_ + 1) * 64]), reads=[vtk_], final=True, q="sp")
                    if not (_dbg & 128):
                        continue
                    bank2 = _psb(g) % 4
                    for k in range(8):
                        P.op("pe", lambda E, k=k, u=u, bank2=bank2: E.matmul(
                            ps[:, bank2, 0:128], lhsT=g.hbuf[:, k, u * 128:(u + 1) * 128], rhs=wk_[:, k, :],
                            start=(k == 0), stop=(k == 7)), reads=[wkk, ("h", 0, k)], writes=[("ps", bank2)])
                    kt, ktk_ = ktm.next()
                    P.op("act", lambda E, bank2=bank2, kt=kt: E.activation(out=kt[:], in_=ps[:, bank2, 0:128], func=ACT.Copy),
                         reads=[("ps", bank2)], writes=[ktk_])
                    for hh_ in range(2):
                        P.dma(lambda E, u=u, kt=kt, hh_=hh_: E.dma_start(
                            out=g.d["nk"][u // 2, 2 * hp + hh_, (u % 2) * 128:(u % 2 + 1) * 128, :],
                            in_=kt[:, hh_ * 64:(hh_ + 1) * 64]), reads=[ktk_], final=True)

            if dbl is not None:
                dbl(hp, ss)
            iters = []
            for u in range(20):
                for hh in range(2):
                    iters.append((u, hh))
            psv8 = lambda bs: ps[:, bs:bs + 2, :].rearrange("p b (s q) -> p (b s) q", q=128)

            def stage_A(u, hh):
                hs = slice(64 * hh, 64 * hh + 64)
                qtile = (u * 128) // TT
                rows = []
                if u < 4:
                    slots = [2 * (u // 2), 2 * (u // 2) + 1]
                    ctx = False
                else:
                    for par in range(2):
                        r = 2 * (u - 4) + par
                        r0 = min(max(r - 4, 0), 24)
                        rows.append((par, r, r0 // 2, (r0 + 7) // 2, r0 % 2 == 1))
                    p_lo = min(x[2] for x in rows)
                    p_hi = max(x[3] for x in rows)
                    slots = [4 + p for p in range(p_lo, p_hi + 1)]
                    ctx = True
                n = len(slots)
                g.sbank = (getattr(g, "sbank", 0) + 2) % 4
                bs = g.sbank
                pv = psv8(bs)
                bk = lambda j: ("ps", bs + j // 4)
                for j, sl in enumerate(slots):
                    P.op("pe", lambda E, j=j, sl=sl: E.matmul(
                        pv[:, j, :], lhsT=kT[hs, sl * 128:(sl + 1) * 128], rhs=qT[hs, u * 128:(u + 1) * 128],
                        start=True, stop=True), reads=[(kTk, sl // 4), (qTk, qtile)], writes=[bk(j)])
                if ctx:
                    for lc in range(2):
                        P.op("pe", lambda E, lc=lc: E.matmul(
                            pv[:, n + lc, :], lhsT=kcx[hs, lc * 128:(lc + 1) * 128], rhs=qT[hs, u * 128:(u + 1) * 128],
                            start=True, stop=True), reads=[kcxk, (qTk, qtile)], writes=[bk(n + lc)])
                pT, pTk = pTr.next()
                allb = [("ps", bs), ("ps", bs + 1)]
                if u >= 4:
                    tmp, tmpk = tmpr.next()
                    for (par, r, pl_r, ph_r, odd) in rows:
                        qs = slice(par * 64, par * 64 + 64)
                        j_lo, j_hi = pl_r - p_lo, ph_r - p_lo
                        n_r = j_hi - j_lo + 1
                        a0 = 2 * pl_r - r + 8
                        P.op("dve", lambda E, qs=qs, j_lo=j_lo, j_hi=j_hi, n_r=n_r, a0=a0: E.scalar_tensor_tensor(
                            out=tmp[:, j_lo:j_hi + 1, qs], in0=pv[:, j_lo:j_hi + 1, qs], scalar=0.125,
                            in1=Tb[:, hh, a0:a0 + 2 * n_r:2, :], op0=ALU.mult, op1=ALU.add),
                            reads=allb + [Tbk], writes=[tmpk])
                        for j in range(n):
                            if j < j_lo or j > j_hi:
                                P.op("dve", lambda E, j=j, qs=qs: E.memset(tmp[:, j, qs], NEG), writes=[tmpk])
                        if odd:
                            P.op("dve", lambda E, qs=qs, j_lo=j_lo: E.memset(tmp[0:64, j_lo, qs], NEG), writes=[tmpk])
                            P.op("dve", lambda E, qs=qs, j_hi=j_hi: E.memset(tmp[64:128, j_hi, qs], NEG), writes=[tmpk])
                    P.op("act", lambda E: E.activation(out=pT[:, 0:n, :], in_=tmp[:, 0:n, :], func=ACT.Exp),
                         reads=[tmpk], writes=[pTk])
                    P.op("act", lambda E: E.activation(out=pT[:, n:n + 2, :], in_=pv[:, n:n + 2, :], func=ACT.Exp, scale=0.125),
                         reads=allb, writes=[pTk])
                else:
                    P.op("act", lambda E: E.activation(out=pT[:, 0:n, :], in_=pv[:, 0:n, :], func=ACT.Exp, scale=0.125),
                         reads=allb, writes=[pTk])
                return (pT, pTk, slots, n, ctx)

            def stage_B(u, hh, st):
                pT, pTk, slots, n, ctx = st
                bank_o = 4 + (u % 2) * 2
                nn = n + (2 if ctx else 0)
                for j in range(nn):
                    if j < n:
                        rhs = Vt[:, slots[j], hh, :]
                        rk = (Vtk, slots[j])
                    else:
                        rhs = Vcx[:, j - n, hh, :]
                        rk = Vcxk
                    P.op("pe", lambda E, j=j, rhs=rhs: E.matmul(
                        ps[:, bank_o, hh * 65:hh * 65 + 65], lhsT=pT[:, j, :], rhs=rhs, start=(j == 0), stop=(j == nn - 1)),
                        reads=[pTk, rk], writes=[("ps", bank_o)])

            def stage_F(u):
                bank_o = 4 + (u % 2) * 2
                bank_t = bank_o + 1
                rc, rck = rcr.next()
                ot, otk = otm.next()
                P.op("dve", lambda E: E.reciprocal(out=rc[:], in_=ps[:, bank_o, 64:130:65]),
                     reads=[("ps", bank_o)], writes=[rck])
                for hh in range(2):
                    P.op("dve", lambda E, hh=hh: E.tensor_scalar(
                        out=ot[:, hh * 64:(hh + 1) * 64], in0=ps[:, bank_o, hh * 65:hh * 65 + 64], scalar1=rc[:, hh:hh + 1],
                        scalar2=None, op0=ALU.mult), reads=[("ps", bank_o), rck], writes=[otk])
                P.op("pe", lambda E: E.transpose(ps[:, bank_t, 0:128], ot[:], g.ident[:]),
                     reads=[otk, "consts"], writes=[("ps", bank_t)])
                P.op("act", lambda E: E.activation(out=obuf[:, hp, u * 128:(u + 1) * 128], in_=ps[:, bank_t, 0:128],
                                                   func=ACT.Copy), reads=[("ps", bank_t)], writes=[("o", hp, u // 4)])

            st_next = stage_A(*iters[0])
            for ii, (u, hh) in enumerate(iters):
                st_cur = st_next
                if ii + 1 < len(iters):
                    st_next = stage_A(*iters[ii + 1])
                stage_B(u, hh, st_cur)
                if hh == 1:
                    stage_F(u)
            for tile in range(NT):
                P.dma(lambda E, tile=tile: E.dma_start(out=g.obs[hp, :, tile * TT:(tile + 1) * TT], in_=obuf[:, hp, tile * TT:(tile + 1) * TT]),
                      reads=[("o", hp, tile)], writes=[("obs", hp, tile)], q="sp")
        P.barrier()
        g.bank = 0


NCH = (64, 64, 512)
BST = (0, 65, 130)
NCOL = 643
TWO_PI = 2.0 * math.pi


def s5_load_raw(g, es, ss):
    nc, P, ps = g.nc, g.P, g.ps
    s = G()
    g.s5 = s

    def t(name, shape, dt=F32):
        return es.enter_context(_sbt(nc, "s5_" + name, shape, dt))

    s.lamre, s.lamim, s.lstep = t("lamre", [128, 2, 16]), t("lamim", [128, 2, 16]), t("lstep", [128, 2, 16])
    s.s0re, s.s0im = t("s0re", [128, 2, 16]), t("s0im", [128, 2, 16])
    s.Bre, s.Bim = t("Bre", [128, 2, 16, 16]), t("Bim", [128, 2, 16, 16])
    s.Cre, s.Cim = t("Cre", [128, 2, 16, 16]), t("Cim", [128, 2, 16, 16])
    if True:
        Cld = ss.enter_context(_sbt(nc, "s5_Cld", [128, 2, 2, 4, 128], F32))
        Lld = ss.enter_context(_sbt(nc, "s5_Lld", [64, 4, 128], F32))
        lsr = ss.enter_context(_sbt(nc, "s5_lsr", [1, 64], F32))
        ones1 = ss.enter_context(_sbt(nc, "s5_ones1", [1, 128], F32))
        s.raw_stage = (Cld, Lld, lsr, ones1)

    def issue():
        P.op("pool", lambda E: E.memset(ones1[:], 1.0), writes=["ones1"])
        for ri, src in enumerate(("ssm_c_re", "ssm_c_im")):
            for d in range(2):
                for dup in range(2):
                    P.dma(lambda E, ri=ri, src=src, d=d, dup=dup: E.dma_start(
                        out=Cld[:, ri, d, :, dup * 64:(dup + 1) * 64],
                        in_=g.d[src][d].rearrange("(cc gl) co n -> (gl co) cc n", gl=8)), writes=[("Cld", ri, d)])
        for ti, src in enumerate(("ssm_lambda_re", "ssm_lambda_im", "state_re", "state_im")):
            for dup in range(2):
                P.dma(lambda E, ti=ti, src=src, dup=dup: E.dma_start(
                    out=Lld[:, ti, dup * 64:(dup + 1) * 64], in_=g.d[src].rearrange("d g n -> (d g) n")), writes=[("Lld", ti)])
        P.dma(lambda E: E.dma_start(out=lsr[:], in_=g.d["ssm_log_step"].rearrange("d g -> (d g)").unsqueeze(0)), writes=["lsr"])
        for g2 in range(2):
            hs = slice(64 * g2, 64 * g2 + 64)
            for (dst, src) in ((s.Bre, "ssm_b_re"), (s.Bim, "ssm_b_im")):
                for d in range(2):
                    P.dma(lambda E, dst=dst, src=src, hs=hs, g2=g2, d=d: E.dma_start(
                        out=dst[hs, d], in_=g.d[src][d, g2::2, :, :].rearrange("p n c -> n p c")), writes=["s5raw"])
    s.issue = issue


def s5_raw_xpose(g):
    nc, P, ps = g.nc, g.P, g.ps
    s = g.s5
    Cld, Lld, lsr, ones1 = s.raw_stage
    if True:
        for ri, dst in enumerate((s.Cre, s.Cim)):
            for d in range(2):
                for cc in range(4):
                    bank = _psb(g)
                    P.op("pe", lambda E, ri=ri, d=d, cc=cc, bank=bank: E.transpose(ps[:, bank, 0:128], Cld[:, ri, d, cc, :], g.ident[:]),
                         reads=[("Cld", ri, d), "consts"], writes=[("ps", bank)])
                    for g2 in range(2):
                        hs = slice(64 * g2, 64 * g2 + 64)
                        P.op("act", lambda E, dst=dst, d=d, cc=cc, bank=bank, g2=g2, hs=hs: E.activation(
                            out=dst[hs, d, 4 * cc:4 * cc + 4, :],
                            in_=ps[hs, bank, 0:128].rearrange("p (pl g2 co) -> p pl g2 co", pl=4, g2=2)[:, :, g2, :], func=ACT.Copy),
                            reads=[("ps", bank)], writes=["s5raw"])
        for ti, dst in enumerate((s.lamre, s.lamim, s.s0re, s.s0im)):
            bank = _psb(g)
            P.op("pe", lambda E, ti=ti, bank=bank: E.transpose(ps[:, bank, 0:64], Lld[:, ti, :], g.ident[0:64, 0:64]),
                 reads=[("Lld", ti), "consts"], writes=[("ps", bank)])
            for g2 in range(2):
                hs = slice(64 * g2, 64 * g2 + 64)
                P.op("act", lambda E, dst=dst, bank=bank, g2=g2, hs=hs: E.activation(
                    out=dst[hs], in_=ps[hs, bank, 0:64].rearrange("p (d pp g2) -> p d pp g2", d=2, g2=2)[:, :, :, g2], func=ACT.Copy),
                    reads=[("ps", bank)], writes=["s5raw"])
        bank = _psb(g)
        P.op("pe", lambda E, bank=bank: E.matmul(ps[:, bank, 0:64], lhsT=ones1[:], rhs=lsr[:], start=True, stop=True),
             reads=["ones1", "lsr"], writes=[("ps", bank)])
        for g2 in range(2):
            hs = slice(64 * g2, 64 * g2 + 64)
            P.op("act", lambda E, bank=bank, g2=g2, hs=hs: E.activation(
                out=s.lstep[hs], in_=ps[hs, bank, 0:64].rearrange("p (d pp g2) -> p d pp g2", d=2, g2=2)[:, :, :, g2], func=ACT.Copy),
                reads=[("ps", bank)], writes=["s5raw"])
        P.barrier()


def s5_emit(g, n=None):
    s = g.s5
    k = len(s.pending) if n is None else min(n, len(s.pending))
    for eng, fn in s.pending[:k]:
        g.P.op(eng, fn, reads=["s5raw", "s5d"], writes=["s5d"])
    s.pending = s.pending[k:]


def s5_setup(g, ss):
    nc, P = g.nc, g.P
    s = g.s5

    def t(name, shape, dt=F32):
        return ss.enter_context(_sbt(nc, "s5_" + name, shape, dt))

    s.Pre, s.Pim = t("Pre", [128, 2, 16, 5]), t("Pim", [128, 2, 16, 5])
    s.nPim = t("nPim", [128, 2, 16, 5])
    s.Bbre, s.Bbim = t("Bbre", [128, 2, 16, 16]), t("Bbim", [128, 2, 16, 16])
    s.r4, s.cph, s.sph = t("r4", [128, 2, 16]), t("cph", [128, 2, 16]), t("sph", [128, 2, 16])
    s.nCim = t("nCim", [128, 2, 16, 16])
    a, b, c, e_ = t("ta", [128, 2, 16]), t("tb", [128, 2, 16]), t("tc", [128, 2, 16]), t("td", [128, 2, 16])
    ti = ss.enter_context(_sbt(nc, "s5_ti", [128, 2, 16], mybir.dt.int32))
    K = "s5d"

    s.pending = []

    def V(fn, eng="pool"):
        s.pending.append((eng, _freeze(fn)))

    step, rho, th = t("stp", [128, 2, 16]), t("rho", [128, 2, 16]), t("th", [128, 2, 16])
    V(lambda E: E.activation(out=step[:], in_=s.lstep[:], func=ACT.Exp), "act")
    V(lambda E: E.tensor_tensor(out=rho[:], in0=s.lamre[:], in1=step[:], op=ALU.mult))
    V(lambda E: E.tensor_tensor(out=th[:], in0=s.lamim[:], in1=step[:], op=ALU.mult))
    er = t("er", [128, 2, 16])
    V(lambda E: E.activation(out=er[:], in_=rho[:], func=ACT.Exp), "act")

    def sincos(out, shift):
        V(lambda E: E.tensor_scalar(out=e_[:], in0=th[:], scalar1=1.0 / TWO_PI, scalar2=shift, op0=ALU.mult, op1=ALU.add))
        V(lambda E: E.tensor_copy(out=ti[:], in_=e_[:]))
        V(lambda E: E.tensor_copy(out=step[:], in_=ti[:]))
        V(lambda E: E.tensor_tensor(out=e_[:], in0=e_[:], in1=step[:], op=ALU.subtract))
        V(lambda E: E.tensor_scalar(out=step[:], in0=e_[:], scalar1=0.5, scalar2=None, op0=ALU.is_gt))
        V(lambda E: E.tensor_tensor(out=e_[:], in0=e_[:], in1=step[:], op=ALU.subtract))
        V(lambda E: E.tensor_scalar(out=step[:], in0=e_[:], scalar1=-0.5, scalar2=None, op0=ALU.is_lt))
        V(lambda E: E.tensor_tensor(out=e_[:], in0=e_[:], in1=step[:], op=ALU.add))
        V(lambda E: E.activation(out=out, in_=e_[:], func=ACT.Sin, scale=TWO_PI), "act")

    sincos(s.Pim[:, :, :, 1], 0.0)
    sincos(s.Pre[:, :, :, 1], 0.25)
    V(lambda E: E.tensor_tensor(out=s.Pre[:, :, :, 1], in0=s.Pre[:, :, :, 1], in1=er[:], op=ALU.mult))
    V(lambda E: E.tensor_tensor(out=s.Pim[:, :, :, 1], in0=s.Pim[:, :, :, 1], in1=er[:], op=ALU.mult))
    V(lambda E: E.memset(s.Pre[:, :, :, 0], 1.0))
    V(lambda E: E.memset(s.Pim[:, :, :, 0], 0.0))
    for k in range(2, 5):
        V(lambda E, k=k: E.tensor_tensor(out=a[:], in0=s.Pre[:, :, :, k - 1], in1=s.Pre[:, :, :, 1], op=ALU.mult))
        V(lambda E, k=k: E.tensor_tensor(out=b[:], in0=s.Pim[:, :, :, k - 1], in1=s.Pim[:, :, :, 1], op=ALU.mult))
        V(lambda E, k=k: E.tensor_tensor(out=s.Pre[:, :, :, k], in0=a[:], in1=b[:], op=ALU.subtract))
        V(lambda E, k=k: E.tensor_tensor(out=a[:], in0=s.Pre[:, :, :, k - 1], in1=s.Pim[:, :, :, 1], op=ALU.mult))
        V(lambda E, k=k: E.tensor_tensor(out=b[:], in0=s.Pim[:, :, :, k - 1], in1=s.Pre[:, :, :, 1], op=ALU.mult))
        V(lambda E, k=k: E.tensor_tensor(out=s.Pim[:, :, :, k], in0=a[:], in1=b[:], op=ALU.add))
    V(lambda E: E.tensor_scalar(out=s.nPim[:], in0=s.Pim[:], scalar1=-1.0, scalar2=None, op0=ALU.mult))
    V(lambda E: E.tensor_scalar(out=s.nCim[:], in0=s.Cim[:], scalar1=-1.0, scalar2=None, op0=ALU.mult))
    cre, cim, den = t("cre", [128, 2, 16]), t("cim", [128, 2, 16]), t("den", [128, 2, 16])
    V(lambda E: E.tensor_scalar(out=a[:], in0=s.Pre[:, :, :, 1], scalar1=-1.0, scalar2=None, op0=ALU.add))
    V(lambda E: E.tensor_tensor(out=den[:], in0=s.lamre[:], in1=s.lamre[:], op=ALU.mult))
    V(lambda E: E.tensor_tensor(out=b[:], in0=s.lamim[:], in1=s.lamim[:], op=ALU.mult))
    V(lambda E: E.tensor_tensor(out=den[:], in0=den[:], in1=b[:], op=ALU.add))
    V(lambda E: E.reciprocal(out=den[:], in_=den[:]), "dve")
    V(lambda E: E.tensor_tensor(out=cre[:], in0=a[:], in1=s.lamre[:], op=ALU.mult))
    V(lambda E: E.tensor_tensor(out=b[:], in0=s.Pim[:, :, :, 1], in1=s.lamim[:], op=ALU.mult))
    V(lambda E: E.tensor_tensor(out=cre[:], in0=cre[:], in1=b[:], op=ALU.add))
    V(lambda E: E.tensor_tensor(out=cre[:], in0=cre[:], in1=den[:], op=ALU.mult))
    V(lambda E: E.tensor_tensor(out=cim[:], in0=s.Pim[:, :, :, 1], in1=s.lamre[:], op=ALU.mult))
    V(lambda E: E.tensor_tensor(out=b[:], in0=a[:], in1=s.lamim[:], op=ALU.mult))
    V(lambda E: E.tensor_tensor(out=cim[:], in0=cim[:], in1=b[:], op=ALU.subtract))
    V(lambda E: E.tensor_tensor(out=cim[:], in0=cim[:], in1=den[:], op=ALU.mult))
    bc = lambda x: x[:].unsqueeze(3).to_broadcast([128, 2, 16, 16])
    tb1 = t("tb1", [128, 2, 16, 16])
    V(lambda E: E.tensor_tensor(out=s.Bbre[:], in0=s.Bre[:], in1=bc(cre), op=ALU.mult))
    V(lambda E: E.tensor_tensor(out=tb1[:], in0=s.Bim[:], in1=bc(cim), op=ALU.mult))
    V(lambda E: E.tensor_tensor(out=s.Bbre[:], in0=s.Bbre[:], in1=tb1[:], op=ALU.subtract))
    V(lambda E: E.tensor_tensor(out=s.Bbim[:], in0=s.Bim[:], in1=bc(cre), op=ALU.mult))
    V(lambda E: E.tensor_tensor(out=tb1[:], in0=s.Bre[:], in1=bc(cim), op=ALU.mult))
    V(lambda E: E.tensor_tensor(out=s.Bbim[:], in0=s.Bbim[:], in1=tb1[:], op=ALU.add))
    V(lambda E: E.activation(out=s.r4[:], in_=rho[:], func=ACT.Exp, scale=4.0), "act")
    V(lambda E: E.reciprocal(out=a[:], in_=s.r4[:]), "dve")
    V(lambda E: E.tensor_tensor(out=s.cph[:], in0=s.Pre[:, :, :, 4], in1=a[:], op=ALU.mult))
    V(lambda E: E.tensor_tensor(out=s.sph[:], in0=s.Pim[:, :, :, 4], in1=a[:], op=ALU.mult))


S5_ITS = [(cc_, h_, d_) for cc_ in range(4) for h_ in range(2) for d_ in range(2)]


def s5_doubling(g, cc_, d_, CS, d1, d2, rk):
    P = g.P
    s = g.s5
    p0_ = 4 * cc_
    if d_ == 0:
        view = lambda bi: CS[:, :, :, BST[bi]:BST[bi] + NCH[bi] + 1]
    else:
        view = lambda bi: CS[:, :, :, BST[bi] + NCH[bi]:(BST[bi] - 1 if BST[bi] > 0 else None):-1]
    T = view(2)
    P.op("pool", lambda E: E.memset(T[:, 0, :, 0:1], 1.0), writes=[rk])
    P.op("pool", lambda E: E.memset(T[:, 1, :, 0:1], 0.0), writes=[rk])
    P.op("pool", lambda E: E.tensor_copy(out=T[:, 0, :, 1:2], in_=s.cph[:, d_, p0_:p0_ + 4].unsqueeze(2)), reads=["s5d"], writes=[rk])
    P.op("pool", lambda E: E.tensor_copy(out=T[:, 1, :, 1:2], in_=s.sph[:, d_, p0_:p0_ + 4].unsqueeze(2)), reads=["s5d"], writes=[rk])
    m = 1
    while m < 512:
        cm = T[:, 0:1, :, m:m + 1].to_broadcast([128, 2, 4, m])
        sm = T[:, 1:2, :, m:m + 1].to_broadcast([128, 2, 4, m])
        lo, hi = slice(1, m + 1), slice(m + 1, 2 * m + 1)
        P.op("pool", lambda E: E.tensor_tensor(out=d1[:, :, :, 0:m], in0=T[:, :, :, lo], in1=cm, op=ALU.mult), reads=[rk], writes=["d1"])
        P.op("pool", lambda E: E.tensor_tensor(out=d2[:, :, :, 0:m], in0=T[:, :, :, lo], in1=sm, op=ALU.mult), reads=[rk], writes=["d2"])
        P.op("pool", lambda E: E.tensor_tensor(out=T[:, 0, :, hi], in0=d1[:, 0, :, 0:m], in1=d2[:, 1, :, 0:m], op=ALU.subtract), reads=["d1", "d2"], writes=[rk])
        P.op("pool", lambda E: E.tensor_tensor(out=T[:, 1, :, hi], in0=d1[:, 1, :, 0:m], in1=d2[:, 0, :, 0:m], op=ALU.add), reads=["d1", "d2"], writes=[rk])
        m *= 2
    for bi in range(2):
        P.op("pool", lambda E, bi=bi: E.tensor_copy(out=view(bi), in_=T[:, :, :, 0:65]), reads=[rk], writes=[rk])
    for h_ in range(2):
        itn = S5_ITS.index((cc_, h_, d_))
        for cs in range(2):
            P.dma(lambda E, cs=cs, h_=h_, itn=itn: E.dma_start(
                out=g.rots[itn, cs], in_=CS[:, cs, 2 * h_:2 * h_ + 2, :].rearrange("p a b -> p (a b)")),
                reads=[rk], writes=[("rots", itn, cs)], q="pool")


def stage_s5(g):
    nc, P, ps = g.nc, g.P, g.ps
    with ExitStack() as ss:
        s = g.s5

        def t(name, shape, dt=F32):
            return ss.enter_context(_sbt(nc, "s5m_" + name, shape, dt))

        wr = Ring(nc, ss, "ws5", 2, [128, 8, 128], BF16)
        ygr = Ring(nc, ss, "ygt", 2, [128, TT], BF16)
        U4 = t("U4", [128, 4, 640], BF16)
        yacc = t("yacc", [128, T])
        XB = t("XB", [128, 2, 2, 4, NCOL], BF16)
        Sres = [t("Sre0", [128, 2, NCOL]), t("Sre1", [128, 2, NCOL])]
        Sims = [t("Sim0", [128, 2, NCOL]), t("Sim1", [128, 2, NCOL])]
        t1, t2 = t("t1", [128, 2, NCOL]), t("t2", [128, 2, NCOL])
        cTs = [t("cT0", [128, 2, NCOL]), t("cT1", [128, 2, NCOL])]
        sTs = [t("sT0", [128, 2, NCOL]), t("sT1", [128, 2, NCOL])]
        ones643 = t("ones643", [128, NCOL])
        P.op("pool", lambda E: E.memset(ones643[:], 1.0), writes=["ones643"])
        its = S5_ITS

        def load_rot(itn):
            rk = ("rot", itn % 2)
            P.dma(lambda E: E.dma_start(out=cTs[itn % 2][:].rearrange("p a b -> p (a b)"), in_=g.rots[itn, 0]),
                  reads=[("rots", itn, 0)], writes=[rk])
            P.dma(lambda E: E.dma_start(out=sTs[itn % 2][:].rearrange("p a b -> p (a b)"), in_=g.rots[itn, 1]),
                  reads=[("rots", itn, 1)], writes=[rk])

        load_rot(0)
        Ats = [t("At0", [128, 2, NCOL]), t("At1", [128, 2, NCOL])]
        Gre, Gim = t("Gre", [128, 4, 7, 32]), t("Gim", [128, 4, 7, 32])
        Cpre, CpimN = t("Cpre", [128, 4, 32]), t("CpimN", [128, 4, 32])
        Cpim = t("Cpim", [128, 4, 32])
        Min = t("Min", [128, 2, 4, 128], BF16)
        Ws = t("Ws", [128, 2, 2, 4, 128], BF16)
        Cup = t("Cup", [128, 2, 2, 4, 4, 32], BF16)
        cu1, cu2 = t("cu1", [128, 4, 4, 32]), t("cu2", [128, 4, 4, 32])
        g1, g2t = t("g1", [128, 4, 4, 16]), t("g2t", [128, 4, 4, 16])
        win = g.d["cd_w_in"].rearrange("(kc p) n -> p kc n", p=128)
        sd = VOFF["ssmd"]

        for cc in range(4):
            wt, wk = wr.next()
            load_cast(g, wt[:], wk, win[:, :, 1536 + cc * 128:1536 + (cc + 1) * 128], [128, 8, 128])
            for tile in range(NT):
                bank = _psb(g)
                for k in range(8):
                    P.op("pe", lambda E, k=k, tile=tile, bank=bank: E.matmul(
                        ps[:, bank, :], lhsT=wt[:, k, :], rhs=g.hbuf[:, k, tile * TT:(tile + 1) * TT],
                        start=(k == 0), stop=(k == 7)), reads=[wk, ("h", tile, k)], writes=[("ps", bank)])
                P.op("act", lambda E, tile=tile, bank=bank, cc=cc: E.activation(
                    out=yacc[:, tile * TT:(tile + 1) * TT], in_=ps[:, bank, :], func=ACT.Copy,
                    scale=g.vecs[:, sd + cc:sd + cc + 1]), reads=[("ps", bank), "vecs"], writes=[("yacc", tile)])
            for pl in range(4):
                b0 = _psb(g)
                if b0 % 2 == 1:
                    b0 = _psb(g)
                _psb(g)
                for hf, (c0, n) in enumerate(((0, 512), (512, 128))):
                    for i in range(4):
                        for k in range(8):
                            P.op("pe", lambda E, i=i, k=k, pl=pl, c0=c0, n=n, b0=b0, hf=hf: E.matmul(
                                ps[32 * i:32 * i + 32, b0 + hf, 0:n], lhsT=wt[:, k, 32 * pl:32 * pl + 32],
                                rhs=g.hbuf[:, k, 4 * c0 + i:4 * (c0 + n - 1) + i + 1:4], start=(k == 0), stop=(k == 7), tile_position=(0, 32 * i)),
                                reads=[wk] + [("h", tt_, k) for tt_ in range(NT)], writes=[("ps", b0 + hf)])
                    P.op("act" if hf == 0 else "dve", lambda E, pl=pl, c0=c0, n=n, b0=b0, hf=hf: (
                        E.activation(out=U4[:, pl, c0:c0 + n], in_=ps[:, b0 + hf, 0:n], func=ACT.Copy) if hf == 0 else
                        E.tensor_copy(out=U4[:, pl, c0:c0 + n], in_=ps[:, b0 + hf, 0:n])),
                        reads=[("ps", b0 + hf)], writes=[("U4", pl)])

            def iter_body(cc, half, d):
                p0 = 4 * cc + 2 * half
                itn_ = S5_ITS.index((cc, half, d))
                Sre, Sim, At = Sres[itn_ % 2], Sims[itn_ % 2], Ats[itn_ % 2]
                SK, AK = ("S", itn_ % 2), ("At", itn_ % 2)
                KT = ("s5tab", d)
                rd = ["s5d", "s5raw"]
                P.op("pool", lambda E: E.memset(Gre[:, 0:2], 0.0), writes=["G"])
                P.op("pool", lambda E: E.memset(Gim[:, 0:2], 0.0), writes=["G"])
                P.op("pool", lambda E: E.memset(Cpre[:, 0:2], 0.0), writes=["Cp"])
                P.op("pool", lambda E: E.memset(Cpim[:, 0:2], 0.0), writes=["Cp"])
                for hh in range(2):
                    hs = slice(64 * hh, 64 * hh + 64)
                    cs = slice(16 * hh, 16 * hh + 16)
                    if d == 0:
                        pk = lambda A: A[hs, d, p0:p0 + 2, 3::-1]
                        ms = slice(0, 4)
                    else:
                        pk = lambda A: A[hs, d, p0:p0 + 2, 0:4]
                        ms = slice(3, 7)
                    Pb = lambda A: pk(A).unsqueeze(3).to_broadcast([64, 2, 4, 16])
                    Bb = lambda A: A[hs, d, p0:p0 + 2, :].unsqueeze(2).to_broadcast([64, 2, 4, 16])
                    P.op("dve", lambda E, Pb=Pb, Bb=Bb, hs=hs: E.tensor_tensor(out=g1[hs, 0:2], in0=Pb(s.Pre), in1=Bb(s.Bbre), op=ALU.mult),
                         reads=rd, writes=["g1"])
                    P.op("dve", lambda E, Pb=Pb, Bb=Bb, hs=hs: E.tensor_tensor(out=g2t[hs, 0:2], in0=Pb(s.Pim), in1=Bb(s.Bbim), op=ALU.mult),
                         reads=rd, writes=["g2t"])
                    P.op("dve", lambda E, hs=hs, ms=ms, cs=cs: E.tensor_tensor(out=Gre[hs, 0:2, ms, cs], in0=g1[hs, 0:2], in1=g2t[hs, 0:2], op=ALU.subtract),
                         reads=["g1", "g2t"], writes=["G"])
                    P.op("dve", lambda E, Pb=Pb, Bb=Bb, hs=hs: E.tensor_tensor(out=g1[hs, 0:2], in0=Pb(s.Pre), in1=Bb(s.Bbim), op=ALU.mult),
                         reads=rd, writes=["g1"])
                    P.op("dve", lambda E, Pb=Pb, Bb=Bb, hs=hs: E.tensor_tensor(out=g2t[hs, 0:2], in0=Pb(s.Pim), in1=Bb(s.Bbre), op=ALU.mult),
                         reads=rd, writes=["g2t"])
                    P.op("dve", lambda E, hs=hs, ms=ms, cs=cs: E.tensor_tensor(out=Gim[hs, 0:2, ms, cs], in0=g1[hs, 0:2], in1=g2t[hs, 0:2], op=ALU.add),
                         reads=["g1", "g2t"], writes=["G"])
                    P.op("dve", lambda E, hs=hs, cs=cs: E.tensor_copy(out=Cpre[hs, 0:2, cs], in_=s.Cre[hs, d, p0:p0 + 2, :]), reads=rd, writes=["Cp"])
                    P.op("dve", lambda E, hs=hs, cs=cs: E.tensor_copy(out=Cpim[hs, 0:2, cs], in_=s.Cim[hs, d, p0:p0 + 2, :]), reads=rd, writes=["Cp"])
                P.op("dve", lambda E: E.tensor_scalar(out=CpimN[:, 0:2], in0=Cpim[:, 0:2], scalar1=-1.0, scalar2=None, op0=ALU.mult),
                     reads=["Cp"], writes=["CpN"])
                for pl2 in range(2):
                    pl = 2 * half + pl2
                    bank = _psb(g)
                    for j in range(4):
                        P.op("pe", lambda E, j=j, pl2=pl2, bank=bank: E.matmul(
                            ps[:, bank, j * 32:(j + 1) * 32], lhsT=Gre[:, pl2, 3 - j:7 - j, :].rearrange("p m c -> p (m c)"),
                            rhs=Cpre[:, pl2, :], start=True, stop=False), reads=["G", "Cp"], writes=[("ps", bank)])
                        P.op("pe", lambda E, j=j, pl2=pl2, bank=bank: E.matmul(
                            ps[:, bank, j * 32:(j + 1) * 32], lhsT=Gim[:, pl2, 3 - j:7 - j, :].rearrange("p m c -> p (m c)"),
                            rhs=CpimN[:, pl2, :], start=False, stop=True), reads=["G", "CpN"], writes=[("ps", bank)])
                    P.op("act", lambda E, pl=pl, bank=bank, d=d: E.activation(out=Min[:, d, pl, :], in_=ps[:, bank, 0:128], func=ACT.Copy),
                         reads=[("ps", bank)], writes=[KT])
                    jw = 3 if d == 0 else 0
                    for ri, Gx in enumerate((Gre, Gim)):
                        bank = _psb(g)
                        P.op("pe", lambda E, Gx=Gx, pl2=pl2, bank=bank, jw=jw: E.transpose(
                            ps[:, bank, 0:128], Gx[:, pl2, 3 - jw:7 - jw, :].rearrange("p m c -> p (m c)"), g.ident[:]),
                            reads=["G", "consts"], writes=[("ps", bank)])
                        P.op("act", lambda E, ri=ri, pl=pl, bank=bank, d=d: E.activation(out=Ws[:, d, ri, pl, :], in_=ps[:, bank, 0:128], func=ACT.Copy),
                             reads=[("ps", bank)], writes=[KT])
                if d == 0:
                    pj = lambda A: A[:, d, p0:p0 + 2, 1:5]
                else:
                    pj = lambda A: A[:, d, p0:p0 + 2, 4:0:-1]
                Pj = lambda A: pj(A).unsqueeze(3).to_broadcast([128, 2, 4, 32])
                Cj = lambda A: A[:, 0:2, :].unsqueeze(2).to_broadcast([128, 2, 4, 32])
                hsl = slice(2 * half, 2 * half + 2)
                P.op("dve", lambda E, Pj=Pj, Cj=Cj: E.tensor_tensor(out=cu1[:, 0:2], in0=Cj(Cpre), in1=Pj(s.Pre), op=ALU.mult), reads=rd + ["Cp"], writes=["cu1"])
                P.op("dve", lambda E, Pj=Pj, Cj=Cj: E.tensor_tensor(out=cu2[:, 0:2], in0=Cj(Cpim), in1=Pj(s.Pim), op=ALU.mult), reads=rd + ["Cp"], writes=["cu2"])
                P.op("dve", lambda E, hsl=hsl, d=d: E.tensor_tensor(out=Cup[:, d, 0, hsl], in0=cu1[:, 0:2], in1=cu2[:, 0:2], op=ALU.subtract),
                     reads=["cu1", "cu2"], writes=[KT])
                P.op("dve", lambda E, Pj=Pj, Cj=Cj: E.tensor_tensor(out=cu1[:, 0:2], in0=Cj(Cpre), in1=Pj(s.nPim), op=ALU.mult), reads=rd + ["Cp"], writes=["cu1"])
                P.op("dve", lambda E, Pj=Pj, Cj=Cj: E.tensor_tensor(out=cu2[:, 0:2], in0=Cj(CpimN), in1=Pj(s.Pre), op=ALU.mult), reads=rd + ["CpN"], writes=["cu2"])
                P.op("dve", lambda E, hsl=hsl, d=d: E.tensor_tensor(out=Cup[:, d, 1, hsl], in0=cu1[:, 0:2], in1=cu2[:, 0:2], op=ALU.add),
                     reads=["cu1", "cu2"], writes=[KT])

                off = 1 if d == 0 else 0
                for pl2 in range(2):
                    pl = 2 * half + pl2
                    for ri, Sx in enumerate((Sre, Sim)):
                        b0 = _psb(g)
                        if b0 % 2 == 1:
                            b0 = _psb(g)
                        _psb(g)
                        for hf, (c0, n) in enumerate(((0, 512), (512, 128))):
                            P.op("pe", lambda E, ri=ri, pl=pl, c0=c0, n=n, b0=b0, hf=hf, d=d: E.matmul(
                                ps[:, b0 + hf, 0:n], lhsT=Ws[:, d, ri, pl, :], rhs=U4[:, pl, c0:c0 + n], start=True, stop=True),
                                reads=[KT, ("U4", pl)], writes=[("ps", b0 + hf)])
                        P.op("act", lambda E, Sx=Sx, pl2=pl2, b0=b0, off=off: E.activation(
                            out=Sx[:, pl2, BST[0] + off:BST[0] + off + 64], in_=ps[:, b0, 0:64], func=ACT.Copy),
                            reads=[("ps", b0)], writes=[SK])
                        P.op("act", lambda E, Sx=Sx, pl2=pl2, b0=b0, off=off: E.activation(
                            out=Sx[:, pl2, BST[1] + off:BST[1] + off + 64], in_=ps[:, b0, 64:128], func=ACT.Copy),
                            reads=[("ps", b0)], writes=[SK])
                        P.op("act", lambda E, Sx=Sx, pl2=pl2, b0=b0, off=off: E.activation(
                            out=Sx[:, pl2, BST[2] + off:BST[2] + off + 384], in_=ps[:, b0, 128:512], func=ACT.Copy),
                            reads=[("ps", b0)], writes=[SK])
                        P.op("act", lambda E, Sx=Sx, pl2=pl2, b0=b0, off=off: E.activation(
                            out=Sx[:, pl2, BST[2] + off + 384:BST[2] + off + 512], in_=ps[:, b0 + 1, 0:128], func=ACT.Copy),
                            reads=[("ps", b0 + 1)], writes=[SK])
                for bi in range(3):
                    col = BST[bi] if d == 0 else BST[bi] + NCH[bi]
                    for Sx, s0 in ((Sre, s.s0re), (Sim, s.s0im)):
                        if bi < 2:
                            P.op("pool", lambda E, Sx=Sx, col=col: E.memset(Sx[:, :, col:col + 1], 0.0), writes=[SK])
                        else:
                            P.op("pool", lambda E, Sx=Sx, col=col, s0=s0, d=d: E.tensor_copy(
                                out=Sx[:, :, col:col + 1], in_=s0[:, d, p0:p0 + 2].unsqueeze(2)), reads=["s5raw"], writes=[SK])
                for pl2 in range(2):
                    P.op("act", lambda E, pl2=pl2, d=d: E.activation(out=At[:, pl2, :], in_=ones643[:], func=ACT.Copy,
                                                                    scale=s.r4[:, d, p0 + pl2:p0 + pl2 + 1]),
                         reads=["s5d", "ones643"], writes=[AK])
                for bi in range(3):
                    col = BST[bi] if d == 0 else BST[bi] + NCH[bi]
                    P.op("pool", lambda E, col=col: E.memset(At[:, :, col:col + 1], 0.0), writes=[AK])

                itn = its.index((cc, half, d))
                cT, sT = cTs[itn % 2], sTs[itn % 2]
                ROT = ("rot", itn % 2)


                def phase_b():
                    if itn + 1 < len(its):
                        load_rot(itn + 1)
                    P.op("dve", lambda E: E.tensor_tensor(out=t1[:], in0=Sre[:], in1=cT[:], op=ALU.mult), reads=[SK, ROT], writes=["t1"])
                    P.op("dve", lambda E: E.tensor_tensor(out=t2[:], in0=Sim[:], in1=sT[:], op=ALU.mult), reads=[SK, ROT], writes=["t2"])
                    P.op("dve", lambda E: E.tensor_tensor(out=t1[:], in0=t1[:], in1=t2[:], op=ALU.add), reads=["t1", "t2"], writes=["t1"])
                    P.op("dve", lambda E: E.tensor_tensor(out=t2[:], in0=Sim[:], in1=cT[:], op=ALU.mult), reads=[SK, ROT, "t1"], writes=["t2"])
                    P.op("dve", lambda E: E.tensor_tensor(out=Sim[:], in0=Sre[:], in1=sT[:], op=ALU.mult), reads=[SK, ROT, "t2"], writes=[SK])
                    P.op("dve", lambda E: E.tensor_tensor(out=t2[:], in0=t2[:], in1=Sim[:], op=ALU.subtract), reads=[SK, "t2"], writes=["t2"])
                    fl = lambda A: (A[:].rearrange("p a b -> p (a b)") if d == 0 else A[:].rearrange("p a b -> p (a b)")[:, ::-1])
                    P.op("dve", lambda E, fl=fl: E.tensor_tensor_scan(out=fl(Sre), data0=fl(At), data1=fl(t1), initial=0.0, op0=ALU.mult, op1=ALU.add),
                         reads=["t1", AK, SK], writes=[SK])
                    P.op("dve", lambda E, fl=fl: E.tensor_tensor_scan(out=fl(Sim), data0=fl(At), data1=fl(t2), initial=0.0, op0=ALU.mult, op1=ALU.add),
                         reads=["t2", AK, SK], writes=[SK])
                    hsl = slice(2 * half, 2 * half + 2)
                    P.op("dve", lambda E: E.tensor_tensor(out=t1[:], in0=Sre[:], in1=cT[:], op=ALU.mult), reads=[SK, ROT], writes=["t1"])
                    P.op("dve", lambda E: E.tensor_tensor(out=t2[:], in0=Sim[:], in1=sT[:], op=ALU.mult), reads=[SK, ROT], writes=["t2"])
                    P.op("dve", lambda E: E.tensor_tensor(out=t1[:], in0=t1[:], in1=t2[:], op=ALU.subtract), reads=["t1", "t2"], writes=["t1"])
                    P.op("dve", lambda E: E.tensor_tensor(out=t2[:], in0=Sim[:], in1=cT[:], op=ALU.mult), reads=[SK, ROT, "t1"], writes=["t2"])
                    P.op("dve", lambda E: E.tensor_tensor(out=Sim[:], in0=Sre[:], in1=sT[:], op=ALU.mult), reads=[SK, ROT, "t2"], writes=[SK])
                    P.op("dve", lambda E: E.tensor_tensor(out=t2[:], in0=t2[:], in1=Sim[:], op=ALU.add), reads=[SK, "t2"], writes=["t2"])
                    P.op("act", lambda E, d=d, hsl=hsl: E.activation(out=XB[:, d, 0, hsl, :], in_=t1[:], func=ACT.Copy), reads=["t1"], writes=[("XB", d)])
                    P.op("act", lambda E, d=d, hsl=hsl: E.activation(out=XB[:, d, 1, hsl, :], in_=t2[:], func=ACT.Copy), reads=["t2"], writes=[("XB", d)])
                    for sq in range(2):
                        col = BST[sq] + NCH[sq] if d == 0 else BST[sq]
                        for ri, tx in enumerate((t1, t2)):
                            P.op("act", lambda E, sq=sq, ri=ri, tx=tx, col=col, d=d: E.activation(
                                out=g.fin[:, sq, d, ri, p0:p0 + 2], in_=tx[:, :, col], func=ACT.Copy), reads=["t1", "t2"], writes=["fin"])
                return phase_b

            pend = None
            for half in range(2):
                for d in range(2):
                    nb = iter_body(cc, half, d)
                    if pend is not None:
                        pend()
                    pend = nb
            pend()

            for tile in range(NT):
                bank = _psb(g)
                if tile == 0:
                    pieces = [(0, 64, 0), (64, 64, 1)]
                else:
                    pieces = [(128 * tile, 128, 2)]
                for pl in range(4):
                    first = True
                    qs = slice(32 * pl, 32 * pl + 32)
                    for j in range(4):
                        for d in range(2):
                            P.op("pe", lambda E, j=j, d=d, pl=pl, qs=qs, tile=tile, bank=bank, first=first: E.matmul(
                                ps[qs, bank, j:j + 4 * 127 + 1:4], lhsT=Min[:, d, pl, j * 32:(j + 1) * 32], rhs=U4[:, pl, 128 * tile:128 * tile + 128],
                                start=first, stop=False, skip_group_check=True, tile_position=(0, 32 * pl)),
                                reads=[("s5tab", d), ("U4", pl)], writes=[("ps", bank)])
                            first = False
                            for (c0, n, bi) in pieces:
                                cl = c0 - (0, 64, 128)[bi]
                                xc = BST[bi] + cl + (0 if d == 0 else 1)
                                o0 = 4 * (c0 - 128 * tile) + j
                                for ri in range(2):
                                    P.op("pe", lambda E, j=j, d=d, pl=pl, qs=qs, bank=bank, xc=xc, n=n, o0=o0, ri=ri: E.matmul(
                                        ps[qs, bank, o0:o0 + 4 * (n - 1) + 1:4], lhsT=Cup[:, d, ri, pl, j, :], rhs=XB[:, d, ri, pl, xc:xc + n],
                                        start=False, stop=False, skip_group_check=True, tile_position=(0, 32 * pl)),
                                        reads=[("s5tab", d), ("XB", d)], writes=[("ps", bank)])
                P.op("dve", lambda E, tile=tile, bank=bank: E.tensor_tensor(
                    out=yacc[:, tile * TT:(tile + 1) * TT], in0=ps[:, bank, :], in1=yacc[:, tile * TT:(tile + 1) * TT], op=ALU.add),
                    reads=[("ps", bank), ("yacc", tile)], writes=[("yacc", tile)])
            for tile in range(NT):
                ys = yacc[:, tile * TT:(tile + 1) * TT]
                a1 = t1[:].rearrange("p a b -> p (a b)")[:, 0:TT]
                a2 = t2[:].rearrange("p a b -> p (a b)")[:, 0:TT]
                P.op("act", lambda E, ys=ys, a1=a1: E.activation(out=a1, in_=ys, func=ACT.Square), reads=[("yacc", tile), "t1"], writes=["t1"])
                P.op("dve", lambda E, a1=a1: E.tensor_scalar(out=a1, in0=a1, scalar1=0.044715, scalar2=1.0, op0=ALU.mult, op1=ALU.add), reads=["t1"], writes=["t1"])
                P.op("dve", lambda E, ys=ys, a1=a1: E.tensor_tensor(out=a1, in0=a1, in1=ys, op=ALU.mult), reads=["t1", ("yacc", tile)], writes=["t1"])
                P.op("act", lambda E, a1=a1, a2=a2: E.activation(out=a2, in_=a1, func=ACT.Sigmoid, scale=2.0 * math.sqrt(2.0 / math.pi)), reads=["t1", "t2"], writes=["t2"])
                ygt, ygk = ygr.next()
                P.op("dve", lambda E, ys=ys, a2=a2, ygt=ygt: E.tensor_tensor(out=ygt[:], in0=ys, in1=a2, op=ALU.mult),
                     reads=["t2", ("yacc", tile)], writes=[ygk])
                P.dma(lambda E, ygt=ygt, tile=tile, cc=cc: E.dma_start(out=g.ygs[cc, :, tile * TT:(tile + 1) * TT], in_=ygt[:]),
                      reads=[ygk], writes=[("ygs", cc, tile)], q="sp")
        fT = t("finT", [128, 2, 64])
        for g2 in range(2):
            bank = _psb(g)
            P.op("pe", lambda E, g2=g2, bank=bank: E.transpose(
                ps[:, bank, 0:64], g.fin[64 * g2:64 * g2 + 64].rearrange("p a b c e -> p (a b c e)"), g.ident[64 * g2:64 * g2 + 64, 64 * g2:64 * g2 + 64]),
                reads=["fin", "consts"], writes=[("ps", bank)])
            P.op("act", lambda E, g2=g2, bank=bank: E.activation(out=fT[:, g2, :], in_=ps[:, bank, 0:64], func=ACT.Copy),
                 reads=[("ps", bank)], writes=[("finT", g2)])
            for sq in range(2):
                for d in range(2):
                    for ri, nm in enumerate(("nre", "nim")):
                        r0 = ((sq * 2 + d) * 2 + ri) * 16
                        P.dma(lambda E, g2=g2, sq=sq, d=d, nm=nm, r0=r0: E.dma_start(
                            out=g.d[nm][sq, d, g2::2, :], in_=fT[r0:r0 + 16, g2, :]), reads=[("finT", g2)], final=True, q="sp")
        P.barrier()


def stage_glu(g):
    nc, P, ps = g.nc, g.P, g.ps
    with ExitStack() as ss:
        gw = ss.enter_context(_sbt(nc, "gw", [128, 4, 512], BF16))
        sg = Ring(nc, ss, "sg", 2, [128, TT], F32)
        ygbuf = ss.enter_context(_sbt(nc, "ygbuf", [128, 4, T], BF16))
        for k in range(4):
            for tile in range(NT):
                P.dma(lambda E, k=k, tile=tile: E.dma_start(out=ygbuf[:, k, tile * TT:(tile + 1) * TT], in_=g.ygs[k, :, tile * TT:(tile + 1) * TT]),
                      reads=[("ygs", k, tile)], writes=[("yg", k, tile)], q="sp")
                P.dma(lambda E, k=k, tile=tile: E.dma_start(out=g.mixbuf[:, k, tile * TT:(tile + 1) * TT], in_=g.obs[k, :, tile * TT:(tile + 1) * TT]),
                      reads=[("obs", k, tile)], writes=[("mix", k, tile)], q="sp")
        load_cast(g, gw[:], "gw", g.d["glu_w"].rearrange("(kc p) n -> p kc n", p=128), [128, 4, 512])
        gb = VOFF["glub"]
        for oc in range(4):
            for tile in range(NT):
                bank = _psb(g)
                for k in range(4):
                    P.op("pe", lambda E, oc=oc, k=k, tile=tile, bank=bank: E.matmul(
                        ps[:, bank, :], lhsT=gw[:, k, oc * 128:(oc + 1) * 128], rhs=ygbuf[:, k, tile * TT:(tile + 1) * TT],
                        start=(k == 0), stop=(k == 3)), reads=["gw", ("yg", k, tile)], writes=[("ps", bank)])
                st, sk = sg.next()
                P.op("act", lambda E, st=st, bank=bank, oc=oc: E.activation(out=st[:], in_=ps[:, bank, :], func=ACT.Sigmoid,
                                                                            bias=g.vecs[:, gb + oc:gb + oc + 1]),
                     reads=[("ps", bank), "vecs"], writes=[sk])
                P.op("dve", lambda E, st=st, oc=oc, tile=tile: E.tensor_tensor(
                    out=g.mixbuf[:, 4 + oc, tile * TT:(tile + 1) * TT], in0=ygbuf[:, oc, tile * TT:(tile + 1) * TT], in1=st[:], op=ALU.mult),
                    reads=[sk, ("yg", oc, tile)], writes=[("mix", 4 + oc, tile)])
        P.barrier()


IN_SPECS = [
    ("xin", [T, D]), ("condT", [128, 8, 2]), ("vecs", [128, NVEC]), ("ident", [128, 128]),
    ("onehot", [32, 64, 64]), ("rcnt", [4, T]),
    ("ada_w", [2, D, 6 * D]), ("mlp_w1", [2, D, 4 * D]), ("mlp_w2", [2, 4 * D, D]),
    ("ab_w_in", [D, 2 * D]), ("ab_w_out", [D, D]), ("cd_w_in", [D, 2 * D]), ("cd_w_out", [D, D]),
    ("pool_w", [4, 128, 128]), ("glu_w", [512, 512]),
    ("cache_k", [8, 256, 64]), ("cache_v", [8, 256, 64]), ("na_rpb", [8, 15, 31]),
    ("state_re", [2, 32, 64]), ("state_im", [2, 32, 64]),
    ("ssm_lambda_re", [2, 32, 64]), ("ssm_lambda_im", [2, 32, 64]), ("ssm_log_step", [2, 32]),
    ("ssm_b_re", [2, 32, 64, 16]), ("ssm_b_im", [2, 32, 64, 16]),
    ("ssm_c_re", [2, 32, 16, 64]), ("ssm_c_im", [2, 32, 16, 64]),
]
OUT_SPECS = [("yp", [512, D]), ("ys", [2048, D]), ("nk", [2, 8, 256, 64]), ("nv", [2, 8, 256, 64]),
             ("nre", [2, 2, 32, 64]), ("nim", [2, 2, 32, 64])]


def build(stop=None, debug=()):
    nc = bass.Bass("TRN2", target_bir_lowering=False)
    g = G()
    g.nc = nc
    g.d = {}
    for name, shape in IN_SPECS:
        g.d[name] = nc.dram_tensor(name, shape, F32, kind="ExternalInput").ap()
    for name, shape in OUT_SPECS:
        g.d[name] = nc.dram_tensor(name, shape, F32, kind="ExternalOutput").ap()
    g.xs = nc.dram_tensor("xs", [8, 128, T], F32, kind=("ExternalOutput" if "xs" in debug else "Internal")).ap()
    g.obs = nc.dram_tensor("obs", [4, 128, T], BF16, kind=("ExternalOutput" if "obs" in debug else "Internal")).ap()
    g.ygs = nc.dram_tensor("ygs", [4, 128, T], BF16, kind=("ExternalOutput" if "ygs" in debug else "Internal")).ap()
    if "hbuf" in debug:
        g.d["dbg_h"] = nc.dram_tensor("dbg_h", [8, 128, T], BF16, kind="ExternalOutput").ap()
    if "mix" in debug:
        g.d["dbg_mix"] = nc.dram_tensor("dbg_mix", [8, 128, T], BF16, kind="ExternalOutput").ap()
    if "mod" in debug:
        g.d["dbg_mod"] = nc.dram_tensor("dbg_mod", [2, 128, 96], F32, kind="ExternalOutput").ap()

    with ExitStack() as es:
        P = Prog(nc, es)
        g.P = P
        g.ps = es.enter_context(nc.psum_tensor("ps", [128, 8, 512], F32))
        g.bank = 0
        g.nbanks = 8

        def t(name, shape, dt=F32):
            return es.enter_context(_sbt(nc, "sb_" + name, shape, dt))

        g.ident = t("ident", [128, 128])
        g.ones_bf = t("ones_bf", [128, 128], BF16)
        g.epst = t("epst", [128, 1])
        g.vecs = t("vecs", [128, NVEC])
        condT = t("condT", [128, 8, 2])
        g.scT = t("scT", [128, 8, 2], BF16)
        g.mod = [t("mod%d" % l, [128, 48, 2]) for l in range(2)]
        g.gm = [t("gm%d" % l, [128, 2, 8, 2]) for l in range(2)]
        g.fin = t("fin", [128, 2, 2, 2, 16])
        g.hbuf = t("hbuf", [128, 8, T], BF16)

        P.dma(lambda E: E.dma_start(out=g.ident[:], in_=g.d["ident"][:, :]), writes=["consts"])
        P.dma(lambda E: E.dma_start(out=g.vecs[:], in_=g.d["vecs"][:, :]), writes=["vecs"], q="sp")
        P.dma(lambda E: E.dma_start(out=condT[:], in_=g.d["condT"][:, :, :]), writes=["condT"])
        P.op("pool", lambda E: E.memset(g.ones_bf[:], 1.0), writes=["consts"])
        P.op("pool", lambda E: E.memset(g.epst[:], EPS), writes=["consts"])
        P.op("act", lambda E: E.activation(out=g.scT[:], in_=condT[:], func=ACT.Silu), reads=["condT"], writes=["scT"])
        g.rots = nc.dram_tensor("rots", [16, 2, 128, 2 * NCOL], F32).ap()

        def dump_h():
            if "hbuf" in debug:
                for tile in range(NT):
                    P.dma(lambda E, tile=tile: E.dma_start(out=g.d["dbg_h"][:, :, tile * TT:(tile + 1) * TT].rearrange("c p t -> p c t"),
                                                           in_=g.hbuf[:, :, tile * TT:(tile + 1) * TT]),
                          reads=hkeys(tile), final=True)

        def dump_mix():
            if "mix" in debug:
                for tile in range(NT):
                    P.dma(lambda E, tile=tile: E.dma_start(out=g.d["dbg_mix"][:, :, tile * TT:(tile + 1) * TT].rearrange("c p t -> p c t"),
                                                           in_=g.mixbuf[:, :, tile * TT:(tile + 1) * TT]),
                          reads=[("mix", c, tile) for c in range(8)], final=True)

        def run():
            with ExitStack() as a0:
                g.aring = Ring(nc, a0, "wrA0", 3, [128, 4096], BF16)
                g.astg = Ring(nc, a0, "stA0", 2, [128, 4096], F32)
                g.nbanks = 7
                for cg in range(4):
                    adaln_group(g, 0, cg, 7)
                adaln_finish(g, 0, 7, sections=(0, 1))

                stage_in_norm1(g)
            if stop == "norm1":
                dump_h()
                return
            with ExitStack() as ms:
                g.mixbuf = ms.enter_context(_sbt(nc, "mixbuf0", [128, 8, T], BF16))
                ar_ = ExitStack()
                g.aring = Ring(nc, ar_, "wrA1", 3, [128, 4096], BF16)
                g.astg = Ring(nc, ar_, "stA1", 2, [128, 4096], F32)
                g.nbanks = 6
                sched = [(0, cg_) for cg_ in range(4, 12)] + [(1, cg_) for cg_ in range(12)]
                counts = [3, 3, 3, 3, 2, 2, 2, 2]

                def hook1(i):
                    k0 = sum(counts[:i])
                    for (l_, cg_) in sched[k0:k0 + counts[i]]:
                        adaln_group(g, l_, cg_, 7 if l_ == 0 else 6)
                        if (l_, cg_) == (0, 11):
                            adaln_finish(g, 0, 7, sections=(2, 3, 4, 5))
                        if (l_, cg_) == (1, 11):
                            adaln_finish(g, 1, 6)

                stage_mixer0(g, hook=hook1)
                ar_.close()
                g.nbanks = 8
                if stop == "mixer0":
                    dump_mix()
                    return
                stage_wout_norm2(g, 0, "ab_w_out", lambda k, tile: g.mixbuf[:, k, tile * TT:(tile + 1) * TT],
                                 lambda k, tile: [("mix", k, tile)])
                P.barrier()
            if stop == "wout0":
                dump_h()
                return
            s5s = ExitStack()
            s5stg = ExitStack()
            s5_load_raw(g, s5s, s5stg)
            with ExitStack() as xs_:
                X = xs_.enter_context(_sbt(nc, "X0", [128, 8, T], F32))
                stage_mlp(g, 0, X, hook=lambda fb: (g.s5.issue() if fb == 1 else None))
                if stop == "mlponly":
                    for tile in range(NT):
                        P.dma(lambda E, tile=tile: E.dma_start(
                            out=g.xs[:, :, tile * TT:(tile + 1) * TT].rearrange("c p t -> p c t"), in_=X[:, :, tile * TT:(tile + 1) * TT]),
                            reads=[("X", tile)], writes=[("xs", tile)], final=True)
                    P.barrier()
                    return
                stage_norm1_from_X(g, 1, X)
                P.barrier()
            if stop == "mlp0":
                dump_h()
                return
            s5_raw_xpose(g)
            s5stg.close()
            if True:
                s5_setup(g, s5s)
                dst = {}

                def dbl(hp, ss):
                    s5_emit(g)
                    if not dst:
                        dst["cs"] = _sbt_enter(nc, ss, "dCS", [128, 2, 4, NCOL], F32)
                        dst["d1"] = _sbt_enter(nc, ss, "dd1", [128, 2, 4, 256], F32)
                        dst["d2"] = _sbt_enter(nc, ss, "dd2", [128, 2, 4, 256], F32)
                    for d_ in range(2):
                        s5_doubling(g, hp, d_, dst["cs"], dst["d1"], dst["d2"], "drot")

                stage_attn(g, dbl=dbl)
                if stop == "attn":
                    return
                stage_s5(g)
                if stop == "s5":
                    return
                s5s.close()
            with ExitStack() as ms:
                g.mixbuf = ms.enter_context(_sbt(nc, "mixbuf1", [128, 8, T], BF16))
                stage_glu(g)
                if stop == "glu":
                    dump_mix()
                    return
                stage_wout_norm2(g, 1, "cd_w_out", lambda k, tile: g.mixbuf[:, k, tile * TT:(tile + 1) * TT],
                                 lambda k, tile: [("mix", k, tile)])
                P.barrier()
            with ExitStack() as xs_:
                X = xs_.enter_context(_sbt(nc, "X1", [128, 8, T], F32))
                stage_mlp(g, 1, X)
                stage_final(g, X)
                P.barrier()
        run()
        P.finish()
        g.n_inst = P.n_inst
        with nc.Block() as block:
            @block.sync
            def _(E):
                for f in P.ops["sp"]:
                    f(E)

            @block.tensor
            def _(E):
                for f in P.ops["pe"]:
                    f(E)

            @block.scalar
            def _(E):
                for f in P.ops["act"]:
                    f(E)

            @block.vector
            def _(E):
                for f in P.ops["dve"]:
                    f(E)

            @block.gpsimd
            def _(E):
                for f in P.ops["pool"]:
                    f(E)
    return nc, g


def _cols(v):
    v = np.asarray(v, np.float32)
    return np.ascontiguousarray(v.reshape(-1, 128).T)


def host_consts():
    ident = np.eye(128, dtype=np.float32)
    oh = np.zeros((32, 64, 64), np.float32)
    for qc in range(64):
        c0 = min(max(qc - 8, 0), 48)
        for kc in range(64):
            oh[0, qc, kc] = 0.0 if (c0 <= kc < c0 + 16) else NEG
            co = kc - qc + 15
            if 0 <= co < 31:
                oh[1 + co, qc, kc] = 1.0
    rc = np.zeros((4, T), np.float32)
    for gi, w in enumerate((2, 4, 8, 16)):
        for (s, n) in SEQS:
            pos = np.arange(n)
            lo = np.maximum(pos - w // 2, 0)
            hi = np.minimum(pos + (w - w // 2), n)
            rc[gi, s:s + n] = 1.0 / (hi - lo).astype(np.float32)
    return ident, oh, rc


def make_in_maps(inp):
    f = lambda k: np.asarray(inp[k], np.float32)
    ident, oh, rc = host_consts()
    vecs = np.zeros((128, NVEC), np.float32)
    for l in range(2):
        vecs[:, VOFF["n1g%d" % l]:VOFF["n1g%d" % l] + 8] = _cols(f("norm1_g")[l])
        vecs[:, VOFF["n2g%d" % l]:VOFF["n2g%d" % l] + 8] = _cols(f("norm2_g")[l])
        vecs[:, VOFF["adab%d" % l]:VOFF["adab%d" % l] + 48] = _cols(f("ada_b")[l])
    vecs[:, VOFF["fing"]:VOFF["fing"] + 8] = _cols(f("final_g"))
    vecs[:, VOFF["pscale"]:VOFF["pscale"] + 4] = _cols(f("pool_scale")[0])
    cw = f("conv_w")[0]
    for k in range(3):
        vecs[:, VOFF["convw"] + 4 * k:VOFF["convw"] + 4 * k + 4] = _cols(cw[k])
    vecs[:, VOFF["ssmd"]:VOFF["ssmd"] + 4] = _cols(f("ssm_d")[0])
    vecs[:, VOFF["glub"]:VOFF["glub"] + 4] = _cols(f("glu_b")[0])
    shared = {
        "vecs": vecs, "ident": ident, "onehot": oh, "rcnt": rc,
        "ada_w": f("ada_w"), "mlp_w1": f("mlp_w1"), "mlp_w2": f("mlp_w2"),
        "ab_w_in": f("ab_w_in")[0], "ab_w_out": f("ab_w_out")[0], "cd_w_in": f("cd_w_in")[0], "cd_w_out": f("cd_w_out")[0],
        "pool_w": f("pool_w")[0], "glu_w": f("glu_w")[0], "na_rpb": f("na_rpb")[0],
        "ssm_lambda_re": f("ssm_lambda_re")[0], "ssm_lambda_im": f("ssm_lambda_im")[0], "ssm_log_step": f("ssm_log_step")[0],
        "ssm_b_re": f("ssm_b_re")[0], "ssm_b_im": f("ssm_b_im")[0], "ssm_c_re": f("ssm_c_re")[0], "ssm_c_im": f("ssm_c_im")[0],
    }
    xp, xs, c, cctx = f("x_prompt"), f("x_sample"), f("c"), f("c_ctx")
    maps = []
    for i in range(8):
        b = i // 2
        m = dict(shared)
        m["xin"] = np.ascontiguousarray(np.concatenate([xp[2 * i], xp[2 * i + 1], xs[b]], axis=0))
        m["condT"] = np.ascontiguousarray(np.stack([_cols(cctx), _cols(c[b])], axis=-1))
        m["cache_k"] = np.ascontiguousarray(f("cache_k")[b, 0])
        m["cache_v"] = np.ascontiguousarray(f("cache_v")[b, 0])
        m["state_re"] = np.ascontiguousarray(f("state_s5_re")[b, 0])
        m["state_im"] = np.ascontiguousarray(f("state_s5_im")[b, 0])
        maps.append(m)
    return maps


_NC_CACHE = {}


def kernel(**inputs):
    if "nc" not in _NC_CACHE:
        _NC_CACHE["nc"] = build()[0]
    nc = _NC_CACHE["nc"]
    maps = make_in_maps(inputs)
    res = run_bass_kernel_spmd(nc, maps, core_ids=list(range(8))).results
    y_prompt = np.zeros((16, 256, D), np.float32)
    y_sample = np.zeros((4, 2048, D), np.float32)
    nk = np.zeros((16, 1, 8, 256, 64), np.float32)
    nv = np.zeros((16, 1, 8, 256, 64), np.float32)
    nre = np.zeros((16, 1, 2, 32, 64), np.float32)
    nim = np.zeros((16, 1, 2, 32, 64), np.float32)
    for i in range(8):
        r = res[i]
        y_prompt[2 * i:2 * i + 2] = np.asarray(r["yp"], np.float32).reshape(2, 256, D)
        if i % 2 == 0:
            y_sample[i // 2, 0:1024] = np.asarray(r["ys"], np.float32)[0:1024]
        else:
            y_sample[i // 2, 1024:2048] = np.asarray(r["ys"], np.float32)[1024:2048]
        nk[2 * i:2 * i + 2, 0] = np.asarray(r["nk"], np.float32)
        nv[2 * i:2 * i + 2, 0] = np.asarray(r["nv"], np.float32)
        nre[2 * i:2 * i + 2, 0] = np.asarray(r["nre"], np.float32)
        nim[2 * i:2 * i + 2, 0] = np.asarray(r["nim"], np.float32)
    return (y_prompt, y_sample, nk, nv, nre, nim)
```

```python
import math
from contextlib import ExitStack

import numpy as np
import ml_dtypes

import concourse.bass as bass
import concourse.mybir as mybir
from concourse.bass_utils import run_bass_kernel_spmd

F32 = mybir.dt.float32
BF16 = mybir.dt.bfloat16
ALU = mybir.AluOpType
ACT = mybir.ActivationFunctionType

ENGS = ("pe", "act", "dve", "pool", "sp")

D = 1024
T = 2560
NT = 5
TT = 512
SEQS = ((0, 256), (256, 256), (512, 2048))
EPS = 1e-6
NEG = -30000.0


import types


def _freeze(fn, depth=0):
    if not isinstance(fn, types.FunctionType) or depth > 4:
        return fn
    cells = None
    if fn.__closure__ is not None:
        new = []
        for c in fn.__closure__:
            try:
                v = c.cell_contents
            except ValueError:
                new.append(c)
                continue
            new.append(types.CellType(_freeze(v, depth + 1)))
        cells = tuple(new)
    defaults = fn.__defaults__
    if defaults is not None:
        defaults = tuple(_freeze(v, depth + 1) for v in defaults)
    nf = types.FunctionType(fn.__code__, fn.__globals__, fn.__name__, defaults, cells)
    nf.__kwdefaults__ = fn.__kwdefaults__
    return nf


class Prog:
    def __init__(self, nc, es, n_dma_sems=56):
        self.nc = nc
        self.ops = {e: [] for e in ENGS}
        self.cnt = {e: 0 for e in ENGS}
        self.sem = {e: es.enter_context(nc.semaphore("s_" + e)) for e in ENGS}
        self.dma_sems = [es.enter_context(nc.semaphore("d%d" % i)) for i in range(n_dma_sems)]
        self.dma_cnt = [0] * n_dma_sems
        self.dma_rr = 0
        self.sw_sems = [es.enter_context(nc.semaphore("w%d" % i)) for i in range(24)]
        self.sw_cnt = [0] * 24
        self.sw_rr = 0
        self.bg_sems = [es.enter_context(nc.semaphore("b%d" % i)) for i in range(16)]
        self.bg_cnt = [0] * 16
        self.bg_rr = 0
        self.last_w = {}
        self.readers = {}
        self.seen = {e: {} for e in ENGS}
        self.final_tokens = []
        self.n_inst = 0
        self.sticky = set()

    def _need(self, eng, tok, war=False, is_dma=False):
        kind, a, v = tok
        if kind == "e":
            if a == eng and not is_dma and (war or eng == "pe"):
                return
            sid = ("e", a)
            sem = self.sem[a]
        elif kind == "b":
            sid = ("b", a)
            sem = self.bg_sems[a]
        elif kind == "w":
            sid = ("w", a)
            sem = self.sw_sems[a]
        else:
            sid = ("d", a)
            sem = self.dma_sems[a]
        if self.seen[eng].get(sid, 0) >= v:
            return
        self.seen[eng][sid] = v
        self.ops[eng].append(lambda E, sem=sem, v=v: E.wait_ge(sem, v))

    def _deps(self, eng, reads, writes, is_dma=False):
        for k in reads:
            t = self.last_w.get(k)
            if t is not None:
                self._need(eng, t, is_dma=is_dma)
        for k in writes:
            t = self.last_w.get(k)
            if t is not None:
                self._need(eng, t, is_dma=is_dma)
            for t in self.readers.get(k, ()):
                self._need(eng, t, war=True, is_dma=is_dma)

    def _commit(self, tok, reads, writes):
        for k in reads:
            if k not in writes:
                self.readers.setdefault(k, []).append(tok)
        for k in writes:
            self.last_w[k] = tok
            self.readers[k] = []

    def op(self, eng, fn, reads=(), writes=()):
        fn = _freeze(fn)
        self._deps(eng, reads, writes)
        self.cnt[eng] += 1
        n = self.cnt[eng]
        sem = self.sem[eng]
        self.ops[eng].append(lambda E, fn=fn, sem=sem: fn(E).then_inc(sem, 1))
        self._commit(("e", eng, n), reads, writes)
        self.n_inst += 1

    def dma(self, fn, reads=(), writes=(), q="sp", final=False, bg=False):
        fn = _freeze(fn)
        if bg:
            sems, cnts, kind = self.bg_sems, self.bg_cnt, "b"
            i = self.bg_rr
            self.bg_rr = (self.bg_rr + 1) % len(sems)
        elif q == "pool":
            sems, cnts, kind = self.sw_sems, self.sw_cnt, "w"
            i = self.sw_rr
            self.sw_rr = (self.sw_rr + 1) % len(sems)
        else:
            sems, cnts, kind = self.dma_sems, self.dma_cnt, "d"
            i = self.dma_rr
            self.dma_rr = (self.dma_rr + 1) % len(sems)
        if cnts[i] > 0:
            self._need(q, (kind, i, cnts[i]))
        self._deps(q, reads, writes, is_dma=True)
        cnts[i] += 16
        v = cnts[i]
        sem = sems[i]
        self.ops[q].append(lambda E, fn=fn, sem=sem: fn(E).then_inc(sem, 16))
        tok = (kind, i, v)
        self._commit(tok, reads, writes)
        if final:
            self.final_tokens.append(tok)
        self.n_inst += 1

    def wait_keys(self, eng, keys):
        for k in keys:
            t = self.last_w.get(k)
            if t is not None:
                self._need(eng, t)

    def barrier(self):
        for e in ENGS:
            for o in ENGS:
                if o != "sp" and self.cnt[o] > 0:
                    self._need(e, ("e", o, self.cnt[o]), is_dma=(o == e))
            for i, c in enumerate(self.dma_cnt):
                if c > 0:
                    self._need(e, ("d", i, c))
            for i, c in enumerate(self.sw_cnt):
                if c > 0:
                    self._need(e, ("w", i, c))
        keep = {k: v for k, v in self.last_w.items() if k in self.sticky}
        self.last_w = keep
        self.readers = {}

    def finish(self):
        for tok in self.final_tokens:
            self._need("sp", tok)
        for e in ENGS:
            if e != "sp" and self.cnt[e] > 0:
                self._need("sp", ("e", e, self.cnt[e]))
        for i, c in enumerate(self.dma_cnt):
            if c > 0:
                self._need("sp", ("d", i, c))
        for i, c in enumerate(self.bg_cnt):
            if c > 0:
                self._need("sp", ("b", i, c))
        for i, c in enumerate(self.sw_cnt):
            if c > 0:
                self._need("sp", ("w", i, c))


_UNIQ = [0]


def _sbt_enter(nc, es, name, shape, dt):
    return es.enter_context(_sbt(nc, name, shape, dt))


def _sbt(nc, name, shape, dt):
    _UNIQ[0] += 1
    return nc.sbuf_tensor("%s_u%d" % (name, _UNIQ[0]), shape, dt)


class Ring:
    def __init__(self, nc, es, name, n, shape, dtype):
        _UNIQ[0] += 1
        name = "%s_%d_" % (name, _UNIQ[0])
        self.tiles = [es.enter_context(_sbt(nc, "%s%d" % (name, i), shape, dtype)) for i in range(n)]
        self.name = name
        self.i = 0

    def next(self):
        i = self.i
        self.i = (self.i + 1) % len(self.tiles)
        return self.tiles[i], (self.name, i)


class G:
    pass


def _vec_layout():
    off = {}
    o = 0

    def add(name, n):
        nonlocal o
        off[name] = o
        o += n

    for l in range(2):
        add("n1g%d" % l, 8)
        add("n2g%d" % l, 8)
        add("adab%d" % l, 48)
    add("fing", 8)
    add("pscale", 4)
    add("convw", 12)
    add("ssmd", 4)
    add("glub", 4)
    return off, o


VOFF, NVEC = _vec_layout()


def _psb(g):
    b = g.bank % g.nbanks
    g.bank = (b + 1) % g.nbanks
    return b


def load_cast(g, dst, dkey, src, stg_shape=None, eng=None, q=None):
    g.P.dma(lambda E: E.dma_start(out=dst, in_=src), writes=[dkey], q="pool")


def adaln_group(g, l, cg, bank):
    P, ps = g.P, g.ps
    wt, wkey = g.aring.next()
    wv = wt[:, 0:8 * 512].rearrange("p (a b) -> p a b", a=8)
    src = g.d["ada_w"][l].rearrange("(kc p) n -> p kc n", p=128)[:, :, cg * 512:(cg + 1) * 512]
    if getattr(g, "astg", None) is not None:
        st, sk = g.astg.next()
        sv = st[:, 0:8 * 512].rearrange("p (a b) -> p a b", a=8)
        P.dma(lambda E: E.dma_start(out=sv, in_=src), writes=[sk], q="sp")
        P.op("act", lambda E: E.activation(out=wv, in_=sv, func=ACT.Copy), reads=[sk], writes=[wkey])
    else:
        load_cast(g, wv, wkey, src)
    for c4 in range(4):
        ch = cg * 4 + c4
        for kc in range(8):
            P.op("pe", lambda E, kc=kc, c4=c4, ch=ch: E.matmul(
                ps[:, bank, ch * 2:ch * 2 + 2], lhsT=wv[:, kc, c4 * 128:(c4 + 1) * 128], rhs=g.scT[:, kc, :],
                start=(kc == 0), stop=(kc == 7)), reads=[wkey, "scT"], writes=[("ps", bank)])


def adaln_finish(g, l, bank, sections=range(6)):
    P, ps = g.P, g.ps
    mod = g.mod[l]
    a0 = VOFF["adab%d" % l]
    for sx in sections:
        for j in range(2):
            P.op("dve", lambda E, j=j, sx=sx: E.tensor_tensor(
                out=mod[:, sx * 8:(sx + 1) * 8, j], in0=ps[:, bank, sx * 16 + j:sx * 16 + 16:2],
                in1=g.vecs[:, a0 + sx * 8:a0 + (sx + 1) * 8], op=ALU.add),
                reads=[("ps", bank), "vecs"], writes=[("mod", l)])
        if sx in (1, 4):
            s2 = 0 if sx == 1 else 1
            gname = ("n1g%d" if sx == 1 else "n2g%d") % l
            gv = g.vecs[:, VOFF[gname]:VOFF[gname] + 8]
            for j in range(2):
                P.op("dve", lambda E, j=j, s2=s2, sx=sx, gv=gv: E.scalar_tensor_tensor(
                    out=g.gm[l][:, s2, :, j], in0=mod[:, sx * 8:(sx + 1) * 8, j], scalar=1.0, in1=gv,
                    op0=ALU.add, op1=ALU.mult), reads=[("mod", l), "vecs"], writes=[("gm", l)])


def stage_adaln(g, l):
    nc, P = g.nc, g.P
    with ExitStack() as ss:
        g.aring = Ring(nc, ss, "wrA", 3, [128, 4096], BF16)
        bank = _psb(g)
        for cg in range(12):
            adaln_group(g, l, cg, bank)
        adaln_finish(g, l, bank)
        P.barrier()


def norm_tile(g, xT, xkey, tile, gmv, shv, l, wkeys, out_keys, defer=False):
    nc, P, ps = g.nc, g.P, g.ps
    sq, sqk = g.sqring.next()
    P.op("act", lambda E: E.activation(out=sq[:], in_=xT, func=ACT.Square), reads=[xkey], writes=[sqk])
    bank = _psb(g)
    for c in range(8):
        P.op("pe", lambda E, c=c: E.matmul(ps[:, bank, :], lhsT=g.ones_bf[:], rhs=sq[:, c, :],
                                           start=(c == 0), stop=(c == 7)), reads=[sqk, "consts"], writes=[("ps", bank)])
    rs, rsk = g.rsring.next()
    P.op("act", lambda E: E.activation(out=rs[:], in_=ps[:, bank, :], func=ACT.Sqrt, bias=g.epst[:], scale=1.0 / D),
         reads=[("ps", bank), "consts"], writes=[rsk])
    P.op("dve", lambda E: E.reciprocal(out=rs[:], in_=rs[:]), reads=[rsk], writes=[rsk])
    xn, xnk = g.xnring.next()
    P.op("dve", lambda E: E.tensor_tensor(out=xn[:], in0=xT, in1=rs[:].unsqueeze(1).to_broadcast([128, 8, TT]),
                                          op=ALU.mult), reads=[xkey, rsk], writes=[xnk])
    def part2():
        j = 0 if tile == 0 else 1
        for c in range(8):
            eng = "pool" if c % 2 == 0 else "act"
            if eng == "pool":
                P.op("pool", lambda E, c=c: E.tensor_scalar(
                    out=g.hbuf[:, c, tile * TT:(tile + 1) * TT], in0=xn[:, c, :], scalar1=gmv[:, c, j:j + 1],
                    scalar2=shv[:, c, j:j + 1], op0=ALU.mult, op1=ALU.add),
                    reads=[xnk] + wkeys, writes=[("h", tile, c)])
            else:
                P.op("act", lambda E, c=c: E.activation(
                    out=g.hbuf[:, c, tile * TT:(tile + 1) * TT], in_=xn[:, c, :], func=ACT.Identity,
                    bias=shv[:, c, j:j + 1], scale=gmv[:, c, j:j + 1]),
                    reads=[xnk] + wkeys, writes=[("h", tile, c)])

    if defer:
        return part2
    part2()
    return None


def hkeys(tile):
    return [("h", tile, c) for c in range(8)]


def stage_in_norm1(g, hook=None):
    nc, P, ps = g.nc, g.P, g.ps
    with ExitStack() as ss:
        xtok = Ring(nc, ss, "xtok", 2, [128, D], F32)
        xTr = Ring(nc, ss, "xTa", 2, [128, 8, TT], F32)
        g.sqring = Ring(nc, ss, "sq", 2, [128, 8, TT], BF16)
        g.rsring = Ring(nc, ss, "rs", 2, [128, TT], F32)
        g.xnring = Ring(nc, ss, "xn", 2, [128, 8, TT], F32)
        pend_ = None
        for tile in range(NT):
            if hook is not None:
                hook(tile)
            xT, xk = xTr.next()
            for sub in range(4):
                tt = tile * 4 + sub
                xt, xtk = xtok.next()
                P.dma(lambda E, xt=xt, tt=tt: E.dma_start(out=xt[:], in_=g.d["xin"][tt * 128:(tt + 1) * 128, :]),
                      writes=[xtk], q="sp")
                for hb in range(2):
                    bank = _psb(g)
                    for c4 in range(4):
                        c = hb * 4 + c4
                        P.op("pe", lambda E, xt=xt, c=c, c4=c4, bank=bank: E.transpose(
                            ps[:, bank, c4 * 128:(c4 + 1) * 128], xt[:, c * 128:(c + 1) * 128], g.ident[:]),
                            reads=[xtk, "consts"], writes=[("ps", bank)])
                    P.op("act" if hb == 0 else "dve", lambda E, xT=xT, hb=hb, sub=sub, bank=bank: (
                        E.activation(out=xT[:, hb * 4:(hb + 1) * 4, sub * 128:(sub + 1) * 128],
                                     in_=ps[:, bank, :].rearrange("p (a b) -> p a b", a=4), func=ACT.Copy)
                        if hb == 0 else
                        E.tensor_copy(out=xT[:, hb * 4:(hb + 1) * 4, sub * 128:(sub + 1) * 128],
                                      in_=ps[:, bank, :].rearrange("p (a b) -> p a b", a=4))),
                        reads=[("ps", bank)], writes=[xk])
            P.dma(lambda E, xT=xT, tile=tile: E.dma_start(
                out=g.xs[:, :, tile * TT:(tile + 1) * TT].rearrange("c p t -> p c t"), in_=xT[:]),
                reads=[xk], writes=[("xs", tile)], q="sp")
            nxt_ = norm_tile(g, xT[:], xk, tile, g.gm[0][:, 0], g.mod[0][:, 0:8], 0, [("gm", 0), ("mod", 0)], None, defer=True)
            if pend_ is not None:
                pend_()
            pend_ = nxt_
        pend_()
        P.barrier()


def linear_fm(g, w_of, wkey_of, rhs_of, rkeys_of, nk, n_oc, tiles, evac, ncols=TT):
    P, ps = g.P, g.ps
    for oc in range(n_oc):
        for tile in tiles:
            bank = _psb(g)
            for k in range(nk):
                P.op("pe", lambda E, oc=oc, k=k, tile=tile, bank=bank: E.matmul(
                    ps[:, bank, 0:ncols], lhsT=w_of(oc, k), rhs=rhs_of(k, tile), start=(k == 0), stop=(k == nk - 1)),
                    reads=[wkey_of(oc)] + rkeys_of(k, tile), writes=[("ps", bank)])
            evac(oc, tile, bank)


PADW = 8
PSEQ = []
_o = PADW
for (_s, _n) in SEQS:
    PSEQ.append((_s, _n, _o))
    _o += _n + 2 * PADW
PTOT = _o - PADW


def segs_of_tile(tile):
    out = []
    a, b = tile * TT, (tile + 1) * TT
    for (s, n, p) in PSEQ:
        lo, hi = max(a, s), min(b, s + n)
        if lo < hi:
            out.append((lo, hi - lo, p + lo - s))
    return out


def stage_mixer0(g, hook=None):
    nc, P, ps = g.nc, g.P, g.ps
    with ExitStack() as ss:
        wr = Ring(nc, ss, "w0", 3, [128, 8, 128], BF16)
        abuf = ss.enter_context(_sbt(nc, "abuf", [128, PTOT], F32))
        p2 = ss.enter_context(_sbt(nc, "p2", [128, PTOT], F32))
        p4 = ss.enter_context(_sbt(nc, "p4", [128, PTOT], F32))
        rc = ss.enter_context(_sbt(nc, "rc", [128, T], F32))
        pooled = ss.enter_context(_sbt(nc, "pooled", [128, T], BF16))
        ctmp = Ring(nc, ss, "ctmp", 2, [128, TT], F32)
        pw = ss.enter_context(_sbt(nc, "pw", [128, 4, 128], BF16))
        P.op("pool", lambda E: E.memset(abuf[:], 0.0), writes=["abuf"])
        P.op("pool", lambda E: E.memset(p2[:], 0.0), writes=["p2"])
        P.op("pool", lambda E: E.memset(p4[:], 0.0), writes=["p4"])
        load_cast(g, pw[:], "pw", g.d["pool_w"].rearrange("g c d -> c g d"), [128, 4, 128])
        win = g.d["ab_w_in"].rearrange("(kc p) n -> p kc n", p=128)

        def wload(col0):
            wt, wk = wr.next()
            load_cast(g, wt[:], wk, win[:, :, col0:col0 + 128], [128, 8, 128])
            return wt, wk

        for gi, w in enumerate((2, 4, 8, 16)):
            if hook is not None:
                hook(gi)
            wt, wk = wload(gi * 128)
            P.dma(lambda E, gi=gi: E.dma_start(out=rc[:], in_=g.d["rcnt"][gi:gi + 1, :].to_broadcast([128, T])),
                  writes=["rc"], q="sp")

            def ev(oc, tile, bank):
                for (fs, n, pc) in segs_of_tile(tile):
                    P.op("act", lambda E, fs=fs, n=n, pc=pc, tile=tile, bank=bank: E.activation(
                        out=abuf[:, pc:pc + n], in_=ps[:, bank, fs - tile * TT:fs - tile * TT + n], func=ACT.Copy),
                        reads=[("ps", bank)], writes=["abuf"])

            linear_fm(g, lambda oc, k, wt=wt: wt[:, k, :], lambda oc, wk=wk: wk,
                      lambda k, tile: g.hbuf[:, k, tile * TT:(tile + 1) * TT], lambda k, tile: [("h", tile, k)],
                      8, 1, range(NT), ev)
            cur, curk, ln = abuf, "abuf", 1
            bufs = [(p2, "p2"), (p4, "p4")]
            bi = 0
            while ln < w:
                nb, nbk = bufs[bi]
                bi ^= 1
                P.op("dve", lambda E, cur=cur, nb=nb, ln=ln: E.tensor_tensor(
                    out=nb[:, 0:PTOT - ln], in0=cur[:, 0:PTOT - ln], in1=cur[:, ln:PTOT], op=ALU.add),
                    reads=[curk], writes=[nbk])
                cur, curk, ln = nb, nbk, ln * 2
            for (fs, n, pc) in PSEQ:
                tb, tbk = bufs[bi]
                P.op("dve", lambda E, cur=cur, fs=fs, n=n, pc=pc, tb=tb, w=w: E.tensor_tensor(
                    out=tb[:, pc:pc + n], in0=cur[:, pc - w // 2:pc - w // 2 + n], in1=rc[:, fs:fs + n], op=ALU.mult),
                    reads=[curk, "rc"], writes=[tbk])
                P.op("dve", lambda E, fs=fs, n=n, pc=pc, tb=tb: E.tensor_tensor(
                    out=pooled[:, fs:fs + n], in0=tb[:, pc:pc + n], in1=abuf[:, pc:pc + n], op=ALU.subtract),
                    reads=[tbk, "abuf"], writes=["pooled"])

            def ev2(oc, tile, bank, gi=gi):
                P.op("act", lambda E, tile=tile, bank=bank: E.activation(
                    out=g.mixbuf[:, gi, tile * TT:(tile + 1) * TT], in_=ps[:, bank, :], func=ACT.Copy,
                    scale=g.vecs[:, VOFF["pscale"] + gi:VOFF["pscale"] + gi + 1]),
                    reads=[("ps", bank), "vecs"], writes=[("mix", gi, tile)])

            linear_fm(g, lambda oc, k, gi=gi: pw[:, gi, :], lambda oc: "pw",
                      lambda k, tile: pooled[:, tile * TT:(tile + 1) * TT], lambda k, tile: ["pooled"],
                      1, 1, range(NT), ev2)
            P.op("pool", lambda E: E.memset(p2[:], 0.0), reads=[], writes=["p2"])
            P.op("pool", lambda E: E.memset(p4[:], 0.0), reads=[], writes=["p4"])

        zbuf, cv1, cv2 = abuf, p2, p4
        for j in range(4):
            if hook is not None:
                hook(4 + j)
            wc, wck = wload(1024 + j * 128)
            wv_, wvk = wload(1536 + j * 128)
            wb, wbk = wload(512 + j * 128)
            for tile in range(NT):
                bank = _psb(g)
                for k in range(8):
                    P.op("pe", lambda E, k=k, tile=tile, bank=bank: E.matmul(
                        ps[:, bank, :], lhsT=wc[:, k, :], rhs=g.hbuf[:, k, tile * TT:(tile + 1) * TT],
                        start=(k == 0), stop=(k == 7)), reads=[wck, ("h", tile, k)], writes=[("ps", bank)])
                ct, ctk = ctmp.next()
                P.op("act", lambda E, ct=ct, bank=bank: E.activation(out=ct[:], in_=ps[:, bank, :], func=ACT.Copy),
                     reads=[("ps", bank)], writes=[ctk])
                bank2 = _psb(g)
                for k in range(8):
                    P.op("pe", lambda E, k=k, tile=tile, bank2=bank2: E.matmul(
                        ps[:, bank2, :], lhsT=wv_[:, k, :], rhs=g.hbuf[:, k, tile * TT:(tile + 1) * TT],
                        start=(k == 0), stop=(k == 7)), reads=[wvk, ("h", tile, k)], writes=[("ps", bank2)])
                for (fs, n, pc) in segs_of_tile(tile):
                    P.op("dve", lambda E, fs=fs, n=n, pc=pc, tile=tile, ct=ct, bank2=bank2: E.tensor_tensor(
                        out=zbuf[:, pc:pc + n], in0=ps[:, bank2, fs - tile * TT:fs - tile * TT + n],
                        in1=ct[:, fs - tile * TT:fs - tile * TT + n], op=ALU.mult),
                        reads=[("ps", bank2), ctk], writes=["abuf"])
            cw = VOFF["convw"]
            ch = j
            P.op("dve", lambda E, ch=ch: E.tensor_scalar(
                out=cv1[:, 1:PTOT - 1], in0=zbuf[:, 0:PTOT - 2], scalar1=g.vecs[:, cw + 0 * 4 + ch:cw + 0 * 4 + ch + 1],
                scalar2=None, op0=ALU.mult), reads=["abuf", "vecs"], writes=["p2"])
            P.op("dve", lambda E, ch=ch: E.scalar_tensor_tensor(
                out=cv2[:, 1:PTOT - 1], in0=zbuf[:, 1:PTOT - 1], scalar=g.vecs[:, cw + 1 * 4 + ch:cw + 1 * 4 + ch + 1],
                in1=cv1[:, 1:PTOT - 1], op0=ALU.mult, op1=ALU.add), reads=["abuf", "p2", "vecs"], writes=["p4"])
            P.op("dve", lambda E, ch=ch: E.scalar_tensor_tensor(
                out=cv1[:, 1:PTOT - 1], in0=zbuf[:, 2:PTOT], scalar=g.vecs[:, cw + 2 * 4 + ch:cw + 2 * 4 + ch + 1],
                in1=cv2[:, 1:PTOT - 1], op0=ALU.mult, op1=ALU.add), reads=["abuf", "p4", "vecs"], writes=["p2"])
            for tile in range(NT):
                bank = _psb(g)
                for k in range(8):
                    P.op("pe", lambda E, k=k, tile=tile, bank=bank: E.matmul(
                        ps[:, bank, :], lhsT=wb[:, k, :], rhs=g.hbuf[:, k, tile * TT:(tile + 1) * TT],
                        start=(k == 0), stop=(k == 7)), reads=[wbk, ("h", tile, k)], writes=[("ps", bank)])
                for (fs, n, pc) in segs_of_tile(tile):
                    P.op("dve", lambda E, fs=fs, n=n, pc=pc, tile=tile, bank=bank, j=j: E.tensor_tensor(
                        out=g.mixbuf[:, 4 + j, fs:fs + n], in0=ps[:, bank, fs - tile * TT:fs - tile * TT + n],
                        in1=cv1[:, pc:pc + n], op=ALU.mult), reads=[("ps", bank), "p2"], writes=[("mix", 4 + j, tile)])
        P.barrier()


def stage_wout_norm2(g, l, wname, mix_of, mixkeys_of):
    nc, P, ps = g.nc, g.P, g.ps
    with ExitStack() as ss:
        wo = ss.enter_context(_sbt(nc, "wo", [128, 8, D], BF16))
        xr = Ring(nc, ss, "xw", 2, [128, 8, TT], F32)
        g.sqring = Ring(nc, ss, "sq", 1, [128, 8, TT], BF16)
        g.rsring = Ring(nc, ss, "rs", 2, [128, TT], F32)
        g.xnring = Ring(nc, ss, "xn", 1, [128, 8, TT], F32)
        wsrc = g.d[wname].rearrange("(kc p) n -> p kc n", p=128)
        for h2 in range(4):
            load_cast(g, wo[:, :, h2 * 256:(h2 + 1) * 256], ("wo", h2), wsrc[:, :, h2 * 256:(h2 + 1) * 256], [128, 8, 256])
        for tile in range(NT):
            xT, xk = xr.next()
            P.dma(lambda E, xT=xT, tile=tile: E.dma_start(
                out=xT[:], in_=g.xs[:, :, tile * TT:(tile + 1) * TT].rearrange("c p t -> p c t")),
                reads=[("xs", tile)], writes=[xk])
            j = 0 if tile == 0 else 1
            for oc in range(8):
                bank = _psb(g)
                for k in range(8):
                    P.op("pe", lambda E, oc=oc, k=k, tile=tile, bank=bank: E.matmul(
                        ps[:, bank, :], lhsT=wo[:, k, oc * 128:(oc + 1) * 128], rhs=mix_of(k, tile),
                        start=(k == 0), stop=(k == 7)),
                        reads=[("wo", oc // 2)] + mixkeys_of(k, tile), writes=[("ps", bank)])
                P.op("dve", lambda E, oc=oc, bank=bank, xT=xT, j=j: E.scalar_tensor_tensor(
                    out=xT[:, oc, :], in0=ps[:, bank, :], scalar=g.mod[l][:, 16 + oc, j:j + 1], in1=xT[:, oc, :],
                    op0=ALU.mult, op1=ALU.add), reads=[("ps", bank), xk, ("mod", l)], writes=[xk])
            P.dma(lambda E, xT=xT, tile=tile: E.dma_start(
                out=g.xs[:, :, tile * TT:(tile + 1) * TT].rearrange("c p t -> p c t"), in_=xT[:]),
                reads=[xk], writes=[("xs", tile)], q="sp")
            norm_tile(g, xT[:], xk, tile, g.gm[l][:, 1], g.mod[l][:, 24:32], l, [("gm", l), ("mod", l)], None)
        P.barrier()


def stage_mlp(g, l, X, hook=None):
    nc, P, ps = g.nc, g.P, g.ps
    with ExitStack() as ss:
        hid = Ring(nc, ss, "hid", 2, [128, 4, TT], BF16)
        rl = Ring(nc, ss, "rl", 3, [128, TT], BF16)
        g.wring = Ring(nc, ss, "wrM", 4, [128, 4096], BF16)
        for tile in range(NT):
            P.dma(lambda E, tile=tile: E.dma_start(
                out=X[:, :, tile * TT:(tile + 1) * TT], in_=g.xs[:, :, tile * TT:(tile + 1) * TT].rearrange("c p t -> p c t")),
                reads=[("xs", tile)], writes=[("X", tile)], q="sp")
        w1src = g.d["mlp_w1"][l].rearrange("(kc p) n -> p kc n", p=128)
        w2src = g.d["mlp_w2"][l].rearrange("(fc p) n -> p fc n", p=128)
        FB = 8

        def wl(fb):
            w1t, w1k = g.wring.next()
            w1v = w1t[:, 0:8 * 512].rearrange("p (a b) -> p a b", a=8)
            load_cast(g, w1v, w1k, w1src[:, :, fb * 512:(fb + 1) * 512])
            w2t, w2k = g.wring.next()
            w2v = w2t[:, 0:4 * D].rearrange("p (a b) -> p a b", a=4)
            load_cast(g, w2v, w2k, w2src[:, fb * 4:(fb + 1) * 4, :])
            return (w1v, w1k, w2v, w2k)

        nxt = wl(0)
        for fb in range(FB):
            w1v, w1k, w2v, w2k = nxt
            if fb + 1 < FB:
                nxt = wl(fb + 1)
            if hook is not None:
                hook(fb)
            for tile in range(NT):
                j = 0 if tile == 0 else 1
                ht, hk = hid.next()
                for fc in range(4):
                    bank = _psb(g)
                    for k in range(8):
                        P.op("pe", lambda E, fc=fc, k=k, tile=tile, bank=bank: E.matmul(
                            ps[:, bank, :], lhsT=w1v[:, k, fc * 128:(fc + 1) * 128],
                            rhs=g.hbuf[:, k, tile * TT:(tile + 1) * TT], start=(k == 0), stop=(k == 7)),
                            reads=[w1k, ("h", tile, k)], writes=[("ps", bank)])
                    rt, rk = rl.next()
                    P.op("act", lambda E, rt=rt, bank=bank: E.activation(out=rt[:], in_=ps[:, bank, :], func=ACT.Relu),
                         reads=[("ps", bank)], writes=[rk])
                    if fc % 2 == 0:
                        P.op("act", lambda E, rt=rt, ht=ht, fc=fc: E.activation(out=ht[:, fc, :], in_=rt[:], func=ACT.Square),
                             reads=[rk], writes=[(hk, fc)])
                    else:
                        P.op("pool", lambda E, rt=rt, ht=ht, fc=fc: E.tensor_tensor(out=ht[:, fc, :], in0=rt[:], in1=rt[:],
                                                                                    op=ALU.mult),
                             reads=[rk], writes=[(hk, fc)])
                for oc in range(8):
                    bank = _psb(g)
                    for fc in range(4):
                        P.op("pe", lambda E, oc=oc, fc=fc, bank=bank, ht=ht: E.matmul(
                            ps[:, bank, :], lhsT=w2v[:, fc, oc * 128:(oc + 1) * 128], rhs=ht[:, fc, :],
                            start=(fc == 0), stop=(fc == 3)), reads=[w2k, (hk, fc)], writes=[("ps", bank)])
                    P.op("dve", lambda E, oc=oc, bank=bank, tile=tile, j=j: E.scalar_tensor_tensor(
                        out=X[:, oc, tile * TT:(tile + 1) * TT], in0=ps[:, bank, :],
                        scalar=g.mod[l][:, 40 + oc, j:j + 1], in1=X[:, oc, tile * TT:(tile + 1) * TT],
                        op0=ALU.mult, op1=ALU.add), reads=[("ps", bank), ("X", tile), ("mod", l)], writes=[("X", tile)])
        P.barrier()


def stage_norm1_from_X(g, l, X):
    nc, P = g.nc, g.P
    with ExitStack() as ss:
        g.sqring = Ring(nc, ss, "sq", 2, [128, 8, TT], BF16)
        g.rsring = Ring(nc, ss, "rs", 2, [128, TT], F32)
        g.xnring = Ring(nc, ss, "xn", 2, [128, 8, TT], F32)
        pend_ = None
        for tile in range(NT):
            P.dma(lambda E, tile=tile: E.dma_start(
                out=g.xs[:, :, tile * TT:(tile + 1) * TT].rearrange("c p t -> p c t"), in_=X[:, :, tile * TT:(tile + 1) * TT]),
                reads=[("X", tile)], writes=[("xs", tile)], q="sp")
            nxt_ = norm_tile(g, X[:, :, tile * TT:(tile + 1) * TT], ("X", tile), tile, g.gm[l][:, 0], g.mod[l][:, 0:8], l,
                             [("gm", l), ("mod", l)], None, defer=True)
            if pend_ is not None:
                pend_()
            pend_ = nxt_
        pend_()
        P.barrier()


def stage_final(g, X):
    nc, P, ps = g.nc, g.P, g.ps
    with ExitStack() as ss:
        sqr = Ring(nc, ss, "sq", 2, [128, 8, TT], BF16)
        rsr = Ring(nc, ss, "rs", 2, [128, TT], F32)
        xnr = Ring(nc, ss, "xn", 2, [128, 8, TT], F32)
        otr = Ring(nc, ss, "ot", 2, [128, D], F32)
        fg = g.vecs[:, VOFF["fing"]:VOFF["fing"] + 8]
        pend_ = None
        for tile in range(NT):
            xT = X[:, :, tile * TT:(tile + 1) * TT]
            xkey = ("X", tile)
            sq, sqk = sqr.next()
            P.op("act", lambda E, sq=sq, xT=xT: E.activation(out=sq[:], in_=xT, func=ACT.Square), reads=[xkey], writes=[sqk])
            bank = _psb(g)
            for c in range(8):
                P.op("pe", lambda E, c=c, sq=sq, bank=bank: E.matmul(ps[:, bank, :], lhsT=g.ones_bf[:], rhs=sq[:, c, :],
                                                                     start=(c == 0), stop=(c == 7)),
                     reads=[sqk, "consts"], writes=[("ps", bank)])
            rs, rsk = rsr.next()
            P.op("act", lambda E, rs=rs, bank=bank: E.activation(out=rs[:], in_=ps[:, bank, :], func=ACT.Sqrt,
                                                                 bias=g.epst[:], scale=1.0 / D),
                 reads=[("ps", bank), "consts"], writes=[rsk])
            P.op("dve", lambda E, rs=rs: E.reciprocal(out=rs[:], in_=rs[:]), reads=[rsk], writes=[rsk])
            xn, xnk = xnr.next()
            P.op("dve", lambda E, xn=xn, xT=xT, rs=rs: E.tensor_tensor(
                out=xn[:], in0=xT, in1=rs[:].unsqueeze(1).to_broadcast([128, 8, TT]), op=ALU.mult),
                reads=[xkey, rsk], writes=[xnk])
            P.op("pool", lambda E, xn=xn: E.tensor_tensor(
                out=xn[:], in0=xn[:], in1=fg.unsqueeze(2).to_broadcast([128, 8, TT]), op=ALU.mult),
                reads=[xnk, "vecs"], writes=[xnk])
            def p2_(tile=tile, xn=xn, xnk=xnk):
                for sub in range(4):
                    ot, otk = otr.next()
                    for hb in range(2):
                        bank = _psb(g)
                        for c4 in range(4):
                            c = hb * 4 + c4
                            P.op("pe", lambda E, xn=xn, c=c, c4=c4, sub=sub, bank=bank: E.transpose(
                                ps[:, bank, c4 * 128:(c4 + 1) * 128], xn[:, c, sub * 128:(sub + 1) * 128], g.ident[:]),
                                reads=[xnk, "consts"], writes=[("ps", bank)])
                        if hb == 0:
                            P.op("act", lambda E, ot=ot, bank=bank: E.activation(out=ot[:, 0:512], in_=ps[:, bank, :], func=ACT.Copy),
                                 reads=[("ps", bank)], writes=[(otk, 0)])
                        else:
                            P.op("dve", lambda E, ot=ot, bank=bank: E.tensor_copy(out=ot[:, 512:1024], in_=ps[:, bank, :]),
                                 reads=[("ps", bank)], writes=[(otk, 1)])
                    tt = tile * 4 + sub
                    if tt < 4:
                        dst = g.d["yp"][tt * 128:(tt + 1) * 128, :]
                    else:
                        dst = g.d["ys"][(tt - 4) * 128:(tt - 3) * 128, :]
                    P.dma(lambda E, ot=ot, dst=dst: E.dma_start(out=dst, in_=ot[:]), reads=[(otk, 0), (otk, 1)],
                          writes=[], q="sp", final=True)
            if pend_ is not None:
                pend_()
            pend_ = p2_
        pend_()
        P.barrier()


def stage_attn(g, dbl=None):
    nc, P, ps = g.nc, g.P, g.ps
    with ExitStack() as ss:
        wr = Ring(nc, ss, "wa", 3, [128, 8, 128], BF16)
        obuf = ss.enter_context(_sbt(nc, "obuf", [128, 4, T], BF16))
        kTr = Ring(nc, ss, "kT", 2, [128, T], BF16)
        qTr = Ring(nc, ss, "qT", 2, [128, T], BF16)
        Vtr = Ring(nc, ss, "Vt", 2, [128, 20, 2, 65], BF16)
        kcxr = Ring(nc, ss, "kcx", 2, [128, 256], BF16)
        Vcxr = Ring(nc, ss, "Vcx", 2, [128, 2, 2, 65], BF16)
        ckr = Ring(nc, ss, "ck", 2, [128, 2, 2, 64], F32)
        cvr = Ring(nc, ss, "cv", 2, [128, 2, 2, 64], F32)
        Tbr = Ring(nc, ss, "Tb", 2, [128, 2, 16, 64], F32)
        oh = ss.enter_context(_sbt(nc, "oh", [32, 64, 64], F32))
        rpbT = ss.enter_context(_sbt(nc, "rpbT", [32, 8, 17], F32))
        ktm = Ring(nc, ss, "ktm", 2, [128, 128], F32)
        vtm = Ring(nc, ss, "vtm", 2, [128, 128], F32)
        tmpr = Ring(nc, ss, "stmp", 2, [128, 5, 128], F32)
        pTr = Ring(nc, ss, "pT", 3, [128, 7, 128], BF16)
        otm = Ring(nc, ss, "otm", 2, [128, 128], F32)
        rcr = Ring(nc, ss, "rcp", 2, [128, 2], F32)

        P.dma(lambda E: E.dma_start(out=oh[:], in_=g.d["onehot"][:, :, :]), writes=["oh"])
        P.op("pool", lambda E: E.memset(rpbT[:], 0.0), writes=["rpbT"])
        P.op("pool", lambda E: E.memset(rpbT[0:1, :, :], 1.0), writes=["rpbT"])
        for h_ in range(8):
            P.dma(lambda E, h_=h_: E.dma_start(out=rpbT[1:32, h_, 1:16], in_=g.d["na_rpb"][h_].rearrange("r c -> c r"),
                                               allow_slow_non_contiguous=True), writes=["rpbT"], q="sp")
        win = g.d["cd_w_in"].rearrange("(kc p) n -> p kc n", p=128)

        for hp in range(4):
            def wload(col0):
                wt, wk = wr.next()
                load_cast(g, wt[:], wk, win[:, :, col0:col0 + 128], [128, 8, 128])
                return wt, wk

            wq, wqk = wload(hp * 128)
            wk_, wkk = wload(512 + hp * 128)
            wv_, wvk = wload(1024 + hp * 128)
            kT, kTk = kTr.next()
            qT, qTk = qTr.next()
            Vt, Vtk = Vtr.next()
            Tb, Tbk = Tbr.next()
            b0 = 4
            psv = ps[:, b0:b0 + 4, :].rearrange("p b (q x) -> p (b q) x", x=32)
            tk = [("ps", b0 + i) for i in range(4)]
            import os
            _dbg = int(os.environ.get("KDBG", "255"))
            for qc in (range(64) if _dbg & 2 else ()):
                P.op("pe", lambda E, qc=qc: E.matmul(psv[0:64, qc, :], lhsT=oh[:, qc, :], rhs=rpbT[:, 2 * hp:2 * hp + 2, 0:16],
                                                     start=True, stop=True), reads=["oh", "rpbT"], writes=[tk[qc // 16]])
                P.op("pe", lambda E, qc=qc: E.matmul(psv[64:128, qc, :], lhsT=oh[:, qc, :], rhs=rpbT[:, 2 * hp:2 * hp + 2, 1:17],
                                                     start=True, stop=True), reads=["oh", "rpbT"], writes=[tk[qc // 16]])
            for hh in range(2):
                P.op("act" if hh == 0 else "dve", lambda E, hh=hh: (
                    E.activation(out=Tb[:, hh, :, :], in_=psv[:, :, hh * 16:(hh + 1) * 16].rearrange("p q a -> p a q"), func=ACT.Copy)
                    if hh == 0 else
                    E.tensor_copy(out=Tb[:, hh, :, :], in_=psv[:, :, hh * 16:(hh + 1) * 16].rearrange("p q a -> p a q"))),
                    reads=tk, writes=[Tbk])
            g.bank = 0
            ck, ckk = ckr.next()
            cv, cvk = cvr.next()
            kcx, kcxk = kcxr.next()
            Vcx, Vcxk = Vcxr.next()
            for lc in range(2):
                P.dma(lambda E, ck=ck, lc=lc: E.dma_start(
                    out=ck[:, lc], in_=g.d["cache_k"][2 * hp:2 * hp + 2, lc * 128:(lc + 1) * 128, :].rearrange("h p d -> p h d")),
                    writes=[ckk])
                P.dma(lambda E, cv=cv, lc=lc: E.dma_start(
                    out=cv[:, lc], in_=g.d["cache_v"][2 * hp:2 * hp + 2, lc * 128:(lc + 1) * 128, :].rearrange("h p d -> p h d")),
                    writes=[cvk], q="sp")
            for lc in (range(2) if _dbg & 4 else ()):
                bank = _psb(g) % 4
                P.op("pe", lambda E, lc=lc, bank=bank, ck=ck: E.transpose(
                    ps[:, bank, 0:128], ck[:, lc, :, :].rearrange("p h d -> p (h d)"), g.ident[:]),
                    reads=[ckk, "consts"], writes=[("ps", bank)])
                P.op("act", lambda E, lc=lc, bank=bank, kcx=kcx: E.activation(
                    out=kcx[:, lc * 128:(lc + 1) * 128], in_=ps[:, bank, 0:128], func=ACT.Copy),
                    reads=[("ps", bank)], writes=[kcxk])
            P.op("pool", lambda E, Vcx=Vcx: E.memset(Vcx[:, :, :, 64:65], 1.0), writes=[Vcxk])
            P.op("pool", lambda E, Vcx=Vcx, cv=cv: E.tensor_copy(out=Vcx[:, :, :, 0:64], in_=cv[:]), reads=[cvk], writes=[Vcxk])
            P.op("pool", lambda E, Vt=Vt: E.memset(Vt[:, :, :, 64:65], 1.0), writes=[(Vtk, u_) for u_ in range(20)])

            def evk(oc, tile, bank, kT=kT):
                P.op("act", lambda E: E.activation(out=kT[:, tile * TT:(tile + 1) * TT], in_=ps[:, bank, :], func=ACT.Copy),
                     reads=[("ps", bank)], writes=[(kTk, tile)])

            def evq(oc, tile, bank, qT=qT):
                P.op("dve", lambda E: E.tensor_copy(out=qT[:, tile * TT:(tile + 1) * TT], in_=ps[:, bank, :]),
                     reads=[("ps", bank)], writes=[(qTk, tile)])

            def banked(f):
                def w(oc, tile, bank):
                    return f(oc, tile, bank)
                return w

            def lin(wt, wkey, ev):
                for tile in range(NT):
                    bank = _psb(g) % 4
                    for k in range(8):
                        P.op("pe", lambda E, k=k, tile=tile, bank=bank: E.matmul(
                            ps[:, bank, :], lhsT=wt[:, k, :], rhs=g.hbuf[:, k, tile * TT:(tile + 1) * TT],
                            start=(k == 0), stop=(k == 7)), reads=[wkey, ("h", tile, k)], writes=[("ps", bank)])
                    ev(0, tile, bank)

            if _dbg & 16:
                lin(wk_, wkk, evk)
                lin(wq, wqk, evq)
            for u in (range(20) if _dbg & 8 else ()):
                bank = _psb(g) % 4
                for k in range(8):
                    P.op("pe", lambda E, k=k, u=u, bank=bank: E.matmul(
                        ps[:, bank, 0:128], lhsT=g.hbuf[:, k, u * 128:(u + 1) * 128], rhs=wv_[:, k, :],
                        start=(k == 0), stop=(k == 7)), reads=[wvk, ("h", u // 4, k)], writes=[("ps", bank)])
                P.op("act", lambda E, u=u, bank=bank, Vt=Vt: E.activation(
                    out=Vt[:, u, :, 0:64], in_=ps[:, bank, 0:128].rearrange("p (h d) -> p h d", h=2), func=ACT.Copy),
                    reads=[("ps", bank)], writes=[(Vtk, u)])
                if u < 4 and (_dbg & 32):
                    vt, vtk_ = vtm.next()
                    P.op("act", lambda E, bank=bank, vt=vt: E.activation(out=vt[:], in_=ps[:, bank, 0:128], func=ACT.Copy),
                         reads=[("ps", bank)], writes=[vtk_])
                    for hh_ in (range(2) if not os.environ.get("KD2") else ()):
                        P.dma(lambda E, u=u, vt=vt, hh_=hh_: E.dma_start(
                            out=g.d["nv"][u // 2, 2 * hp + hh_, (u % 2) * 128:(u % 2 + 1) * 128, :],
                            in_=vt[:, hh_ * 64:(hh_ + 1) * 64]), reads=[vtk_], final=True, q="sp")
                    if not (_dbg & 128):
                        continue
                    bank2 = _psb(g) % 4
                    for k in range(8):
                        P.op("pe", lambda E, k=k, u=u, bank2=bank2: E.matmul(
                            ps[:, bank2, 0:128], lhsT=g.hbuf[:, k, u * 128:(u + 1) * 128], rhs=wk_[:, k, :],
                            start=(k == 0), stop=(k == 7)), reads=[wkk, ("h", 0, k)], writes=[("ps", bank2)])
                    kt, ktk_ = ktm.next()
                    P.op("act", lambda E, bank2=bank2, kt=kt: E.activation(out=kt[:], in_=ps[:, bank2, 0:128], func=ACT.Copy),
                         reads=[("ps", bank2)], writes=[ktk_])
                    for hh_ in range(2):
                        P.dma(lambda E, u=u, kt=kt, hh_=hh_: E.dma_start(
                            out=g.d["nk"][u // 2, 2 * hp + hh_, (u % 2) * 128:(u % 2 + 1) * 128, :],
                            in_=kt[:, hh_ * 64:(hh_ + 1) * 64]), reads=[ktk_], final=True)

            if dbl is not None:
                dbl(hp, ss)
            iters = []
            for u in range(20):
                for hh in range(2):
                    iters.append((u, hh))
            psv8 = lambda bs: ps[:, bs:bs + 2, :].rearrange("p b (s q) -> p (b s) q", q=128)

            def stage_A(u, hh):
                hs = slice(64 * hh, 64 * hh + 64)
                qtile = (u * 128) // TT
                rows = []
                if u < 4:
                    slots = [2 * (u // 2), 2 * (u // 2) + 1]
                    ctx = False
                else:
                    for par in range(2):
                        r = 2 * (u - 4) + par
                        r0 = min(max(r - 4, 0), 24)
                        rows.append((par, r, r0 // 2, (r0 + 7) // 2, r0 % 2 == 1))
                    p_lo = min(x[2] for x in rows)
                    p_hi = max(x[3] for x in rows)
                    slots = [4 + p for p in range(p_lo, p_hi + 1)]
                    ctx = True
                n = len(slots)
                g.sbank = (getattr(g, "sbank", 0) + 2) % 4
                bs = g.sbank
                pv = psv8(bs)
                bk = lambda j: ("ps", bs + j // 4)
                for j, sl in enumerate(slots):
                    P.op("pe", lambda E, j=j, sl=sl: E.matmul(
                        pv[:, j, :], lhsT=kT[hs, sl * 128:(sl + 1) * 128], rhs=qT[hs, u * 128:(u + 1) * 128],
                        start=True, stop=True), reads=[(kTk, sl // 4), (qTk, qtile)], writes=[bk(j)])
                if ctx:
                    for lc in range(2):
                        P.op("pe", lambda E, lc=lc: E.matmul(
                            pv[:, n + lc, :], lhsT=kcx[hs, lc * 128:(lc + 1) * 128], rhs=qT[hs, u * 128:(u + 1) * 128],
                            start=True, stop=True), reads=[kcxk, (qTk, qtile)], writes=[bk(n + lc)])
                pT, pTk = pTr.next()
                allb = [("ps", bs), ("ps", bs + 1)]
                if u >= 4:
                    tmp, tmpk = tmpr.next()
                    for (par, r, pl_r, ph_r, odd) in rows:
                        qs = slice(par * 64, par * 64 + 64)
                        j_lo, j_hi = pl_r - p_lo, ph_r - p_lo
                        n_r = j_hi - j_lo + 1
                        a0 = 2 * pl_r - r + 8
                        P.op("dve", lambda E, qs=qs, j_lo=j_lo, j_hi=j_hi, n_r=n_r, a0=a0: E.scalar_tensor_tensor(
                            out=tmp[:, j_lo:j_hi + 1, qs], in0=pv[:, j_lo:j_hi + 1, qs], scalar=0.125,
                            in1=Tb[:, hh, a0:a0 + 2 * n_r:2, :], op0=ALU.mult, op1=ALU.add),
                            reads=allb + [Tbk], writes=[tmpk])
                        for j in range(n):
                            if j < j_lo or j > j_hi:
                                P.op("dve", lambda E, j=j, qs=qs: E.memset(tmp[:, j, qs], NEG), writes=[tmpk])
                        if odd:
                            P.op("dve", lambda E, qs=qs, j_lo=j_lo: E.memset(tmp[0:64, j_lo, qs], NEG), writes=[tmpk])
                            P.op("dve", lambda E, qs=qs, j_hi=j_hi: E.memset(tmp[64:128, j_hi, qs], NEG), writes=[tmpk])
                    P.op("act", lambda E: E.activation(out=pT[:, 0:n, :], in_=tmp[:, 0:n, :], func=ACT.Exp),
                         reads=[tmpk], writes=[pTk])
                    P.op("act", lambda E: E.activation(out=pT[:, n:n + 2, :], in_=pv[:, n:n + 2, :], func=ACT.Exp, scale=0.125),
                         reads=allb, writes=[pTk])
                else:
                    P.op("act", lambda E: E.activation(out=pT[:, 0:n, :], in_=pv[:, 0:n, :], func=ACT.Exp, scale=0.125),
                         reads=allb, writes=[pTk])
                return (pT, pTk, slots, n, ctx)

            def stage_B(u, hh, st):
                pT, pTk, slots, n, ctx = st
                bank_o = 4 + (u % 2) * 2
                nn = n + (2 if ctx else 0)
                for j in range(nn):
                    if j < n:
                        rhs = Vt[:, slots[j], hh, :]
                        rk = (Vtk, slots[j])
                    else:
                        rhs = Vcx[:, j - n, hh, :]
                        rk = Vcxk
                    P.op("pe", lambda E, j=j, rhs=rhs: E.matmul(
                        ps[:, bank_o, hh * 65:hh * 65 + 65], lhsT=pT[:, j, :], rhs=rhs, start=(j == 0), stop=(j == nn - 1)),
                        reads=[pTk, rk], writes=[("ps", bank_o)])

            def stage_F(u):
                bank_o = 4 + (u % 2) * 2
                bank_t = bank_o + 1
                rc, rck = rcr.next()
                ot, otk = otm.next()
                P.op("dve", lambda E: E.reciprocal(out=rc[:], in_=ps[:, bank_o, 64:130:65]),
                     reads=[("ps", bank_o)], writes=[rck])
                for hh in range(2):
                    P.op("dve", lambda E, hh=hh: E.tensor_scalar(
                        out=ot[:, hh * 64:(hh + 1) * 64], in0=ps[:, bank_o, hh * 65:hh * 65 + 64], scalar1=rc[:, hh:hh + 1],
                        scalar2=None, op0=ALU.mult), reads=[("ps", bank_o), rck], writes=[otk])
                P.op("pe", lambda E: E.transpose(ps[:, bank_t, 0:128], ot[:], g.ident[:]),
                     reads=[otk, "consts"], writes=[("ps", bank_t)])
                P.op("act", lambda E: E.activation(out=obuf[:, hp, u * 128:(u + 1) * 128], in_=ps[:, bank_t, 0:128],
                                                   func=ACT.Copy), reads=[("ps", bank_t)], writes=[("o", hp, u // 4)])

            st_next = stage_A(*iters[0])
            for ii, (u, hh) in enumerate(iters):
                st_cur = st_next
                if ii + 1 < len(iters):
                    st_next = stage_A(*iters[ii + 1])
                stage_B(u, hh, st_cur)
                if hh == 1:
                    stage_F(u)
            for tile in range(NT):
                P.dma(lambda E, tile=tile: E.dma_start(out=g.obs[hp, :, tile * TT:(tile + 1) * TT], in_=obuf[:, hp, tile * TT:(tile + 1) * TT]),
                      reads=[("o", hp, tile)], writes=[("obs", hp, tile)], q="sp")
        P.barrier()
        g.bank = 0


NCH = (64, 64, 512)
BST = (0, 65, 130)
NCOL = 643
TWO_PI = 2.0 * math.pi


def s5_load_raw(g, es, ss):
    nc, P, ps = g.nc, g.P, g.ps
    s = G()
    g.s5 = s

    def t(name, shape, dt=F32):
        return es.enter_context(_sbt(nc, "s5_" + name, shape, dt))

    s.lamre, s.lamim, s.lstep = t("lamre", [128, 2, 16]), t("lamim", [128, 2, 16]), t("lstep", [128, 2, 16])
    s.s0re, s.s0im = t("s0re", [128, 2, 16]), t("s0im", [128, 2, 16])
    s.Bre, s.Bim = t("Bre", [128, 2, 16, 16]), t("Bim", [128, 2, 16, 16])
    s.Cre, s.Cim = t("Cre", [128, 2, 16, 16]), t("Cim", [128, 2, 16, 16])
    if True:
        Cld = ss.enter_context(_sbt(nc, "s5_Cld", [128, 2, 2, 4, 128], F32))
        Lld = ss.enter_context(_sbt(nc, "s5_Lld", [64, 4, 128], F32))
        lsr = ss.enter_context(_sbt(nc, "s5_lsr", [1, 64], F32))
        ones1 = ss.enter_context(_sbt(nc, "s5_ones1", [1, 128], F32))
        s.raw_stage = (Cld, Lld, lsr, ones1)

    def issue():
        P.op("pool", lambda E: E.memset(ones1[:], 1.0), writes=["ones1"])
        for ri, src in enumerate(("ssm_c_re", "ssm_c_im")):
            for d in range(2):
                for dup in range(2):
                    P.dma(lambda E, ri=ri, src=src, d=d, dup=dup: E.dma_start(
                        out=Cld[:, ri, d, :, dup * 64:(dup + 1) * 64],
                        in_=g.d[src][d].rearrange("(cc gl) co n -> (gl co) cc n", gl=8)), writes=[("Cld", ri, d)])
        for ti, src in enumerate(("ssm_lambda_re", "ssm_lambda_im", "state_re", "state_im")):
            for dup in range(2):
                P.dma(lambda E, ti=ti, src=src, dup=dup: E.dma_start(
                    out=Lld[:, ti, dup * 64:(dup + 1) * 64], in_=g.d[src].rearrange("d g n -> (d g) n")), writes=[("Lld", ti)])
        P.dma(lambda E: E.dma_start(out=lsr[:], in_=g.d["ssm_log_step"].rearrange("d g -> (d g)").unsqueeze(0)), writes=["lsr"])
        for g2 in range(2):
            hs = slice(64 * g2, 64 * g2 + 64)
            for (dst, src) in ((s.Bre, "ssm_b_re"), (s.Bim, "ssm_b_im")):
                for d in range(2):
                    P.dma(lambda E, dst=dst, src=src, hs=hs, g2=g2, d=d: E.dma_start(
                        out=dst[hs, d], in_=g.d[src][d, g2::2, :, :].rearrange("p n c -> n p c")), writes=["s5raw"])
    s.issue = issue


def s5_raw_xpose(g):
    nc, P, ps = g.nc, g.P, g.ps
    s = g.s5
    Cld, Lld, lsr, ones1 = s.raw_stage
    if True:
        for ri, dst in enumerate((s.Cre, s.Cim)):
            for d in range(2):
                for cc in range(4):
                    bank = _psb(g)
                    P.op("pe", lambda E, ri=ri, d=d, cc=cc, bank=bank: E.transpose(ps[:, bank, 0:128], Cld[:, ri, d, cc, :], g.ident[:]),
                         reads=[("Cld", ri, d), "consts"], writes=[("ps", bank)])
                    for g2 in range(2):
                        hs = slice(64 * g2, 64 * g2 + 64)
                        P.op("act", lambda E, dst=dst, d=d, cc=cc, bank=bank, g2=g2, hs=hs: E.activation(
                            out=dst[hs, d, 4 * cc:4 * cc + 4, :],
                            in_=ps[hs, bank, 0:128].rearrange("p (pl g2 co) -> p pl g2 co", pl=4, g2=2)[:, :, g2, :], func=ACT.Copy),
                            reads=[("ps", bank)], writes=["s5raw"])
        for ti, dst in enumerate((s.lamre, s.lamim, s.s0re, s.s0im)):
            bank = _psb(g)
            P.op("pe", lambda E, ti=ti, bank=bank: E.transpose(ps[:, bank, 0:64], Lld[:, ti, :], g.ident[0:64, 0:64]),
                 reads=[("Lld", ti), "consts"], writes=[("ps", bank)])
            for g2 in range(2):
                hs = slice(64 * g2, 64 * g2 + 64)
                P.op("act", lambda E, dst=dst, bank=bank, g2=g2, hs=hs: E.activation(
                    out=dst[hs], in_=ps[hs, bank, 0:64].rearrange("p (d pp g2) -> p d pp g2", d=2, g2=2)[:, :, :, g2], func=ACT.Copy),
                    reads=[("ps", bank)], writes=["s5raw"])
        bank = _psb(g)
        P.op("pe", lambda E, bank=bank: E.matmul(ps[:, bank, 0:64], lhsT=ones1[:], rhs=lsr[:], start=True, stop=True),
             reads=["ones1", "lsr"], writes=[("ps", bank)])
        for g2 in range(2):
            hs = slice(64 * g2, 64 * g2 + 64)
            P.op("act", lambda E, bank=bank, g2=g2, hs=hs: E.activation(
                out=s.lstep[hs], in_=ps[hs, bank, 0:64].rearrange("p (d pp g2) -> p d pp g2", d=2, g2=2)[:, :, :, g2], func=ACT.Copy),
                reads=[("ps", bank)], writes=["s5raw"])
        P.barrier()


def s5_emit(g, n=None):
    s = g.s5
    k = len(s.pending) if n is None else min(n, len(s.pending))
    for eng, fn in s.pending[:k]:
        g.P.op(eng, fn, reads=["s5raw", "s5d"], writes=["s5d"])
    s.pending = s.pending[k:]


def s5_setup(g, ss):
    nc, P = g.nc, g.P
    s = g.s5

    def t(name, shape, dt=F32):
        return ss.enter_context(_sbt(nc, "s5_" + name, shape, dt))

    s.Pre, s.Pim = t("Pre", [128, 2, 16, 5]), t("Pim", [128, 2, 16, 5])
    s.nPim = t("nPim", [128, 2, 16, 5])
    s.Bbre, s.Bbim = t("Bbre", [128, 2, 16, 16]), t("Bbim", [128, 2, 16, 16])
    s.r4, s.cph, s.sph = t("r4", [128, 2, 16]), t("cph", [128, 2, 16]), t("sph", [128, 2, 16])
    s.nCim = t("nCim", [128, 2, 16, 16])
    a, b, c, e_ = t("ta", [128, 2, 16]), t("tb", [128, 2, 16]), t("tc", [128, 2, 16]), t("td", [128, 2, 16])
    ti = ss.enter_context(_sbt(nc, "s5_ti", [128, 2, 16], mybir.dt.int32))
    K = "s5d"

    s.pending = []

    def V(fn, eng="pool"):
        s.pending.append((eng, _freeze(fn)))

    step, rho, th = t("stp", [128, 2, 16]), t("rho", [128, 2, 16]), t("th", [128, 2, 16])
    V(lambda E: E.activation(out=step[:], in_=s.lstep[:], func=ACT.Exp), "act")
    V(lambda E: E.tensor_tensor(out=rho[:], in0=s.lamre[:], in1=step[:], op=ALU.mult))
    V(lambda E: E.tensor_tensor(out=th[:], in0=s.lamim[:], in1=step[:], op=ALU.mult))
    er = t("er", [128, 2, 16])
    V(lambda E: E.activation(out=er[:], in_=rho[:], func=ACT.Exp), "act")

    def sincos(out, shift):
        V(lambda E: E.tensor_scalar(out=e_[:], in0=th[:], scalar1=1.0 / TWO_PI, scalar2=shift, op0=ALU.mult, op1=ALU.add))
        V(lambda E: E.tensor_copy(out=ti[:], in_=e_[:]))
        V(lambda E: E.tensor_copy(out=step[:], in_=ti[:]))
        V(lambda E: E.tensor_tensor(out=e_[:], in0=e_[:], in1=step[:], op=ALU.subtract))
        V(lambda E: E.tensor_scalar(out=step[:], in0=e_[:], scalar1=0.5, scalar2=None, op0=ALU.is_gt))
        V(lambda E: E.tensor_tensor(out=e_[:], in0=e_[:], in1=step[:], op=ALU.subtract))
        V(lambda E: E.tensor_scalar(out=step[:], in0=e_[:], scalar1=-0.5, scalar2=None, op0=ALU.is_lt))
        V(lambda E: E.tensor_tensor(out=e_[:], in0=e_[:], in1=step[:], op=ALU.add))
        V(lambda E: E.activation(out=out, in_=e_[:], func=ACT.Sin, scale=TWO_PI), "act")

    sincos(s.Pim[:, :, :, 1], 0.0)
    sincos(s.Pre[:, :, :, 1], 0.25)
    V(lambda E: E.tensor_tensor(out=s.Pre[:, :, :, 1], in0=s.Pre[:, :, :, 1], in1=er[:], op=ALU.mult))
    V(lambda E: E.tensor_tensor(out=s.Pim[:, :, :, 1], in0=s.Pim[:, :, :, 1], in1=er[:], op=ALU.mult))
    V(lambda E: E.memset(s.Pre[:, :, :, 0], 1.0))
    V(lambda E: E.memset(s.Pim[:, :, :, 0], 0.0))
    for k in range(2, 5):
        V(lambda E, k=k: E.tensor_tensor(out=a[:], in0=s.Pre[:, :, :, k - 1], in1=s.Pre[:, :, :, 1], op=ALU.mult))
        V(lambda E, k=k: E.tensor_tensor(out=b[:], in0=s.Pim[:, :, :, k - 1], in1=s.Pim[:, :, :, 1], op=ALU.mult))
        V(lambda E, k=k: E.tensor_tensor(out=s.Pre[:, :, :, k], in0=a[:], in1=b[:], op=ALU.subtract))
        V(lambda E, k=k: E.tensor_tensor(out=a[:], in0=s.Pre[:, :, :, k - 1], in1=s.Pim[:, :, :, 1], op=ALU.mult))
        V(lambda E, k=k: E.tensor_tensor(out=b[:], in0=s.Pim[:, :, :, k - 1], in1=s.Pre[:, :, :, 1], op=ALU.mult))
        V(lambda E, k=k: E.tensor_tensor(out=s.Pim[:, :, :, k], in0=a[:], in1=b[:], op=ALU.add))
    V(lambda E: E.tensor_scalar(out=s.nPim[:], in0=s.Pim[:], scalar1=-1.0, scalar2=None, op0=ALU.mult))
    V(lambda E: E.tensor_scalar(out=s.nCim[:], in0=s.Cim[:], scalar1=-1.0, scalar2=None, op0=ALU.mult))
    cre, cim, den = t("cre", [128, 2, 16]), t("cim", [128, 2, 16]), t("den", [128, 2, 16])
    V(lambda E: E.tensor_scalar(out=a[:], in0=s.Pre[:, :, :, 1], scalar1=-1.0, scalar2=None, op0=ALU.add))
    V(lambda E: E.tensor_tensor(out=den[:], in0=s.lamre[:], in1=s.lamre[:], op=ALU.mult))
    V(lambda E: E.tensor_tensor(out=b[:], in0=s.lamim[:], in1=s.lamim[:], op=ALU.mult))
    V(lambda E: E.tensor_tensor(out=den[:], in0=den[:], in1=b[:], op=ALU.add))
    V(lambda E: E.reciprocal(out=den[:], in_=den[:]), "dve")
    V(lambda E: E.tensor_tensor(out=cre[:], in0=a[:], in1=s.lamre[:], op=ALU.mult))
    V(lambda E: E.tensor_tensor(out=b[:], in0=s.Pim[:, :, :, 1], in1=s.lamim[:], op=ALU.mult))
    V(lambda E: E.tensor_tensor(out=cre[:], in0=cre[:], in1=b[:], op=ALU.add))
    V(lambda E: E.tensor_tensor(out=cre[:], in0=cre[:], in1=den[:], op=ALU.mult))
    V(lambda E: E.tensor_tensor(out=cim[:], in0=s.Pim[:, :, :, 1], in1=s.lamre[:], op=ALU.mult))
    V(lambda E: E.tensor_tensor(out=b[:], in0=a[:], in1=s.lamim[:], op=ALU.mult))
    V(lambda E: E.tensor_tensor(out=cim[:], in0=cim[:], in1=b[:], op=ALU.subtract))
    V(lambda E: E.tensor_tensor(out=cim[:], in0=cim[:], in1=den[:], op=ALU.mult))
    bc = lambda x: x[:].unsqueeze(3).to_broadcast([128, 2, 16, 16])
    tb1 = t("tb1", [128, 2, 16, 16])
    V(lambda E: E.tensor_tensor(out=s.Bbre[:], in0=s.Bre[:], in1=bc(cre), op=ALU.mult))
    V(lambda E: E.tensor_tensor(out=tb1[:], in0=s.Bim[:], in1=bc(cim), op=ALU.mult))
    V(lambda E: E.tensor_tensor(out=s.Bbre[:], in0=s.Bbre[:], in1=tb1[:], op=ALU.subtract))
    V(lambda E: E.tensor_tensor(out=s.Bbim[:], in0=s.Bim[:], in1=bc(cre), op=ALU.mult))
    V(lambda E: E.tensor_tensor(out=tb1[:], in0=s.Bre[:], in1=bc(cim), op=ALU.mult))
    V(lambda E: E.tensor_tensor(out=s.Bbim[:], in0=s.Bbim[:], in1=tb1[:], op=ALU.add))
    V(lambda E: E.activation(out=s.r4[:], in_=rho[:], func=ACT.Exp, scale=4.0), "act")
    V(lambda E: E.reciprocal(out=a[:], in_=s.r4[:]), "dve")
    V(lambda E: E.tensor_tensor(out=s.cph[:], in0=s.Pre[:, :, :, 4], in1=a[:], op=ALU.mult))
    V(lambda E: E.tensor_tensor(out=s.sph[:], in0=s.Pim[:, :, :, 4], in1=a[:], op=ALU.mult))


S5_ITS = [(cc_, h_, d_) for cc_ in range(4) for h_ in range(2) for d_ in range(2)]


def s5_doubling(g, cc_, d_, CS, d1, d2, rk):
    P = g.P
    s = g.s5
    p0_ = 4 * cc_
    if d_ == 0:
        view = lambda bi: CS[:, :, :, BST[bi]:BST[bi] + NCH[bi] + 1]
    else:
        view = lambda bi: CS[:, :, :, BST[bi] + NCH[bi]:(BST[bi] - 1 if BST[bi] > 0 else None):-1]
    T = view(2)
    P.op("pool", lambda E: E.memset(T[:, 0, :, 0:1], 1.0), writes=[rk])
    P.op("pool", lambda E: E.memset(T[:, 1, :, 0:1], 0.0), writes=[rk])
    P.op("pool", lambda E: E.tensor_copy(out=T[:, 0, :, 1:2], in_=s.cph[:, d_, p0_:p0_ + 4].unsqueeze(2)), reads=["s5d"], writes=[rk])
    P.op("pool", lambda E: E.tensor_copy(out=T[:, 1, :, 1:2], in_=s.sph[:, d_, p0_:p0_ + 4].unsqueeze(2)), reads=["s5d"], writes=[rk])
    m = 1
    while m < 512:
        cm = T[:, 0:1, :, m:m + 1].to_broadcast([128, 2, 4, m])
        sm = T[:, 1:2, :, m:m + 1].to_broadcast([128, 2, 4, m])
        lo, hi = slice(1, m + 1), slice(m + 1, 2 * m + 1)
        P.op("pool", lambda E: E.tensor_tensor(out=d1[:, :, :, 0:m], in0=T[:, :, :, lo], in1=cm, op=ALU.mult), reads=[rk], writes=["d1"])
        P.op("pool", lambda E: E.tensor_tensor(out=d2[:, :, :, 0:m], in0=T[:, :, :, lo], in1=sm, op=ALU.mult), reads=[rk], writes=["d2"])
        P.op("pool", lambda E: E.tensor_tensor(out=T[:, 0, :, hi], in0=d1[:, 0, :, 0:m], in1=d2[:, 1, :, 0:m], op=ALU.subtract), reads=["d1", "d2"], writes=[rk])
        P.op("pool", lambda E: E.tensor_tensor(out=T[:, 1, :, hi], in0=d1[:, 1, :, 0:m], in1=d2[:, 0, :, 0:m], op=ALU.add), reads=["d1", "d2"], writes=[rk])
        m *= 2
    for bi in range(2):
        P.op("pool", lambda E, bi=bi: E.tensor_copy(out=view(bi), in_=T[:, :, :, 0:65]), reads=[rk], writes=[rk])
    for h_ in range(2):
        itn = S5_ITS.index((cc_, h_, d_))
        for cs in range(2):
            P.dma(lambda E, cs=cs, h_=h_, itn=itn: E.dma_start(
                out=g.rots[itn, cs], in_=CS[:, cs, 2 * h_:2 * h_ + 2, :].rearrange("p a b -> p (a b)")),
                reads=[rk], writes=[("rots", itn, cs)], q="pool")


def stage_s5(g):
    nc, P, ps = g.nc, g.P, g.ps
    with ExitStack() as ss:
        s = g.s5

        def t(name, shape, dt=F32):
            return ss.enter_context(_sbt(nc, "s5m_" + name, shape, dt))

        wr = Ring(nc, ss, "ws5", 2, [128, 8, 128], BF16)
        ygr = Ring(nc, ss, "ygt", 2, [128, TT], BF16)
        U4 = t("U4", [128, 4, 640], BF16)
        yacc = t("yacc", [128, T])
        XB = t("XB", [128, 2, 2, 4, NCOL], BF16)
        Sres = [t("Sre0", [128, 2, NCOL]), t("Sre1", [128, 2, NCOL])]
        Sims = [t("Sim0", [128, 2, NCOL]), t("Sim1", [128, 2, NCOL])]
        t1, t2 = t("t1", [128, 2, NCOL]), t("t2", [128, 2, NCOL])
        cTs = [t("cT0", [128, 2, NCOL]), t("cT1", [128, 2, NCOL])]
        sTs = [t("sT0", [128, 2, NCOL]), t("sT1", [128, 2, NCOL])]
        ones643 = t("ones643", [128, NCOL])
        P.op("pool", lambda E: E.memset(ones643[:], 1.0), writes=["ones643"])
        its = S5_ITS

        def load_rot(itn):
            rk = ("rot", itn % 2)
            P.dma(lambda E: E.dma_start(out=cTs[itn % 2][:].rearrange("p a b -> p (a b)"), in_=g.rots[itn, 0]),
                  reads=[("rots", itn, 0)], writes=[rk])
            P.dma(lambda E: E.dma_start(out=sTs[itn % 2][:].rearrange("p a b -> p (a b)"), in_=g.rots[itn, 1]),
                  reads=[("rots", itn, 1)], writes=[rk])

        load_rot(0)
        Ats = [t("At0", [128, 2, NCOL]), t("At1", [128, 2, NCOL])]
        Gre, Gim = t("Gre", [128, 4, 7, 32]), t("Gim", [128, 4, 7, 32])
        Cpre, CpimN = t("Cpre", [128, 4, 32]), t("CpimN", [128, 4, 32])
        Cpim = t("Cpim", [128, 4, 32])
        Min = t("Min", [128, 2, 4, 128], BF16)
        Ws = t("Ws", [128, 2, 2, 4, 128], BF16)
        Cup = t("Cup", [128, 2, 2, 4, 4, 32], BF16)
        cu1, cu2 = t("cu1", [128, 4, 4, 32]), t("cu2", [128, 4, 4, 32])
        g1, g2t = t("g1", [128, 4, 4, 16]), t("g2t", [128, 4, 4, 16])
        win = g.d["cd_w_in"].rearrange("(kc p) n -> p kc n", p=128)
        sd = VOFF["ssmd"]

        for cc in range(4):
            wt, wk = wr.next()
            load_cast(g, wt[:], wk, win[:, :, 1536 + cc * 128:1536 + (cc + 1) * 128], [128, 8, 128])
            for tile in range(NT):
                bank = _psb(g)
                for k in range(8):
                    P.op("pe", lambda E, k=k, tile=tile, bank=bank: E.matmul(
                        ps[:, bank, :], lhsT=wt[:, k, :], rhs=g.hbuf[:, k, tile * TT:(tile + 1) * TT],
                        start=(k == 0), stop=(k == 7)), reads=[wk, ("h", tile, k)], writes=[("ps", bank)])
                P.op("act", lambda E, tile=tile, bank=bank, cc=cc: E.activation(
                    out=yacc[:, tile * TT:(tile + 1) * TT], in_=ps[:, bank, :], func=ACT.Copy,
                    scale=g.vecs[:, sd + cc:sd + cc + 1]), reads=[("ps", bank), "vecs"], writes=[("yacc", tile)])
            for pl in range(4):
                b0 = _psb(g)
                if b0 % 2 == 1:
                    b0 = _psb(g)
                _psb(g)
                for hf, (c0, n) in enumerate(((0, 512), (512, 128))):
                    for i in range(4):
                        for k in range(8):
                            P.op("pe", lambda E, i=i, k=k, pl=pl, c0=c0, n=n, b0=b0, hf=hf: E.matmul(
                                ps[32 * i:32 * i + 32, b0 + hf, 0:n], lhsT=wt[:, k, 32 * pl:32 * pl + 32],
                                rhs=g.hbuf[:, k, 4 * c0 + i:4 * (c0 + n - 1) + i + 1:4], start=(k == 0), stop=(k == 7), tile_position=(0, 32 * i)),
                                reads=[wk] + [("h", tt_, k) for tt_ in range(NT)], writes=[("ps", b0 + hf)])
                    P.op("act" if hf == 0 else "dve", lambda E, pl=pl, c0=c0, n=n, b0=b0, hf=hf: (
                        E.activation(out=U4[:, pl, c0:c0 + n], in_=ps[:, b0 + hf, 0:n], func=ACT.Copy) if hf == 0 else
                        E.tensor_copy(out=U4[:, pl, c0:c0 + n], in_=ps[:, b0 + hf, 0:n])),
                        reads=[("ps", b0 + hf)], writes=[("U4", pl)])

            def iter_body(cc, half, d):
                p0 = 4 * cc + 2 * half
                itn_ = S5_ITS.index((cc, half, d))
                Sre, Sim, At = Sres[itn_ % 2], Sims[itn_ % 2], Ats[itn_ % 2]
                SK, AK = ("S", itn_ % 2), ("At", itn_ % 2)
                KT = ("s5tab", d)
                rd = ["s5d", "s5raw"]
                P.op("pool", lambda E: E.memset(Gre[:, 0:2], 0.0), writes=["G"])
                P.op("pool", lambda E: E.memset(Gim[:, 0:2], 0.0), writes=["G"])
                P.op("pool", lambda E: E.memset(Cpre[:, 0:2], 0.0), writes=["Cp"])
                P.op("pool", lambda E: E.memset(Cpim[:, 0:2], 0.0), writes=["Cp"])
                for hh in range(2):
                    hs = slice(64 * hh, 64 * hh + 64)
                    cs = slice(16 * hh, 16 * hh + 16)
                    if d == 0:
                        pk = lambda A: A[hs, d, p0:p0 + 2, 3::-1]
                        ms = slice(0, 4)
                    else:
                        pk = lambda A: A[hs, d, p0:p0 + 2, 0:4]
                        ms = slice(3, 7)
                    Pb = lambda A: pk(A).unsqueeze(3).to_broadcast([64, 2, 4, 16])
                    Bb = lambda A: A[hs, d, p0:p0 + 2, :].unsqueeze(2).to_broadcast([64, 2, 4, 16])
                    P.op("dve", lambda E, Pb=Pb, Bb=Bb, hs=hs: E.tensor_tensor(out=g1[hs, 0:2], in0=Pb(s.Pre), in1=Bb(s.Bbre), op=ALU.mult),
                         reads=rd, writes=["g1"])
                    P.op("dve", lambda E, Pb=Pb, Bb=Bb, hs=hs: E.tensor_tensor(out=g2t[hs, 0:2], in0=Pb(s.Pim), in1=Bb(s.Bbim), op=ALU.mult),
                         reads=rd, writes=["g2t"])
                    P.op("dve", lambda E, hs=hs, ms=ms, cs=cs: E.tensor_tensor(out=Gre[hs, 0:2, ms, cs], in0=g1[hs, 0:2], in1=g2t[hs, 0:2], op=ALU.subtract),
                         reads=["g1", "g2t"], writes=["G"])
                    P.op("dve", lambda E, Pb=Pb, Bb=Bb, hs=hs: E.tensor_tensor(out=g1[hs, 0:2], in0=Pb(s.Pre), in1=Bb(s.Bbim), op=ALU.mult),
                         reads=rd, writes=["g1"])
                    P.op("dve", lambda E, Pb=Pb, Bb=Bb, hs=hs: E.tensor_tensor(out=g2t[hs, 0:2], in0=Pb(s.Pim), in1=Bb(s.Bbre), op=ALU.mult),
                         reads=rd, writes=["g2t"])
                    P.op("dve", lambda E, hs=hs, ms=ms, cs=cs: E.tensor_tensor(out=Gim[hs, 0:2, ms, cs], in0=g1[hs, 0:2], in1=g2t[hs, 0:2], op=ALU.add),
                         reads=["g1", "g2t"], writes=["G"])
                    P.op("dve", lambda E, hs=hs, cs=cs: E.tensor_copy(out=Cpre[hs, 0:2, cs], in_=s.Cre[hs, d, p0:p0 + 2, :]), reads=rd, writes=["Cp"])
                    P.op("dve", lambda E, hs=hs, cs=cs: E.tensor_copy(out=Cpim[hs, 0:2, cs], in_=s.Cim[hs, d, p0:p0 + 2, :]), reads=rd, writes=["Cp"])
                P.op("dve", lambda E: E.tensor_scalar(out=CpimN[:, 0:2], in0=Cpim[:, 0:2], scalar1=-1.0, scalar2=None, op0=ALU.mult),
                     reads=["Cp"], writes=["CpN"])
                for pl2 in range(2):
                    pl = 2 * half + pl2
                    bank = _psb(g)
                    for j in range(4):
                        P.op("pe", lambda E, j=j, pl2=pl2, bank=bank: E.matmul(
                            ps[:, bank, j * 32:(j + 1) * 32], lhsT=Gre[:, pl2, 3 - j:7 - j, :].rearrange("p m c -> p (m c)"),
                            rhs=Cpre[:, pl2, :], start=True, stop=False), reads=["G", "Cp"], writes=[("ps", bank)])
                        P.op("pe", lambda E, j=j, pl2=pl2, bank=bank: E.matmul(
                            ps[:, bank, j * 32:(j + 1) * 32], lhsT=Gim[:, pl2, 3 - j:7 - j, :].rearrange("p m c -> p (m c)"),
                            rhs=CpimN[:, pl2, :], start=False, stop=True), reads=["G", "CpN"], writes=[("ps", bank)])
                    P.op("act", lambda E, pl=pl, bank=bank, d=d: E.activation(out=Min[:, d, pl, :], in_=ps[:, bank, 0:128], func=ACT.Copy),
                         reads=[("ps", bank)], writes=[KT])
                    jw = 3 if d == 0 else 0
                    for ri, Gx in enumerate((Gre, Gim)):
                        bank = _psb(g)
                        P.op("pe", lambda E, Gx=Gx, pl2=pl2, bank=bank, jw=jw: E.transpose(
                            ps[:, bank, 0:128], Gx[:, pl2, 3 - jw:7 - jw, :].rearrange("p m c -> p (m c)"), g.ident[:]),
                            reads=["G", "consts"], writes=[("ps", bank)])
                        P.op("act", lambda E, ri=ri, pl=pl, bank=bank, d=d: E.activation(out=Ws[:, d, ri, pl, :], in_=ps[:, bank, 0:128], func=ACT.Copy),
                             reads=[("ps", bank)], writes=[KT])
                if d == 0:
                    pj = lambda A: A[:, d, p0:p0 + 2, 1:5]
                else:
                    pj = lambda A: A[:, d, p0:p0 + 2, 4:0:-1]
                Pj = lambda A: pj(A).unsqueeze(3).to_broadcast([128, 2, 4, 32])
                Cj = lambda A: A[:, 0:2, :].unsqueeze(2).to_broadcast([128, 2, 4, 32])
                hsl = slice(2 * half, 2 * half + 2)
                P.op("dve", lambda E, Pj=Pj, Cj=Cj: E.tensor_tensor(out=cu1[:, 0:2], in0=Cj(Cpre), in1=Pj(s.Pre), op=ALU.mult), reads=rd + ["Cp"], writes=["cu1"])
                P.op("dve", lambda E, Pj=Pj, Cj=Cj: E.tensor_tensor(out=cu2[:, 0:2], in0=Cj(Cpim), in1=Pj(s.Pim), op=ALU.mult), reads=rd + ["Cp"], writes=["cu2"])
                P.op("dve", lambda E, hsl=hsl, d=d: E.tensor_tensor(out=Cup[:, d, 0, hsl], in0=cu1[:, 0:2], in1=cu2[:, 0:2], op=ALU.subtract),
                     reads=["cu1", "cu2"], writes=[KT])
                P.op("dve", lambda E, Pj=Pj, Cj=Cj: E.tensor_tensor(out=cu1[:, 0:2], in0=Cj(Cpre), in1=Pj(s.nPim), op=ALU.mult), reads=rd + ["Cp"], writes=["cu1"])
                P.op("dve", lambda E, Pj=Pj, Cj=Cj: E.tensor_tensor(out=cu2[:, 0:2], in0=Cj(CpimN), in1=Pj(s.Pre), op=ALU.mult), reads=rd + ["CpN"], writes=["cu2"])
                P.op("dve", lambda E, hsl=hsl, d=d: E.tensor_tensor(out=Cup[:, d, 1, hsl], in0=cu1[:, 0:2], in1=cu2[:, 0:2], op=ALU.add),
                     reads=["cu1", "cu2"], writes=[KT])

                off = 1 if d == 0 else 0
                for pl2 in range(2):
                    pl = 2 * half + pl2
                    for ri, Sx in enumerate((Sre, Sim)):
                        b0 = _psb(g)
                        if b0 % 2 == 1:
                            b0 = _psb(g)
                        _psb(g)
                        for hf, (c0, n) in enumerate(((0, 512), (512, 128))):
                            P.op("pe", lambda E, ri=ri, pl=pl, c0=c0, n=n, b0=b0, hf=hf, d=d: E.matmul(
                                ps[:, b0 + hf, 0:n], lhsT=Ws[:, d, ri, pl, :], rhs=U4[:, pl, c0:c0 + n], start=True, stop=True),
                                reads=[KT, ("U4", pl)], writes=[("ps", b0 + hf)])
                        P.op("act", lambda E, Sx=Sx, pl2=pl2, b0=b0, off=off: E.activation(
                            out=Sx[:, pl2, BST[0] + off:BST[0] + off + 64], in_=ps[:, b0, 0:64], func=ACT.Copy),
                            reads=[("ps", b0)], writes=[SK])
                        P.op("act", lambda E, Sx=Sx, pl2=pl2, b0=b0, off=off: E.activation(
                            out=Sx[:, pl2, BST[1] + off:BST[1] + off + 64], in_=ps[:, b0, 64:128], func=ACT.Copy),
                            reads=[("ps", b0)], writes=[SK])
                        P.op("act", lambda E, Sx=Sx, pl2=pl2, b0=b0, off=off: E.activation(
                            out=Sx[:, pl2, BST[2] + off:BST[2] + off + 384], in_=ps[:, b0, 128:512], func=ACT.Copy),
                            reads=[("ps", b0)], writes=[SK])
                        P.op("act", lambda E, Sx=Sx, pl2=pl2, b0=b0, off=off: E.activation(
                            out=Sx[:, pl2, BST[2] + off + 384:BST[2] + off + 512], in_=ps[:, b0 + 1, 0:128], func=ACT.Copy),
                            reads=[("ps", b0 + 1)], writes=[SK])
                for bi in range(3):
                    col = BST[bi] if d == 0 else BST[bi] + NCH[bi]
                    for Sx, s0 in ((Sre, s.s0re), (Sim, s.s0im)):
                        if bi < 2:
                            P.op("pool", lambda E, Sx=Sx, col=col: E.memset(Sx[:, :, col:col + 1], 0.0), writes=[SK])
                        else:
                            P.op("pool", lambda E, Sx=Sx, col=col, s0=s0, d=d: E.tensor_copy(
                                out=Sx[:, :, col:col + 1], in_=s0[:, d, p0:p0 + 2].unsqueeze(2)), reads=["s5raw"], writes=[SK])
                for pl2 in range(2):
                    P.op("act", lambda E, pl2=pl2, d=d: E.activation(out=At[:, pl2, :], in_=ones643[:], func=ACT.Copy,
                                                                    scale=s.r4[:, d, p0 + pl2:p0 + pl2 + 1]),
                         reads=["s5d", "ones643"], writes=[AK])
                for bi in range(3):
                    col = BST[bi] if d == 0 else BST[bi] + NCH[bi]
                    P.op("pool", lambda E, col=col: E.memset(At[:, :, col:col + 1], 0.0), writes=[AK])

                itn = its.index((cc, half, d))
                cT, sT = cTs[itn % 2], sTs[itn % 2]
                ROT = ("rot", itn % 2)


                def phase_b():
                    if itn + 1 < len(its):
                        load_rot(itn + 1)
                    P.op("dve", lambda E: E.tensor_tensor(out=t1[:], in0=Sre[:], in1=cT[:], op=ALU.mult), reads=[SK, ROT], writes=["t1"])
                    P.op("dve", lambda E: E.tensor_tensor(out=t2[:], in0=Sim[:], in1=sT[:], op=ALU.mult), reads=[SK, ROT], writes=["t2"])
                    P.op("dve", lambda E: E.tensor_tensor(out=t1[:], in0=t1[:], in1=t2[:], op=ALU.add), reads=["t1", "t2"], writes=["t1"])
                    P.op("dve", lambda E: E.tensor_tensor(out=t2[:], in0=Sim[:], in1=cT[:], op=ALU.mult), reads=[SK, ROT, "t1"], writes=["t2"])
                    P.op("dve", lambda E: E.tensor_tensor(out=Sim[:], in0=Sre[:], in1=sT[:], op=ALU.mult), reads=[SK, ROT, "t2"], writes=[SK])
                    P.op("dve", lambda E: E.tensor_tensor(out=t2[:], in0=t2[:], in1=Sim[:], op=ALU.subtract), reads=[SK, "t2"], writes=["t2"])
                    fl = lambda A: (A[:].rearrange("p a b -> p (a b)") if d == 0 else A[:].rearrange("p a b -> p (a b)")[:, ::-1])
                    P.op("dve", lambda E, fl=fl: E.tensor_tensor_scan(out=fl(Sre), data0=fl(At), data1=fl(t1), initial=0.0, op0=ALU.mult, op1=ALU.add),
                         reads=["t1", AK, SK], writes=[SK])
                    P.op("dve", lambda E, fl=fl: E.tensor_tensor_scan(out=fl(Sim), data0=fl(At), data1=fl(t2), initial=0.0, op0=ALU.mult, op1=ALU.add),
                         reads=["t2", AK, SK], writes=[SK])
                    hsl = slice(2 * half, 2 * half + 2)
                    P.op("dve", lambda E: E.tensor_tensor(out=t1[:], in0=Sre[:], in1=cT[:], op=ALU.mult), reads=[SK, ROT], writes=["t1"])
                    P.op("dve", lambda E: E.tensor_tensor(out=t2[:], in0=Sim[:], in1=sT[:], op=ALU.mult), reads=[SK, ROT], writes=["t2"])
                    P.op("dve", lambda E: E.tensor_tensor(out=t1[:], in0=t1[:], in1=t2[:], op=ALU.subtract), reads=["t1", "t2"], writes=["t1"])
                    P.op("dve", lambda E: E.tensor_tensor(out=t2[:], in0=Sim[:], in1=cT[:], op=ALU.mult), reads=[SK, ROT, "t1"], writes=["t2"])
                    P.op("dve", lambda E: E.tensor_tensor(out=Sim[:], in0=Sre[:], in1=sT[:], op=ALU.mult), reads=[SK, ROT, "t2"], writes=[SK])
                    P.op("dve", lambda E: E.tensor_tensor(out=t2[:], in0=t2[:], in1=Sim[:], op=ALU.add), reads=[SK, "t2"], writes=["t2"])
                    P.op("act", lambda E, d=d, hsl=hsl: E.activation(out=XB[:, d, 0, hsl, :], in_=t1[:], func=ACT.Copy), reads=["t1"], writes=[("XB", d)])
                    P.op("act", lambda E, d=d, hsl=hsl: E.activation(out=XB[:, d, 1, hsl, :], in_=t2[:], func=ACT.Copy), reads=["t2"], writes=[("XB", d)])
                    for sq in range(2):
                        col = BST[sq] + NCH[sq] if d == 0 else BST[sq]
                        for ri, tx in enumerate((t1, t2)):
                            P.op("act", lambda E, sq=sq, ri=ri, tx=tx, col=col, d=d: E.activation(
                                out=g.fin[:, sq, d, ri, p0:p0 + 2], in_=tx[:, :, col], func=ACT.Copy), reads=["t1", "t2"], writes=["fin"])
                return phase_b

            pend = None
            for half in range(2):
                for d in range(2):
                    nb = iter_body(cc, half, d)
                    if pend is not None:
                        pend()
                    pend = nb
            pend()

            for tile in range(NT):
                bank = _psb(g)
                if tile == 0:
                    pieces = [(0, 64, 0), (64, 64, 1)]
                else:
                    pieces = [(128 * tile, 128, 2)]
                for pl in range(4):
                    first = True
                    qs = slice(32 * pl, 32 * pl + 32)
                    for j in range(4):
                        for d in range(2):
                            P.op("pe", lambda E, j=j, d=d, pl=pl, qs=qs, tile=tile, bank=bank, first=first: E.matmul(
                                ps[qs, bank, j:j + 4 * 127 + 1:4], lhsT=Min[:, d, pl, j * 32:(j + 1) * 32], rhs=U4[:, pl, 128 * tile:128 * tile + 128],
                                start=first, stop=False, skip_group_check=True, tile_position=(0, 32 * pl)),
                                reads=[("s5tab", d), ("U4", pl)], writes=[("ps", bank)])
                            first = False
                            for (c0, n, bi) in pieces:
                                cl = c0 - (0, 64, 128)[bi]
                                xc = BST[bi] + cl + (0 if d == 0 else 1)
                                o0 = 4 * (c0 - 128 * tile) + j
                                for ri in range(2):
                                    P.op("pe", lambda E, j=j, d=d, pl=pl, qs=qs, bank=bank, xc=xc, n=n, o0=o0, ri=ri: E.matmul(
                                        ps[qs, bank, o0:o0 + 4 * (n - 1) + 1:4], lhsT=Cup[:, d, ri, pl, j, :], rhs=XB[:, d, ri, pl, xc:xc + n],
                                        start=False, stop=False, skip_group_check=True, tile_position=(0, 32 * pl)),
                                        reads=[("s5tab", d), ("XB", d)], writes=[("ps", bank)])
                P.op("dve", lambda E, tile=tile, bank=bank: E.tensor_tensor(
                    out=yacc[:, tile * TT:(tile + 1) * TT], in0=ps[:, bank, :], in1=yacc[:, tile * TT:(tile + 1) * TT], op=ALU.add),
                    reads=[("ps", bank), ("yacc", tile)], writes=[("yacc", tile)])
            for tile in range(NT):
                ys = yacc[:, tile * TT:(tile + 1) * TT]
                a1 = t1[:].rearrange("p a b -> p (a b)")[:, 0:TT]
                a2 = t2[:].rearrange("p a b -> p (a b)")[:, 0:TT]
                P.op("act", lambda E, ys=ys, a1=a1: E.activation(out=a1, in_=ys, func=ACT.Square), reads=[("yacc", tile), "t1"], writes=["t1"])
                P.op("dve", lambda E, a1=a1: E.tensor_scalar(out=a1, in0=a1, scalar1=0.044715, scalar2=1.0, op0=ALU.mult, op1=ALU.add), reads=["t1"], writes=["t1"])
                P.op("dve", lambda E, ys=ys, a1=a1: E.tensor_tensor(out=a1, in0=a1, in1=ys, op=ALU.mult), reads=["t1", ("yacc", tile)], writes=["t1"])
                P.op("act", lambda E, a1=a1, a2=a2: E.activation(out=a2, in_=a1, func=ACT.Sigmoid, scale=2.0 * math.sqrt(2.0 / math.pi)), reads=["t1", "t2"], writes=["t2"])
                ygt, ygk = ygr.next()
                P.op("dve", lambda E, ys=ys, a2=a2, ygt=ygt: E.tensor_tensor(out=ygt[:], in0=ys, in1=a2, op=ALU.mult),
                     reads=["t2", ("yacc", tile)], writes=[ygk])
                P.dma(lambda E, ygt=ygt, tile=tile, cc=cc: E.dma_start(out=g.ygs[cc, :, tile * TT:(tile + 1) * TT], in_=ygt[:]),
                      reads=[ygk], writes=[("ygs", cc, tile)], q="sp")
        fT = t("finT", [128, 2, 64])
        for g2 in range(2):
            bank = _psb(g)
            P.op("pe", lambda E, g2=g2, bank=bank: E.transpose(
                ps[:, bank, 0:64], g.fin[64 * g2:64 * g2 + 64].rearrange("p a b c e -> p (a b c e)"), g.ident[64 * g2:64 * g2 + 64, 64 * g2:64 * g2 + 64]),
                reads=["fin", "consts"], writes=[("ps", bank)])
            P.op("act", lambda E, g2=g2, bank=bank: E.activation(out=fT[:, g2, :], in_=ps[:, bank, 0:64], func=ACT.Copy),
                 reads=[("ps", bank)], writes=[("finT", g2)])
            for sq in range(2):
                for d in range(2):
                    for ri, nm in enumerate(("nre", "nim")):
                        r0 = ((sq * 2 + d) * 2 + ri) * 16
                        P.dma(lambda E, g2=g2, sq=sq, d=d, nm=nm, r0=r0: E.dma_start(
                            out=g.d[nm][sq, d, g2::2, :], in_=fT[r0:r0 + 16, g2, :]), reads=[("finT", g2)], final=True, q="sp")
        P.barrier()


def stage_glu(g):
    nc, P, ps = g.nc, g.P, g.ps
    with ExitStack() as ss:
        gw = ss.enter_context(_sbt(nc, "gw", [128, 4, 512], BF16))
        sg = Ring(nc, ss, "sg", 2, [128, TT], F32)
        ygbuf = ss.enter_context(_sbt(nc, "ygbuf", [128, 4, T], BF16))
        for k in range(4):
            for tile in range(NT):
                P.dma(lambda E, k=k, tile=tile: E.dma_start(out=ygbuf[:, k, tile * TT:(tile + 1) * TT], in_=g.ygs[k, :, tile * TT:(tile + 1) * TT]),
                      reads=[("ygs", k, tile)], writes=[("yg", k, tile)], q="sp")
                P.dma(lambda E, k=k, tile=tile: E.dma_start(out=g.mixbuf[:, k, tile * TT:(tile + 1) * TT], in_=g.obs[k, :, tile * TT:(tile + 1) * TT]),
                      reads=[("obs", k, tile)], writes=[("mix", k, tile)], q="act")
        load_cast(g, gw[:], "gw", g.d["glu_w"].rearrange("(kc p) n -> p kc n", p=128), [128, 4, 512])
        gb = VOFF["glub"]
        for oc in range(4):
            for tile in range(NT):
                bank = _psb(g)
                for k in range(4):
                    P.op("pe", lambda E, oc=oc, k=k, tile=tile, bank=bank: E.matmul(
                        ps[:, bank, :], lhsT=gw[:, k, oc * 128:(oc + 1) * 128], rhs=ygbuf[:, k, tile * TT:(tile + 1) * TT],
                        start=(k == 0), stop=(k == 3)), reads=["gw", ("yg", k, tile)], writes=[("ps", bank)])
                st, sk = sg.next()
                P.op("act", lambda E, st=st, bank=bank, oc=oc: E.activation(out=st[:], in_=ps[:, bank, :], func=ACT.Sigmoid,
                                                                            bias=g.vecs[:, gb + oc:gb + oc + 1]),
                     reads=[("ps", bank), "vecs"], writes=[sk])
                P.op("dve", lambda E, st=st, oc=oc, tile=tile: E.tensor_tensor(
                    out=g.mixbuf[:, 4 + oc, tile * TT:(tile + 1) * TT], in0=ygbuf[:, oc, tile * TT:(tile + 1) * TT], in1=st[:], op=ALU.mult),
                    reads=[sk, ("yg", oc, tile)], writes=[("mix", 4 + oc, tile)])
        P.barrier()


IN_SPECS = [
    ("xin", [T, D]), ("condT", [128, 8, 2]), ("vecs", [128, NVEC]), ("ident", [128, 128]),
    ("onehot", [32, 64, 64]), ("rcnt", [4, T]),
    ("ada_w", [2, D, 6 * D]), ("mlp_w1", [2, D, 4 * D]), ("mlp_w2", [2, 4 * D, D]),
    ("ab_w_in", [D, 2 * D]), ("ab_w_out", [D, D]), ("cd_w_in", [D, 2 * D]), ("cd_w_out", [D, D]),
    ("pool_w", [4, 128, 128]), ("glu_w", [512, 512]),
    ("cache_k", [8, 256, 64]), ("cache_v", [8, 256, 64]), ("na_rpb", [8, 15, 31]),
    ("state_re", [2, 32, 64]), ("state_im", [2, 32, 64]),
    ("ssm_lambda_re", [2, 32, 64]), ("ssm_lambda_im", [2, 32, 64]), ("ssm_log_step", [2, 32]),
    ("ssm_b_re", [2, 32, 64, 16]), ("ssm_b_im", [2, 32, 64, 16]),
    ("ssm_c_re", [2, 32, 16, 64]), ("ssm_c_im", [2, 32, 16, 64]),
]
OUT_SPECS = [("yp", [512, D]), ("ys", [2048, D]), ("nk", [2, 8, 256, 64]), ("nv", [2, 8, 256, 64]),
             ("nre", [2, 2, 32, 64]), ("nim", [2, 2, 32, 64])]


def build(stop=None, debug=()):
    nc = bass.Bass("TRN2", target_bir_lowering=False)
    g = G()
    g.nc = nc
    g.d = {}
    for name, shape in IN_SPECS:
        g.d[name] = nc.dram_tensor(name, shape, F32, kind="ExternalInput").ap()
    for name, shape in OUT_SPECS:
        g.d[name] = nc.dram_tensor(name, shape, F32, kind="ExternalOutput").ap()
    g.xs = nc.dram_tensor("xs", [8, 128, T], F32, kind=("ExternalOutput" if "xs" in debug else "Internal")).ap()
    g.obs = nc.dram_tensor("obs", [4, 128, T], BF16, kind=("ExternalOutput" if "obs" in debug else "Internal")).ap()
    g.ygs = nc.dram_tensor("ygs", [4, 128, T], BF16, kind=("ExternalOutput" if "ygs" in debug else "Internal")).ap()
    if "hbuf" in debug:
        g.d["dbg_h"] = nc.dram_tensor("dbg_h", [8, 128, T], BF16, kind="ExternalOutput").ap()
    if "mix" in debug:
        g.d["dbg_mix"] = nc.dram_tensor("dbg_mix", [8, 128, T], BF16, kind="ExternalOutput").ap()
    if "mod" in debug:
        g.d["dbg_mod"] = nc.dram_tensor("dbg_mod", [2, 128, 96], F32, kind="ExternalOutput").ap()

    with ExitStack() as es:
        P = Prog(nc, es)
        g.P = P
        g.ps = es.enter_context(nc.psum_tensor("ps", [128, 8, 512], F32))
        g.bank = 0
        g.nbanks = 8

        def t(name, shape, dt=F32):
            return es.enter_context(_sbt(nc, "sb_" + name, shape, dt))

        g.ident = t("ident", [128, 128])
        g.ones_bf = t("ones_bf", [128, 128], BF16)
        g.epst = t("epst", [128, 1])
        g.vecs = t("vecs", [128, NVEC])
        condT = t("condT", [128, 8, 2])
        g.scT = t("scT", [128, 8, 2], BF16)
        g.mod = [t("mod%d" % l, [128, 48, 2]) for l in range(2)]
        g.gm = [t("gm%d" % l, [128, 2, 8, 2]) for l in range(2)]
        g.fin = t("fin", [128, 2, 2, 2, 16])
        g.hbuf = t("hbuf", [128, 8, T], BF16)

        P.dma(lambda E: E.dma_start(out=g.ident[:], in_=g.d["ident"][:, :]), writes=["consts"])
        P.dma(lambda E: E.dma_start(out=g.vecs[:], in_=g.d["vecs"][:, :]), writes=["vecs"], q="sp")
        P.dma(lambda E: E.dma_start(out=condT[:], in_=g.d["condT"][:, :, :]), writes=["condT"])
        P.op("pool", lambda E: E.memset(g.ones_bf[:], 1.0), writes=["consts"])
        P.op("pool", lambda E: E.memset(g.epst[:], EPS), writes=["consts"])
        P.op("act", lambda E: E.activation(out=g.scT[:], in_=condT[:], func=ACT.Silu), reads=["condT"], writes=["scT"])
        g.rots = nc.dram_tensor("rots", [16, 2, 128, 2 * NCOL], F32).ap()

        def dump_h():
            if "hbuf" in debug:
                for tile in range(NT):
                    P.dma(lambda E, tile=tile: E.dma_start(out=g.d["dbg_h"][:, :, tile * TT:(tile + 1) * TT].rearrange("c p t -> p c t"),
                                                           in_=g.hbuf[:, :, tile * TT:(tile + 1) * TT]),
                          reads=hkeys(tile), final=True)

        def dump_mix():
            if "mix" in debug:
                for tile in range(NT):
                    P.dma(lambda E, tile=tile: E.dma_start(out=g.d["dbg_mix"][:, :, tile * TT:(tile + 1) * TT].rearrange("c p t -> p c t"),
                                                           in_=g.mixbuf[:, :, tile * TT:(tile + 1) * TT]),
                          reads=[("mix", c, tile) for c in range(8)], final=True)

        def run():
            with ExitStack() as a0:
                g.aring = Ring(nc, a0, "wrA0", 3, [128, 4096], BF16)
                g.astg = Ring(nc, a0, "stA0", 2, [128, 4096], F32)
                g.nbanks = 7
                for cg in range(4):
                    adaln_group(g, 0, cg, 7)
                adaln_finish(g, 0, 7, sections=(0, 1))

                stage_in_norm1(g)
            if stop == "norm1":
                dump_h()
                return
            with ExitStack() as ms:
                g.mixbuf = ms.enter_context(_sbt(nc, "mixbuf0", [128, 8, T], BF16))
                ar_ = ExitStack()
                g.aring = Ring(nc, ar_, "wrA1", 3, [128, 4096], BF16)
                g.astg = Ring(nc, ar_, "stA1", 2, [128, 4096], F32)
                g.nbanks = 6
                sched = [(0, cg_) for cg_ in range(4, 12)] + [(1, cg_) for cg_ in range(12)]
                counts = [3, 3, 3, 3, 2, 2, 2, 2]

                def hook1(i):
                    k0 = sum(counts[:i])
                    for (l_, cg_) in sched[k0:k0 + counts[i]]:
                        adaln_group(g, l_, cg_, 7 if l_ == 0 else 6)
                        if (l_, cg_) == (0, 11):
                            adaln_finish(g, 0, 7, sections=(2, 3, 4, 5))
                        if (l_, cg_) == (1, 11):
                            adaln_finish(g, 1, 6)

                stage_mixer0(g, hook=hook1)
                ar_.close()
                g.nbanks = 8
                if stop == "mixer0":
                    dump_mix()
                    return
                stage_wout_norm2(g, 0, "ab_w_out", lambda k, tile: g.mixbuf[:, k, tile * TT:(tile + 1) * TT],
                                 lambda k, tile: [("mix", k, tile)])
                P.barrier()
            if stop == "wout0":
                dump_h()
                return
            s5s = ExitStack()
            s5stg = ExitStack()
            s5_load_raw(g, s5s, s5stg)
            with ExitStack() as xs_:
                X = xs_.enter_context(_sbt(nc, "X0", [128, 8, T], F32))
                stage_mlp(g, 0, X, hook=lambda fb: (g.s5.issue() if fb == 1 else None))
                if stop == "mlponly":
                    for tile in range(NT):
                        P.dma(lambda E, tile=tile: E.dma_start(
                            out=g.xs[:, :, tile * TT:(tile + 1) * TT].rearrange("c p t -> p c t"), in_=X[:, :, tile * TT:(tile + 1) * TT]),
                            reads=[("X", tile)], writes=[("xs", tile)], final=True)
                    P.barrier()
                    return
                stage_norm1_from_X(g, 1, X)
                P.barrier()
            if stop == "mlp0":
                dump_h()
                return
            s5_raw_xpose(g)
            s5stg.close()
            if True:
                s5_setup(g, s5s)
                dst = {}

                def dbl(hp, ss):
                    s5_emit(g)
                    if not dst:
                        dst["cs"] = _sbt_enter(nc, ss, "dCS", [128, 2, 4, NCOL], F32)
                        dst["d1"] = _sbt_enter(nc, ss, "dd1", [128, 2, 4, 256], F32)
                        dst["d2"] = _sbt_enter(nc, ss, "dd2", [128, 2, 4, 256], F32)
                    for d_ in range(2):
                        s5_doubling(g, hp, d_, dst["cs"], dst["d1"], dst["d2"], "drot")

                stage_attn(g, dbl=dbl)
                if stop == "attn":
                    return
                stage_s5(g)
                if stop == "s5":
                    return
                s5s.close()
            with ExitStack() as ms:
                g.mixbuf = ms.enter_context(_sbt(nc, "mixbuf1", [128, 8, T], BF16))
                stage_glu(g)
                if stop == "glu":
                    dump_mix()
                    return
                stage_wout_norm2(g, 1, "cd_w_out", lambda k, tile: g.mixbuf[:, k, tile * TT:(tile + 1) * TT],
                                 lambda k, tile: [("mix", k, tile)])
                P.barrier()
            with ExitStack() as xs_:
                X = xs_.enter_context(_sbt(nc, "X1", [128, 8, T], F32))
                stage_mlp(g, 1, X)
                stage_final(g, X)
                P.barrier()
        run()
        P.finish()
        g.n_inst = P.n_inst
        with nc.Block() as block:
            @block.sync
            def _(E):
                for f in P.ops["sp"]:
                    f(E)

            @block.tensor
            def _(E):
                for f in P.ops["pe"]:
                    f(E)

            @block.scalar
            def _(E):
                for f in P.ops["act"]:
                    f(E)

            @block.vector
            def _(E):
                for f in P.ops["dve"]:
                    f(E)

            @block.gpsimd
            def _(E):
                for f in P.ops["pool"]:
                    f(E)
    return nc, g


def _cols(v):
    v = np.asarray(v, np.float32)
    return np.ascontiguousarray(v.reshape(-1, 128).T)


def host_consts():
    ident = np.eye(128, dtype=np.float32)
    oh = np.zeros((32, 64, 64), np.float32)
    for qc in range(64):
        c0 = min(max(qc - 8, 0), 48)
        for kc in range(64):
            oh[0, qc, kc] = 0.0 if (c0 <= kc < c0 + 16) else NEG
            co = kc - qc + 15
            if 0 <= co < 31:
                oh[1 + co, qc, kc] = 1.0
    rc = np.zeros((4, T), np.float32)
    for gi, w in enumerate((2, 4, 8, 16)):
        for (s, n) in SEQS:
            pos = np.arange(n)
            lo = np.maximum(pos - w // 2, 0)
            hi = np.minimum(pos + (w - w // 2), n)
            rc[gi, s:s + n] = 1.0 / (hi - lo).astype(np.float32)
    return ident, oh, rc


def make_in_maps(inp):
    f = lambda k: np.asarray(inp[k], np.float32)
    ident, oh, rc = host_consts()
    vecs = np.zeros((128, NVEC), np.float32)
    for l in range(2):
        vecs[:, VOFF["n1g%d" % l]:VOFF["n1g%d" % l] + 8] = _cols(f("norm1_g")[l])
        vecs[:, VOFF["n2g%d" % l]:VOFF["n2g%d" % l] + 8] = _cols(f("norm2_g")[l])
        vecs[:, VOFF["adab%d" % l]:VOFF["adab%d" % l] + 48] = _cols(f("ada_b")[l])
    vecs[:, VOFF["fing"]:VOFF["fing"] + 8] = _cols(f("final_g"))
    vecs[:, VOFF["pscale"]:VOFF["pscale"] + 4] = _cols(f("pool_scale")[0])
    cw = f("conv_w")[0]
    for k in range(3):
        vecs[:, VOFF["convw"] + 4 * k:VOFF["convw"] + 4 * k + 4] = _cols(cw[k])
    vecs[:, VOFF["ssmd"]:VOFF["ssmd"] + 4] = _cols(f("ssm_d")[0])
    vecs[:, VOFF["glub"]:VOFF["glub"] + 4] = _cols(f("glu_b")[0])
    shared = {
        "vecs": vecs, "ident": ident, "onehot": oh, "rcnt": rc,
        "ada_w": f("ada_w"), "mlp_w1": f("mlp_w1"), "mlp_w2": f("mlp_w2"),
        "ab_w_in": f("ab_w_in")[0], "ab_w_out": f("ab_w_out")[0], "cd_w_in": f("cd_w_in")[0], "cd_w_out": f("cd_w_out")[0],
        "pool_w": f("pool_w")[0], "glu_w": f("glu_w")[0], "na_rpb": f("na_rpb")[0],
        "ssm_lambda_re": f("ssm_lambda_re")[0], "ssm_lambda_im": f("ssm_lambda_im")[0], "ssm_log_step": f("ssm_log_step")[0],
        "ssm_b_re": f("ssm_b_re")[0], "ssm_b_im": f("ssm_b_im")[0], "ssm_c_re": f("ssm_c_re")[0], "ssm_c_im": f("ssm_c_im")[0],
    }
    xp, xs, c, cctx = f("x_prompt"), f("x_sample"), f("c"), f("c_ctx")
    maps = []
    for i in range(8):
        b = i // 2
        m = dict(shared)
        m["xin"] = np.ascontiguousarray(np.concatenate([xp[2 * i], xp[2 * i + 1], xs[b]], axis=0))
        m["condT"] = np.ascontiguousarray(np.stack([_cols(cctx), _cols(c[b])], axis=-1))
        m["cache_k"] = np.ascontiguousarray(f("cache_k")[b, 0])
        m["cache_v"] = np.ascontiguousarray(f("cache_v")[b, 0])
        m["state_re"] = np.ascontiguousarray(f("state_s5_re")[b, 0])
        m["state_im"] = np.ascontiguousarray(f("state_s5_im")[b, 0])
        maps.append(m)
    return maps


_NC_CACHE = {}


def kernel(**inputs):
    if "nc" not in _NC_CACHE:
        _NC_CACHE["nc"] = build()[0]
    nc = _NC_CACHE["nc"]
    maps = make_in_maps(inputs)
    res = run_bass_kernel_spmd(nc, maps, core_ids=list(range(8))).results
    y_prompt = np.zeros((16, 256, D), np.float32)
    y_sample = np.zeros((4, 2048, D), np.float32)
    nk = np.zeros((16, 1, 8, 256, 64), np.float32)
    nv = np.zeros((16, 1, 8, 256, 64), np.float32)
    nre = np.zeros((16, 1, 2, 32, 64), np.float32)
    nim = np.zeros((16, 1, 2, 32, 64), np.float32)
    for i in range(8):
        r = res[i]
        y_prompt[2 * i:2 * i + 2] = np.asarray(r["yp"], np.float32).reshape(2, 256, D)
        if i % 2 == 0:
            y_sample[i // 2, 0:1024] = np.asarray(r["ys"], np.float32)[0:1024]
        else:
            y_sample[i // 2, 1024:2048] = np.asarray(r["ys"], np.float32)[1024:2048]
        nk[2 * i:2 * i + 2, 0] = np.asarray(r["nk"], np.float32)
        nv[2 * i:2 * i + 2, 0] = np.asarray(r["nv"], np.float32)
        nre[2 * i:2 * i + 2, 0] = np.asarray(r["nre"], np.float32)
        nim[2 * i:2 * i + 2, 0] = np.asarray(r["nim"], np.float32)
    return (y_prompt, y_sample, nk, nv, nre, nim)
```
